# Optimizing a Trainium2 kernel written in Bass

```python
import jax, jax.numpy as jnp
from jax import lax
import numpy as np

D_MODEL = 2048
BATCH = 2
SEQ = 8192
DEPTH = 2

GRID_W = 64
CTX_LEN = 256
D_LRU = 1024
N_LRU_HEADS = 4
LRU_HEAD_DIM = D_LRU // N_LRU_HEADS
RG_C = 8.0
CONV_SHORT = 4
SHORT_PAD_L = 2
SHORT_PAD_R = 1
D_CONV = 1024
CONV_K = 31
D_MIX = D_LRU + D_CONV
D_IN = 2 * D_LRU + 2 * D_CONV
D_FF = 5632
N_MOD = 9
ALPHA = (2 * DEPTH) ** 0.25
BETA = (8 * DEPTH) ** -0.25
ADA_SCALE = 0.5
EPS = 1e-6

kernel_name = 'hybrid_rglru_conformer_dit_block'


def _layernorm(x, g, b):
    xf = x.astype(jnp.float32)
    mu = jnp.mean(xf, axis=-1, keepdims=True)
    var = jnp.mean(jnp.square(xf - mu), axis=-1, keepdims=True)
    y = (xf - mu) * lax.rsqrt(var + EPS) * g.astype(jnp.float32) + b.astype(jnp.float32)
    return y.astype(x.dtype)


def _modulate(s, shift, scale):
    return s * (1 + scale) + shift


def _swiglu(h, w_in, w_out):
    gate, up = jnp.split(h @ w_in, 2, axis=-1)
    return (jax.nn.silu(gate) * up) @ w_out


def _half_ffn(s, shift, scale, gate, w1, w2, g, b):
    return _layernorm(ALPHA * s + 0.5 * gate * _swiglu(_modulate(s, shift, scale), w1, w2), g, b)


def _dwconv(x, w, b, pad_left, pad_right):
    y = lax.conv_general_dilated(
        x, w.astype(x.dtype)[:, None, :], window_strides=(1,),
        padding=[(pad_left, pad_right)], dimension_numbers=('NWC', 'WIO', 'NWC'),
        feature_group_count=x.shape[-1])
    return y + b


def _lin_combine(left, right):
    a1, b1 = left
    a2, b2 = right
    return a1 * a2, a2 * b1 + b2


def _rglru(xc, w_r, b_r, w_i, b_i, lam, h0, reverse):
    bsz, t, c = xc.shape
    xf = xc.astype(jnp.float32)
    xh = xf.reshape(bsz, t, N_LRU_HEADS, LRU_HEAD_DIM)
    r = jax.nn.sigmoid(jnp.einsum('bthi,hij->bthj', xh, w_r.astype(jnp.float32)).reshape(bsz, t, c)
                       + b_r.astype(jnp.float32))
    i = jax.nn.sigmoid(jnp.einsum('bthi,hij->bthj', xh, w_i.astype(jnp.float32)).reshape(bsz, t, c)
                       + b_i.astype(jnp.float32))
    log_a = -RG_C * r * jax.nn.softplus(-lam.astype(jnp.float32))
    a = jnp.exp(log_a)
    b = jnp.sqrt(-jnp.expm1(2.0 * log_a)) * (i * xf)
    if h0 is not None:
        idx = t - 1 if reverse else 0
        b = b.at[:, idx].add(a[:, idx] * h0)
    _, h = lax.associative_scan(_lin_combine, (a, b), axis=1, reverse=reverse)
    return h


def _conv_module(v, gate, w, b, g, bb, n_seg):
    bsz, t, c = v.shape
    u = (v * jax.nn.sigmoid(gate)).reshape(bsz * n_seg, t // n_seg, c)
    u = _dwconv(u, w, b, CONV_K // 2, CONV_K // 2).reshape(bsz, t, c)
    return jax.nn.silu(_layernorm(u, g, bb))


def _mixer(h_lat, h_ctx, rows, w_in, conv4_w, conv4_b, w_rg, b_rg, w_ig, b_ig, lam,
           conv31_w, conv31_b, cln_g, cln_b, w_out, b_out, ctx_out):
    splits = [D_LRU, 2 * D_LRU, 2 * D_LRU + D_CONV]
    xr_l, gr_l, cv_l, cg_l = jnp.split(h_lat @ w_in, splits, axis=-1)
    if ctx_out:
        xr_c, gr_c, cv_c, cg_c = jnp.split(h_ctx @ w_in, splits, axis=-1)
    else:
        xr_c = h_ctx @ w_in[:, :D_LRU]
    xr_l = _dwconv(xr_l, conv4_w, conv4_b, SHORT_PAD_L, SHORT_PAD_R)
    xr_c = _dwconv(xr_c, conv4_w, conv4_b, SHORT_PAD_L, SHORT_PAD_R)
    rec_l = []
    rec_c = []
    for d, rev in ((0, False), (1, True)):
        h_c = _rglru(xr_c, w_rg[d], b_rg[d], w_ig[d], b_ig[d], lam[d], None, rev)
        h0 = h_c[:, 0] if rev else h_c[:, -1]
        h_l = _rglru(xr_l, w_rg[d], b_rg[d], w_ig[d], b_ig[d], lam[d], h0, rev)
        rec_l.append(h_l)
        rec_c.append(h_c)
    y_rec_l = (rec_l[0] + rec_l[1]).astype(h_lat.dtype) * jax.nn.gelu(gr_l)
    y_conv_l = _conv_module(cv_l, cg_l, conv31_w, conv31_b, cln_g, cln_b, rows)
    y_lat = jnp.concatenate([y_rec_l, y_conv_l], axis=-1) @ w_out + b_out
    if not ctx_out:
        return y_lat, None
    y_rec_c = (rec_c[0] + rec_c[1]).astype(h_ctx.dtype) * jax.nn.gelu(gr_c)
    y_conv_c = _conv_module(cv_c, cg_c, conv31_w, conv31_b, cln_g, cln_b, 1)
    y_ctx = jnp.concatenate([y_rec_c, y_conv_c], axis=-1) @ w_out + b_out
    return y_lat, y_ctx


def setup_inputs(seed: int = 0) -> dict:
    key = jax.random.key(seed)
    ks = jax.random.split(key, 32)

    def nrm(k, shape, scale):
        return jax.random.normal(k, shape, jnp.float32) * scale

    x = nrm(ks[0], (BATCH, SEQ, D_MODEL), 1.0)
    c = nrm(ks[1], (BATCH, D_MODEL), 1.0)
    ctx = nrm(ks[2], (BATCH, CTX_LEN, D_MODEL), 1.0)
    c_ctx = nrm(ks[3], (D_MODEL,), 1.0)
    w_ada = nrm(ks[4], (DEPTH, D_MODEL, N_MOD * D_MODEL), ADA_SCALE * D_MODEL ** -0.5)
    b_ada = nrm(ks[5], (DEPTH, N_MOD * D_MODEL), 0.02)
    ln_g = 1.0 + nrm(ks[6], (DEPTH, 3, D_MODEL), 0.02)
    ln_b = nrm(ks[7], (DEPTH, 3, D_MODEL), 0.02)
    ff1_in = nrm(ks[8], (DEPTH, D_MODEL, 2 * D_FF), D_MODEL ** -0.5)
    ff1_out = nrm(ks[9], (DEPTH, D_FF, D_MODEL), BETA * D_FF ** -0.5)
    ff2_in = nrm(ks[10], (DEPTH, D_MODEL, 2 * D_FF), D_MODEL ** -0.5)
    ff2_out = nrm(ks[11], (DEPTH, D_FF, D_MODEL), BETA * D_FF ** -0.5)
    w_in = nrm(ks[12], (DEPTH, D_MODEL, D_IN), D_MODEL ** -0.5)
    conv4_w = nrm(ks[13], (DEPTH, CONV_SHORT, D_LRU), CONV_SHORT ** -0.5)
    conv4_b = nrm(ks[14], (DEPTH, D_LRU), 0.02)
    w_rg = nrm(ks[15], (DEPTH, 2, N_LRU_HEADS, LRU_HEAD_DIM, LRU_HEAD_DIM), LRU_HEAD_DIM ** -0.5)
    b_rg = nrm(ks[16], (DEPTH, 2, D_LRU), 0.02)
    w_ig = nrm(ks[17], (DEPTH, 2, N_LRU_HEADS, LRU_HEAD_DIM, LRU_HEAD_DIM), LRU_HEAD_DIM ** -0.5)
    b_ig = nrm(ks[18], (DEPTH, 2, D_LRU), 0.02)
    a_c = jax.random.uniform(ks[19], (DEPTH, 2, D_LRU), jnp.float32, minval=0.9, maxval=0.999)
    a_base = a_c ** (1.0 / RG_C)
    lam = jnp.log(a_base) - jnp.log1p(-a_base)
    conv31_w = nrm(ks[20], (DEPTH, CONV_K, D_CONV), CONV_K ** -0.5)
    conv31_b = nrm(ks[21], (DEPTH, D_CONV), 0.02)
    cln_g = 1.0 + nrm(ks[22], (DEPTH, D_CONV), 0.02)
    cln_b = nrm(ks[23], (DEPTH, D_CONV), 0.02)
    w_out = nrm(ks[24], (DEPTH, D_MIX, D_MODEL), BETA * D_MIX ** -0.5)
    b_out = nrm(ks[25], (DEPTH, D_MODEL), 0.02)
    return {'x': x, 'c': c, 'ctx': ctx, 'c_ctx': c_ctx, 'w_ada': w_ada, 'b_ada': b_ada,
            'ln_g': ln_g, 'ln_b': ln_b, 'ff1_in': ff1_in, 'ff1_out': ff1_out,
            'ff2_in': ff2_in, 'ff2_out': ff2_out, 'w_in': w_in, 'conv4_w': conv4_w,
            'conv4_b': conv4_b, 'w_rg': w_rg, 'b_rg': b_rg, 'w_ig': w_ig, 'b_ig': b_ig,
            'lam': lam, 'conv31_w': conv31_w, 'conv31_b': conv31_b, 'cln_g': cln_g,
            'cln_b': cln_b, 'w_out': w_out, 'b_out': b_out}


def reference(x, c, ctx, c_ctx, w_ada, b_ada, ln_g, ln_b, ff1_in, ff1_out, ff2_in, ff2_out,
              w_in, conv4_w, conv4_b, w_rg, b_rg, w_ig, b_ig, lam, conv31_w, conv31_b,
              cln_g, cln_b, w_out, b_out):
    rows = x.shape[1] // GRID_W
    for l in range(DEPTH):
        last = l == DEPTH - 1
        m = jnp.split((jax.nn.silu(c) @ w_ada[l] + b_ada[l])[:, None, :], N_MOD, axis=-1)
        mc = jnp.split((jax.nn.silu(c_ctx) @ w_ada[l] + b_ada[l])[None, None, :], N_MOD, axis=-1)
        x = _half_ffn(x, m[0], m[1], m[2], ff1_in[l], ff1_out[l], ln_g[l, 0], ln_b[l, 0])
        ctx = _half_ffn(ctx, mc[0], mc[1], mc[2], ff1_in[l], ff1_out[l], ln_g[l, 0], ln_b[l, 0])
        y_lat, y_ctx = _mixer(_modulate(x, m[3], m[4]), _modulate(ctx, mc[3], mc[4]), rows,
                              w_in[l], conv4_w[l], conv4_b[l], w_rg[l], b_rg[l], w_ig[l], b_ig[l],
                              lam[l], conv31_w[l], conv31_b[l], cln_g[l], cln_b[l], w_out[l],
                              b_out[l], not last)
        x = _layernorm(ALPHA * x + m[5] * y_lat, ln_g[l, 1], ln_b[l, 1])
        if not last:
            ctx = _layernorm(ALPHA * ctx + mc[5] * y_ctx, ln_g[l, 1], ln_b[l, 1])
            ctx = _half_ffn(ctx, mc[6], mc[7], mc[8], ff2_in[l], ff2_out[l], ln_g[l, 2], ln_b[l, 2])
        x = _half_ffn(x, m[6], m[7], m[8], ff2_in[l], ff2_out[l], ln_g[l, 2], ln_b[l, 2])
    return x
```

```python
import numpy as np
import concourse.bass as bass
import concourse.mybir as mybir
from concourse.bass_utils import run_bass_kernel_spmd

F32 = mybir.dt.float32
BF16 = mybir.dt.bfloat16
AF = mybir.ActivationFunctionType
ALU = mybir.AluOpType

RG_C = 8.0
EPS = 1e-6
SAME_SYNC = True
STQ = "act"
ENGS = ("pe", "act", "dve", "pool", "sp")


class Cfg:
    def __init__(s, D=2048, FF=5632, DLRU=1024, NH=4, DCONV=1024, T=8192, TC=256, DEPTH=2,
                 TB=768, TS=256, NCORES=2):
        s.D, s.FF, s.DLRU, s.NH, s.DCONV, s.T, s.TC, s.DEPTH = D, FF, DLRU, NH, DCONV, T, TC, DEPTH
        s.TB, s.TS, s.NCORES = TB, TS, NCORES
        s.HALF = TB // 2
        s.GW, s.K31 = 64, 31
        s.KD, s.FK, s.LK, s.CK = D // 128, FF // 128, DLRU // 128, DCONV // 128
        s.DMIX = DLRU + DCONV
        s.MK = s.DMIX // 128
        s.DIN = 2 * DLRU + 2 * DCONV
        s.NT = T + TC
        s.NB = s.NT // TB
        s.HD = DLRU // NH
        s.HC = s.HD // 128
        s.ALPHA = (2 * DEPTH) ** 0.25
        assert s.NT % TB == 0 and TB % 128 == 0 and s.HALF % 64 == 0 and TC <= s.HALF
        assert TC <= TS and (T % TB) in (0, TB - TC)
        o = {}
        off = 0
        for name, n in (("bada", 9 * s.KD), ("lng", 3 * s.KD), ("lnb", 3 * s.KD), ("c4w", s.LK * 4),
                        ("c4b", s.LK), ("brg", 2 * s.LK), ("big", 2 * s.LK), ("lam", 2 * s.LK),
                        ("c31w", s.CK * 31), ("c31b", s.CK), ("clng", s.CK), ("clnb", s.CK),
                        ("bout", s.KD)):
            o[name] = off
            off += n
        s.so, s.NS = o, off


class Op:
    __slots__ = ("eng", "fn", "deps", "sig", "dkey", "waits", "needed")

    def __init__(s, eng, fn, dkey):
        s.eng, s.fn, s.dkey = eng, fn, dkey
        s.deps, s.sig, s.waits, s.needed = [], None, [], False


class Sched:
    def __init__(s, nc, stack):
        s.nc = nc
        s.stack = stack
        s.ops = {e: [] for e in ENGS}
        s.lastw, s.readers = {}, {}
        s.engsem = {e: stack.enter_context(nc.semaphore("es_" + e)) for e in ENGS if e != "sp"}
        s.engcnt = {e: 0 for e in s.engsem}
        s.dsem = {}
        s.seen = {e: {} for e in ENGS}
        s.first_phase = True
        s.nops = 0

    def add(s, eng, fn, r=(), w=(), dma=None):
        op = Op(eng, fn, dma)
        deps = []
        for k in r:
            lw = s.lastw.get(k)
            if lw is not None:
                deps.append(lw)
        for k in w:
            lw = s.lastw.get(k)
            if lw is not None:
                deps.append(lw)
            deps.extend(s.readers.get(k, ()))
        for k in r:
            s.readers.setdefault(k, []).append(op)
        for k in w:
            s.lastw[k] = op
            s.readers[k] = []
        seen = set()
        for d in deps:
            if d is op or id(d) in seen:
                continue
            seen.add(id(d))
            if d.dkey is None and d.eng == eng and (eng == "pe" or (not SAME_SYNC and eng != "pool")):
                continue
            op.deps.append(d)
            d.needed = True
        s.ops[eng].append(op)
        s.nops += 1
        return op

    def _dsem(s, key):
        if key not in s.dsem:
            s.dsem[key] = [s.stack.enter_context(s.nc.semaphore("ds%d" % len(s.dsem))), 0]
        return s.dsem[key]

    def flush(s, final=False):
        nc = s.nc
        pre = {e: [] for e in ENGS}
        if not s.first_phase:
            for e in ENGS:
                for e2, sem in s.engsem.items():
                    if e2 != e and s.engcnt[e2] > 0:
                        pre[e].append((sem, s.engcnt[e2]))
                for key, (sem, cnt) in s.dsem.items():
                    if cnt > 0:
                        pre[e].append((sem, cnt))
        s.first_phase = False
        for e in ENGS:
            lst = s.ops[e]
            lastc = max([i for i, op in enumerate(lst) if op.dkey is None], default=-1)
            for i, op in enumerate(lst):
                last = i == lastc
                if op.dkey is not None:
                    d = s._dsem(op.dkey)
                    d[1] += 16
                    op.sig = (d[0], d[1], 16)
                elif op.needed or last:
                    s.engcnt[e] += 1
                    op.sig = (s.engsem[e], s.engcnt[e], 1)
        for e in ENGS:
            seen = s.seen[e]
            for sem, v in pre[e]:
                seen[id(sem)] = max(seen.get(id(sem), 0), v)
            for op in s.ops[e]:
                for d in op.deps:
                    sem, v, _ = d.sig
                    if seen.get(id(sem), 0) >= v:
                        continue
                    seen[id(sem)] = v
                    op.waits.append((sem, v))
        post = []
        if final:
            post = [(sem, cnt) for (sem, cnt) in s.dsem.values() if cnt > 0]
            post += [(sem, s.engcnt[e2]) for e2, sem in s.engsem.items() if s.engcnt[e2] > 0]
        ops = s.ops

        def replay(eng_name):
            def run(e):
                for sem, v in pre[eng_name]:
                    e.wait_ge(sem, v)
                for op in ops[eng_name]:
                    for sem, v in op.waits:
                        e.wait_ge(sem, v)
                    ins = op.fn(e)
                    if op.sig is not None:
                        ins.then_inc(op.sig[0], op.sig[2])
                if eng_name == "sp":
                    for sem, v in post:
                        e.wait_ge(sem, v)
            return run

        with nc.Block() as block:
            block.tensor(replay("pe"))
            block.scalar(replay("act"))
            block.vector(replay("dve"))
            block.gpsimd(replay("pool"))
            block.sync(replay("sp"))
        s.ops = {e: [] for e in ENGS}
        s.lastw, s.readers = {}, {}


class Builder:
    def __init__(s, cfg):
        s.c = cfg

    def build(s):
        from contextlib import ExitStack
        c = s.c
        nc = bass.Bass("TRN2", target_bir_lowering=False)
        s.nc = nc
        L = c.DEPTH
        dt = nc.dram_tensor
        s.xin = dt("xin", [c.D, c.NT], F32, kind="ExternalInput").ap()
        s.cvec = dt("cvec", [128, c.KD * 2], F32, kind="ExternalInput").ap()
        s.small = dt("small", [L, 128, c.NS], F32, kind="ExternalInput").ap()
        s.wada = dt("wada", [L, 9 * c.KD, 128, c.D], F32, kind="ExternalInput").ap()
        s.w1 = dt("w1", [L, 2, 2 * c.FK, 128, c.D], F32, kind="ExternalInput").ap()
        s.w2 = dt("w2", [L, 2, c.KD, 128, c.FF], F32, kind="ExternalInput").ap()
        s.win = dt("win", [L, c.DIN // 128, 128, c.D], F32, kind="ExternalInput").ap()
        s.wout = dt("wout", [L, c.KD, 128, c.DMIX], F32, kind="ExternalInput").ap()
        s.gwt = dt("gwt", [L, 128, 2 * 2 * c.NH * c.HC * c.HD], F32, kind="ExternalInput").ap()
        s.identd = dt("ident", [128, 128], F32, kind="ExternalInput").ap()
        s.yout = dt("yout", [c.D, c.NT], F32, kind="ExternalOutput").ap()
        s.SA = dt("SA", [c.D, c.NT], F32, kind="Internal").ap()
        s.SB = dt("SB", [c.D, c.NT], F32, kind="Internal").ap()
        s.SC = dt("SC", [c.D, c.NT], F32, kind="Internal").ap()
        s.Uscr = dt("Uscr", [2, c.D, c.TB], F32, kind="Internal").ap()
        s.XR = dt("XR", [c.DLRU, c.NT], F32, kind="Internal").ap()
        s.GG = dt("GG", [c.DLRU, c.NT], F32, kind="Internal").ap()
        s.REC = dt("REC", [c.DLRU, c.NT], F32, kind="Internal").ap()
        s.YR = dt("YR", [c.DLRU, c.NT], BF16, kind="Internal").ap()
        s.YC = dt("YC", [c.DCONV, c.NT], BF16, kind="Internal").ap()

        with ExitStack() as st:
            s.st = st
            s.S = Sched(nc, st)
            s.ps = [st.enter_context(nc.psum_tensor("ps%d" % i, [128, 512], F32)) for i in range(8)]
            T_ = s.mkT(st)
            s.ones = T_("ones", [128, 128])
            s.cv32 = T_("cv32", [128, c.KD * 2])
            s.scb = T_("scb", [128, c.KD, 2], BF16)
            s.smallt = T_("smallt", [128, c.NS])
            s.modv = T_("modv", [128, 9 * c.KD, 2])
            s.bgv = T_("bgv", [128, c.KD, 2])
            s.clam = T_("clam", [128, 2 * c.LK])
            s.clam2 = T_("clam2", [128, 2 * c.LK])
            s.ctmp = [T_("ctmp%d" % i, [128, 2 * c.LK]) for i in range(4)]
            S = s.S
            s.ident = T_("ident", [128, 128])
            S.add("sp", lambda e: e.dma_start(out=s.ident[:], in_=s.identd), w=[("ident",)], dma=("ident",))
            S.add("dve", lambda e: e.memset(s.ones[:], 1.0), w=[("ones",)])
            S.add("sp", lambda e: e.dma_start(out=s.cv32[:], in_=s.cvec), w=[("cv32",)], dma=("cv32",))
            S.add("act", lambda e: e.activation(out=s.scb[:].rearrange("p k c -> p (k c)"), in_=s.cv32[:], func=AF.Silu),
                  r=[("cv32",)], w=[("scb",)])
            stop = getattr(c, "STOP", None)
            done = False
            for l in range(L):
                last = l == L - 1
                src_ = s.xin if l == 0 else s.SC
                steps = [("ada", lambda: s.phase_ada(l), None),
                         ("ffn0", lambda: s.phase_ffn(l, 0, src_, s.SA, final=(stop == (l, "ffn0"))), s.SA),
                         ("mixa", lambda: s.phase_mixa(l, s.SA), None),
                         ("scan", lambda: s.phase_scan(l, last), None),
                         ("mixc", lambda: s.phase_mixc(l, s.SA, s.SB), s.SB),
                         ("ffn1", lambda: s.phase_ffn(l, 1, s.SB, s.yout if last else s.SC, final=last), s.SC)]
                for name, fn, outstream in steps:
                    fn()
                    if stop == (l, name):
                        s.dbg_copy(outstream)
                        done = True
                        break
                if done:
                    break
        return nc

    def dbg_copy(s, stream):
        S = s.S
        c = s.c
        for j in range(c.KD):
            S.add("sp", lambda e, j=j: e.dma_start(out=s.yout[j * 128:(j + 1) * 128, :], in_=stream[j * 128:(j + 1) * 128, :]),
                  dma=("dbg", j))
        S.flush(final=True)

    def mkT(s, st):
        def T_(name, shape, dtp=F32):
            s.uid = getattr(s, "uid", 0) + 1
            return st.enter_context(s.nc.sbuf_tensor("%s_%d" % (name, s.uid), shape, dtp))
        return T_

    def sm(s, name, idx, n=1):
        o = s.c.so[name] + idx
        return s.smallt[:, o:o + n]

    def segs(s, blk):
        c = s.c
        lo, hi = blk * c.TB, (blk + 1) * c.TB
        out = []
        if lo < c.T:
            e_ = min(hi, c.T) - lo
            if getattr(c, "SPLIT", False) and e_ == c.TB:
                out.append((0, 512, 0))
                out.append((512, e_, 0))
            else:
                out.append((0, e_, 0))
        if hi > c.T:
            out.append((max(lo, c.T) - lo, c.TB, 1))
        return out

    def phase_ada(s, l):
        c, S, nc = s.c, s.S, s.nc
        NJ = 9 * c.KD
        with nc.sbuf_tensor("wa_%d" % l, [128, 3, c.D], BF16) as wa:
            S.add("sp", lambda e: e.dma_start(out=s.smallt[:], in_=s.small[l]), w=[("small",)], dma=("small",))
            for j in range(NJ):
                sl = j % 3
                S.add("pool", lambda e, j=j, sl=sl: e.dma_start(out=wa[:, sl, :], in_=s.wada[l, j], max_dma_last_dim=8192),
                      w=[("wa", sl)], dma=("wa", sl))
                for k in range(c.KD):
                    S.add("pe", lambda e, j=j, k=k, sl=sl: e.matmul(s.ps[0][:, 2 * j:2 * j + 2], wa[:, sl, k * 128:(k + 1) * 128],
                                                                  s.scb[:, k, :], start=(k == 0), stop=(k == c.KD - 1)),
                          r=[("wa", sl), ("scb",)], w=[("ps", 0)])
            psv = s.ps[0][:, 0:2 * NJ].rearrange("p (j c) -> p j c", c=2)
            for col in range(2):
                S.add("dve", lambda e, col=col: e.tensor_tensor(out=s.modv[:, :, col], in0=psv[:, :, col],
                                                                in1=s.sm("bada", 0, NJ), op=ALU.add),
                      r=[("ps", 0), ("small",)], w=[("modv",)])
            for sub in range(3):
                coef = (1.0 if sub == 1 else 0.5) / c.ALPHA
                S.add("dve", lambda e, sub=sub: e.tensor_scalar(s.modv[:, (3 * sub + 1) * c.KD:(3 * sub + 2) * c.KD, :],
                                                                s.modv[:, (3 * sub + 1) * c.KD:(3 * sub + 2) * c.KD, :],
                                                                1.0, None, ALU.add), r=[("modv",)], w=[("modv",)])
                S.add("dve", lambda e, sub=sub, coef=coef: e.tensor_scalar(s.modv[:, (3 * sub + 2) * c.KD:(3 * sub + 3) * c.KD, :],
                                                                           s.modv[:, (3 * sub + 2) * c.KD:(3 * sub + 3) * c.KD, :],
                                                                           coef, None, ALU.mult), r=[("modv",)], w=[("modv",)])
            for col in range(2):
                S.add("dve", lambda e, col=col: e.tensor_tensor(out=s.bgv[:, :, col], in0=s.modv[:, 5 * c.KD:6 * c.KD, col],
                                                                in1=s.sm("bout", 0, c.KD), op=ALU.mult),
                      r=[("modv",), ("small",)], w=[("bgv",)])
            t0, t1, t2, t3 = s.ctmp
            lam = s.sm("lam", 0, 2 * c.LK)
            K = ("clamk",)
            S.add("act", lambda e: e.activation(out=t0[:], in_=lam, func=AF.Exp, scale=-1.0), r=[("small",)], w=[K])
            S.add("act", lambda e: e.activation(out=t1[:], in_=t0[:], func=AF.Ln, bias=1.0), r=[K], w=[K])
            S.add("dve", lambda e: e.tensor_scalar(t2[:], t0[:], 0.05, None, ALU.min), r=[K], w=[K])
            S.add("dve", lambda e: e.tensor_scalar(t3[:], t2[:], 0.2, -0.25, ALU.mult, ALU.add), r=[K], w=[K])
            for cst in (1.0 / 3.0, -0.5, 1.0):
                S.add("dve", lambda e: e.tensor_tensor(out=t3[:], in0=t3[:], in1=t2[:], op=ALU.mult), r=[K], w=[K])
                S.add("dve", lambda e, cst=cst: e.tensor_scalar(t3[:], t3[:], cst, None, ALU.add), r=[K], w=[K])
            S.add("dve", lambda e: e.tensor_tensor(out=t3[:], in0=t3[:], in1=t2[:], op=ALU.mult), r=[K], w=[K])
            S.add("dve", lambda e: e.tensor_scalar(t2[:], t0[:], 0.05, None, ALU.is_lt), r=[K], w=[K])
            S.add("dve", lambda e: e.tensor_tensor(out=t3[:], in0=t3[:], in1=t1[:], op=ALU.subtract), r=[K], w=[K])
            S.add("dve", lambda e: e.tensor_tensor(out=t3[:], in0=t3[:], in1=t2[:], op=ALU.mult), r=[K], w=[K])
            S.add("dve", lambda e: e.tensor_tensor(out=t3[:], in0=t3[:], in1=t1[:], op=ALU.add), r=[K], w=[K])
            S.add("dve", lambda e: e.tensor_scalar(s.clam[:], t3[:], -RG_C, None, ALU.mult), r=[K], w=[("clam",)])
            S.add("dve", lambda e: e.tensor_scalar(s.clam2[:], t3[:], -2.0 * RG_C, None, ALU.mult), r=[K], w=[("clam",)])
            S.flush()

    def modulate(s, src, blk, sub, xmod, xs, nslot, cnt):
        c, S = s.c, s.S
        for k in range(c.KD):
            sl = cnt[0] % nslot
            cnt[0] += 1
            S.add("sp", lambda e, k=k, sl=sl: e.dma_start(out=xs[:, sl, :], in_=src[k * 128:(k + 1) * 128, blk * c.TB:(blk + 1) * c.TB]),
                  w=[("xs", sl)], dma=("xs", sl))
            for (lo, hi, which) in s.segs(blk):
                sc1 = s.modv[:, (3 * sub + 1) * c.KD + k, which:which + 1]
                sh = s.modv[:, (3 * sub) * c.KD + k, which:which + 1]
                if k % 2 == 0:
                    S.add("act", lambda e, k=k, sl=sl, lo=lo, hi=hi, sc1=sc1, sh=sh: e.activation(
                        out=xmod[:, k, lo:hi], in_=xs[:, sl, lo:hi], func=AF.Identity, scale=sc1, bias=sh),
                        r=[("xs", sl), ("modv",)], w=[("R1", k)])
                else:
                    S.add("dve", lambda e, k=k, sl=sl, lo=lo, hi=hi, sc1=sc1, sh=sh: e.tensor_scalar(
                        xmod[:, k, lo:hi], xs[:, sl, lo:hi], sc1, sh, ALU.mult, ALU.add),
                        r=[("xs", sl), ("modv",)], w=[("R1", k)])

    def ln_tail(s, u32, nch, Dn, eps, lnt, emit_out, ukeys=None):
        c, S = s.c, s.S
        (mean, kmean), (rstd, krstd), (nmr, knmr), (tmp, ktmp) = lnt
        if ukeys is None:
            ukeys = lambda j: [("u", j)]
        H = c.HALF
        inv = 1.0 / Dn
        for h in range(2):
            cs = slice(h * H, (h + 1) * H)
            S.add("act", lambda e, h=h, cs=cs: e.activation(out=mean[:, cs], in_=s.ps[4 + h][:, 0:H], func=AF.Identity, scale=inv),
                  r=[("ps", 4 + h)], w=[kmean])
            S.add("dve", lambda e, cs=cs: e.tensor_tensor(out=tmp[:, cs], in0=mean[:, cs], in1=mean[:, cs], op=ALU.mult),
                  r=[kmean], w=[ktmp])
            S.add("dve", lambda e, h=h, cs=cs: e.scalar_tensor_tensor(out=tmp[:, cs], in0=s.ps[6 + h][:, 0:H], scalar=inv, in1=tmp[:, cs],
                                                                      op0=ALU.mult, op1=ALU.subtract),
                  r=[("ps", 6 + h), ktmp], w=[ktmp])
            S.add("act", lambda e, cs=cs: e.activation(out=tmp[:, cs], in_=tmp[:, cs], func=AF.Sqrt, bias=s.epst[eps][:, 0:1]),
                  r=[ktmp, ("eps", eps)], w=[ktmp])
            S.add("dve", lambda e, cs=cs: e.reciprocal(rstd[:, cs], tmp[:, cs]), r=[ktmp], w=[krstd])
            S.add("dve", lambda e, cs=cs: e.scalar_tensor_tensor(out=nmr[:, cs], in0=mean[:, cs], scalar=-1.0, in1=rstd[:, cs],
                                                                 op0=ALU.mult, op1=ALU.mult),
                  r=[kmean, krstd], w=[knmr])
        for j in range(nch):
            S.add("dve", lambda e, j=j: e.tensor_tensor(out=u32[:, j, :], in0=u32[:, j, :], in1=rstd[:, :], op=ALU.mult),
                  r=ukeys(j) + [krstd], w=ukeys(j))
            S.add("dve", lambda e, j=j: e.tensor_tensor(out=u32[:, j, :], in0=u32[:, j, :], in1=nmr[:, :], op=ALU.add),
                  r=ukeys(j) + [knmr], w=ukeys(j))
            emit_out(j)

    def stats_mm(s, j, nch, src_u, src_sq, ukey, sqkey):
        c, S = s.c, s.S
        H = c.HALF
        for h in range(2):
            S.add("pe", lambda e, h=h: e.matmul(s.ps[4 + h][:, 0:H], s.ones[:], src_u[:, h * H:(h + 1) * H], start=(j == 0), stop=(j == nch - 1)),
                  r=[ukey, ("ones",)], w=[("ps", 4 + h)])
            S.add("pe", lambda e, h=h: e.matmul(s.ps[6 + h][:, 0:H], s.ones[:], src_sq[:, h * H:(h + 1) * H], start=(j == 0), stop=(j == nch - 1)),
                  r=[sqkey, ("ones",)], w=[("ps", 6 + h)])

    def u_keys(s, j):
        return [("R1", 2 * j), ("R1", 2 * j + 1), ("u", j)]

    def eps_tiles(s, st):
        c = s.c
        s.epst = {}
        for name, val in (("ffn", EPS / (c.ALPHA ** 2)), ("conv", EPS)):
            t = s.mkT(st)("eps_" + name, [128, 1], F32)
            s.S.add("dve", lambda e, t=t, val=val: e.memset(t[:], val), w=[("eps", name)])
            s.epst[name] = t

    def phase_ffn(s, l, which_ffn, src, dst, final=False):
        from contextlib import ExitStack
        c, S, nc = s.c, s.S, s.nc
        sub = 0 if which_ffn == 0 else 2
        H, TB, KD, FK = c.HALF, c.TB, c.KD, c.FK
        NW1 = 3
        U = s.Uscr
        with ExitStack() as st:
            T_ = s.mkT(st)
            xm = T_("xm", [128, 2, KD, TB], BF16)
            act = T_("act", [128, FK, TB], BF16)
            w1t = T_("w1t", [128, NW1, 2, c.D], BF16)
            w2t = T_("w2t", [128, 2, c.FF], BF16)
            xs = T_("xs", [128, 2, TB])
            uo = T_("uo", [128, 2, TB])
            sq = T_("sq", [128, 2, TB])
            nin = T_("nin", [128, 2, TB])
            sgt = T_("sgt", [128, 2, H])
            lr = T_("lr", [128, 2, TB])
            lnn = T_("lnn", [128, 2, TB])
            s.eps_tiles(st)
            cn = {"x": 0, "w1": 0, "w2": 0, "sg": 0, "n": 0}

            def modulate(blk):
                par = blk % 2
                for k in range(KD):
                    sl = cn["x"] % 2
                    cn["x"] += 1
                    S.add("sp", lambda e, k=k, sl=sl: e.dma_start(out=xs[:, sl, :], in_=src[k * 128:(k + 1) * 128, blk * TB:(blk + 1) * TB]),
                          w=[("xs", sl)], dma=("xs", sl))
                    for (lo, hi, which) in s.segs(blk):
                        sc1 = s.modv[:, (3 * sub + 1) * KD + k, which:which + 1]
                        sh = s.modv[:, (3 * sub) * KD + k, which:which + 1]
                        if k % 2 == 0:
                            S.add("act", lambda e, k=k, sl=sl, lo=lo, hi=hi, sc1=sc1, sh=sh: e.activation(
                                out=xm[:, par, k, lo:hi], in_=xs[:, sl, lo:hi], func=AF.Identity, scale=sc1, bias=sh),
                                r=[("xs", sl), ("modv",)], w=[("xm", par, k)])
                        else:
                            S.add("dve", lambda e, k=k, sl=sl, lo=lo, hi=hi, sc1=sc1, sh=sh: e.tensor_scalar(
                                xm[:, par, k, lo:hi], xs[:, sl, lo:hi], sc1, sh, ALU.mult, ALU.add),
                                r=[("xs", sl), ("modv",)], w=[("xm", par, k)])

            def norm_chunk(blk, j):
                par = blk % 2
                o = cn["n"] % 2
                cn["n"] += 1
                S.add("sp", lambda e: e.dma_start(out=nin[:, o, :], in_=U[par, j * 128:(j + 1) * 128, :]),
                      r=[("U", par, j)], w=[("nin", o)], dma=("nin", o))
                S.add("dve", lambda e: e.tensor_tensor(out=nin[:, o, :], in0=nin[:, o, :], in1=lr[:, par, :], op=ALU.mult),
                      r=[("nin", o), ("lr", par)], w=[("nin", o)])
                S.add("dve", lambda e: e.tensor_tensor(out=nin[:, o, :], in0=nin[:, o, :], in1=lnn[:, par, :], op=ALU.add),
                      r=[("nin", o), ("lnn", par)], w=[("nin", o)])
                S.add("act", lambda e: e.activation(out=nin[:, o, :], in_=nin[:, o, :], func=AF.Identity,
                                                    scale=s.sm("lng", sub * KD + j), bias=s.sm("lnb", sub * KD + j)),
                      r=[("nin", o), ("small",)], w=[("nin", o)])
                S.add(STQ, lambda e: e.dma_start(out=dst[j * 128:(j + 1) * 128, blk * TB:(blk + 1) * TB], in_=nin[:, o, :]),
                      r=[("nin", o)], dma=("nin", "st", o))

            def h_phase(blk):
                par = blk % 2
                for f in range(FK):
                    sl = cn["w1"] % NW1
                    cn["w1"] += 1
                    for gu in range(2):
                        S.add("pool", lambda e, f=f, sl=sl, gu=gu: e.dma_start(out=w1t[:, sl, gu, :], in_=s.w1[l, which_ffn, gu * FK + f],
                                                                              max_dma_last_dim=8192),
                              w=[("w1", sl, gu)], dma=("w1", sl, gu))
                    bs = 4 * (f % 2)
                    for gu in range(2):
                        for k in range(KD):
                            for h in range(2):
                                S.add("pe", lambda e, sl=sl, gu=gu, k=k, h=h, bs=bs: e.matmul(
                                    s.ps[bs + 2 * gu + h][:, 0:H], w1t[:, sl, gu, k * 128:(k + 1) * 128], xm[:, par, k, h * H:(h + 1) * H],
                                    start=(k == 0), stop=(k == KD - 1)),
                                    r=[("w1", sl, gu), ("xm", par, k)], w=[("ps", bs + 2 * gu + h)])
                    for h in range(2):
                        g = cn["sg"] % 2
                        cn["sg"] += 1
                        S.add("act", lambda e, g=g, h=h, bs=bs: e.activation(out=sgt[:, g, :], in_=s.ps[bs + h][:, 0:H], func=AF.Silu),
                              r=[("ps", bs + h)], w=[("sg", g)])
                        S.add("dve", lambda e, g=g, h=h, bs=bs, f=f: e.tensor_tensor(out=act[:, f, h * H:(h + 1) * H], in0=s.ps[bs + 2 + h][:, 0:H],
                                                                                     in1=sgt[:, g, :], op=ALU.mult),
                              r=[("ps", bs + 2 + h), ("sg", g)], w=[("act", f)])
                    if blk > 0 and f < KD:
                        norm_chunk(blk - 1, f)

            def y_phase(blk):
                par = blk % 2
                segs = s.segs(blk)
                pending = None
                for j in range(KD):
                    sl = cn["w2"] % 2
                    cn["w2"] += 1
                    S.add("pool", lambda e, j=j, sl=sl: e.dma_start(out=w2t[:, sl, :], in_=s.w2[l, which_ffn, j], max_dma_last_dim=8192),
                          w=[("w2", sl)], dma=("w2", sl))
                    xl = cn["x"] % 2
                    cn["x"] += 1
                    S.add("sp", lambda e, j=j, xl=xl: e.dma_start(out=xs[:, xl, :], in_=src[j * 128:(j + 1) * 128, blk * TB:(blk + 1) * TB]),
                          w=[("xs", xl)], dma=("xs", xl))
                    bs = 2 * (j % 2)
                    for f in range(FK):
                        for h in range(2):
                            S.add("pe", lambda e, sl=sl, f=f, h=h, bs=bs: e.matmul(
                                s.ps[bs + h][:, 0:H], w2t[:, sl, f * 128:(f + 1) * 128], act[:, f, h * H:(h + 1) * H],
                                start=(f == 0), stop=(f == FK - 1)),
                                r=[("w2", sl), ("act", f)], w=[("ps", bs + h)])
                    if pending is not None:
                        pending()
                    q = j % 2
                    for (lo, hi, which) in segs:
                        gcol = s.modv[:, (3 * sub + 2) * KD + j, which:which + 1]
                        for h in range(2):
                            a0, a1 = max(lo, h * H), min(hi, (h + 1) * H)
                            if a0 >= a1:
                                continue
                            S.add("dve", lambda e, h=h, a0=a0, a1=a1, gcol=gcol, xl=xl, bs=bs, q=q: e.scalar_tensor_tensor(
                                out=uo[:, q, a0:a1], in0=s.ps[bs + h][:, a0 - h * H:a1 - h * H], scalar=gcol, in1=xs[:, xl, a0:a1],
                                op0=ALU.mult, op1=ALU.add),
                                r=[("ps", bs + h), ("xs", xl), ("modv",)], w=[("uo", q)])
                    S.add("act", lambda e, q=q: e.activation(out=sq[:, q, :], in_=uo[:, q, :], func=AF.Square),
                          r=[("uo", q)], w=[("sq", q)])
                    S.add("sp", lambda e, j=j, q=q: e.dma_start(out=U[par, j * 128:(j + 1) * 128, :], in_=uo[:, q, :]),
                          r=[("uo", q)], w=[("U", par, j)], dma=("uo", q))

                    def mk(j=j, q=q):
                        return lambda: s.stats_mm(j, KD, uo[:, q, :], sq[:, q, :], ("uo", q), ("sq", q))
                    pending = mk()
                pending()
                lnt = [(sq[:, 1, :], ("sq", 1)), (lr[:, par, :], ("lr", par)), (lnn[:, par, :], ("lnn", par)), (sq[:, 0, :], ("sq", 0))]
                s.ln_tail(None, 0, c.D, "ffn", lnt, None)

            modulate(0)
            for blk in range(c.NB):
                h_phase(blk)
                if blk + 1 < c.NB:
                    modulate(blk + 1)
                y_phase(blk)
            for j in range(KD):
                norm_chunk(c.NB - 1, j)
            S.flush(final=final)

    def phase_mixa(s, l, src):
        from contextlib import ExitStack
        c, S, nc = s.c, s.S, s.nc
        H, TB, KD, LK, CK = c.HALF, c.TB, c.KD, c.LK, c.CK
        NR = TB // 64
        PW = 64 + 30
        PL = NR * PW
        with ExitStack() as st:
            T_ = s.mkT(st)
            xmod = T_("hmod", [128, KD, TB], BF16)
            xs = T_("xs", [128, 3, TB])
            wt = T_("wint", [128, 4, c.D], BF16)
            xo = T_("xo", [128, 3, TB])
            sig = T_("sig", [128, 2, TB])
            upad = T_("upad", [128, CK, PL], BF16)
            dg = T_("dg", [128, CK, c.K31, 128], BF16)
            acc = T_("acc", [128, CK, TB])
            sq = T_("sq", [128, 2, TB])
            yo = T_("yo", [128, 2, TB], BF16)
            lnt = [(T_("lnm", [128, TB]), ("ln", "mean")), (T_("lnr", [128, TB]), ("ln", "rstd")),
                   (T_("lnn", [128, TB]), ("ln", "nmr")), (T_("lnt", [128, TB]), ("ln", "tmp"))]
            s.eps_tiles(st)
            xcnt = [0]
            wcnt = 0
            xocnt = 0
            for cc in range(CK):
                for kk in range(c.K31):
                    wk = s.sm("c31w", cc * 31 + kk)
                    if (cc * 31 + kk) % 2 == 0:
                        S.add("dve", lambda e, cc=cc, kk=kk, wk=wk: e.tensor_scalar(dg[:, cc, kk, :], s.ident[:], wk, None, ALU.mult),
                              r=[("ident",), ("small",)], w=[("dg", cc)])
                    else:
                        S.add("act", lambda e, cc=cc, kk=kk, wk=wk: e.activation(out=dg[:, cc, kk, :], in_=s.ident[:], func=AF.Identity, scale=wk),
                              r=[("ident",), ("small",)], w=[("dg", cc)])

            def body(blk):
                nonlocal wcnt, xocnt
                segs = s.segs(blk)
                cols = slice(blk * TB, (blk + 1) * TB)
                has_ctx = any(w == 1 for (_, _, w) in segs)
                nlat = segs[0][1] if segs[0][2] == 0 else 0
                nrl = nlat // 64
                s.modulate(src, blk, 1, xmod, xs, 3, xcnt)
                if blk == 0 or has_ctx:
                    S.add("dve", lambda e: e.memset(upad[:], 0.0), w=[("upad", cc) for cc in range(CK)])
                order = list(range(2 * LK))
                for cc in range(CK):
                    order += [2 * LK + CK + cc, 2 * LK + cc]
                for oi, o in enumerate(order):
                    sl = wcnt % 4
                    bs = 2 * (wcnt % 2)
                    wcnt += 1
                    S.add("pool", lambda e, o=o, sl=sl: e.dma_start(out=wt[:, sl, :], in_=s.win[l, o], max_dma_last_dim=8192),
                          w=[("win", sl)], dma=("win", sl))
                    for k in range(KD):
                        for h in range(2):
                            S.add("pe", lambda e, sl=sl, k=k, h=h, bs=bs: e.matmul(
                                s.ps[bs + h][:, 0:H], wt[:, sl, k * 128:(k + 1) * 128], xmod[:, k, h * H:(h + 1) * H],
                                start=(k == 0), stop=(k == KD - 1)),
                                r=[("win", sl), ("R1", k)], w=[("ps", bs + h)])
                    if o < 2 * LK:
                        xl = xocnt % 3
                        xocnt += 1
                        isg = o >= LK
                        for h in range(2):
                            if isg:
                                S.add("act", lambda e, xl=xl, h=h, bs=bs: e.activation(out=xo[:, xl, h * H:(h + 1) * H], in_=s.ps[bs + h][:, 0:H], func=AF.Gelu),
                                      r=[("ps", bs + h)], w=[("xo", xl)])
                            else:
                                S.add("act", lambda e, xl=xl, h=h, bs=bs: e.activation(out=xo[:, xl, h * H:(h + 1) * H], in_=s.ps[bs + h][:, 0:H], func=AF.Identity),
                                      r=[("ps", bs + h)], w=[("xo", xl)])
                        dstt = s.GG if isg else s.XR
                        ch = o - LK if isg else o
                        S.add(STQ, lambda e, xl=xl, dstt=dstt, ch=ch: e.dma_start(out=dstt[ch * 128:(ch + 1) * 128, cols], in_=xo[:, xl, :]),
                              r=[("xo", xl)], dma=("xo", "st", xl))
                    elif o >= 2 * LK + CK:
                        cc = o - 2 * LK - CK
                        sgl = cc % 2
                        for h in range(2):
                            S.add("act", lambda e, sgl=sgl, h=h, bs=bs: e.activation(out=sig[:, sgl, h * H:(h + 1) * H], in_=s.ps[bs + h][:, 0:H], func=AF.Sigmoid),
                                  r=[("ps", bs + h)], w=[("sig", sgl)])
                    else:
                        cc = o - 2 * LK
                        sgl = cc % 2
                        for h in range(2):
                            r0, r1 = h * (H // 64), min((h + 1) * (H // 64), nrl)
                            if r1 > r0:
                                n = (r1 - r0) * 64
                                outv = upad[:, cc, r0 * PW:r1 * PW].rearrange("p (r w) -> p r w", w=PW)[:, :, 15:79]
                                S.add("dve", lambda e, outv=outv, n=n, h=h, bs=bs, sgl=sgl: e.tensor_tensor(
                                    out=outv, in0=s.ps[bs + h][:, 0:n].rearrange("p (r w) -> p r w", w=64),
                                    in1=sig[:, sgl, h * H:h * H + n].rearrange("p (r w) -> p r w", w=64), op=ALU.mult),
                                    r=[("ps", bs + h), ("sig", sgl)], w=[("upad", cc)])
                            if has_ctx and h == 1:
                                a0 = nlat - H
                                S.add("dve", lambda e, a0=a0, bs=bs, sgl=sgl, cc=cc: e.tensor_tensor(
                                    out=upad[:, cc, nrl * PW + 15:nrl * PW + 15 + c.TC], in0=s.ps[bs + 1][:, a0:a0 + c.TC],
                                    in1=sig[:, sgl, H + a0:H + a0 + c.TC], op=ALU.mult),
                                    r=[("ps", bs + 1), ("sig", sgl)], w=[("upad", cc)])
                        cb = 2 * (wcnt % 2)
                        wcnt += 1
                        R6 = H // 64
                        for h in range(2):
                            r0, r1 = h * R6, min((h + 1) * R6, nrl)
                            if r1 > r0:
                                n = (r1 - r0) * 64
                                for kk in range(c.K31):
                                    rv = upad[:, cc, r0 * PW:r1 * PW].rearrange("p (r w) -> p r w", w=PW)[:, :, kk:kk + 64]
                                    S.add("pe", lambda e, cc=cc, kk=kk, h=h, n=n, rv=rv, cb=cb: e.matmul(
                                        s.ps[cb + h][:, 0:n].rearrange("p (r w) -> p r w", w=64), dg[:, cc, kk, :], rv,
                                        start=(kk == 0), stop=(kk == c.K31 - 1)),
                                        r=[("dg", cc), ("upad", cc)], w=[("ps", cb + h)])
                            if has_ctx and h == 1:
                                a0 = nlat - H
                                for kk in range(c.K31):
                                    S.add("pe", lambda e, cc=cc, kk=kk, a0=a0, cb=cb: e.matmul(
                                        s.ps[cb + 1][:, a0:a0 + c.TC], dg[:, cc, kk, :], upad[:, cc, nrl * PW + kk:nrl * PW + kk + c.TC],
                                        start=(kk == 0), stop=(kk == c.K31 - 1)),
                                        r=[("dg", cc), ("upad", cc)], w=[("ps", cb + 1)])
                        q = cc % 2
                        for h in range(2):
                            S.add("act", lambda e, cc=cc, h=h, cb=cb: e.activation(out=acc[:, cc, h * H:(h + 1) * H], in_=s.ps[cb + h][:, 0:H], func=AF.Identity,
                                                                                   bias=s.sm("c31b", cc)),
                                  r=[("ps", cb + h), ("small",)], w=[("acc", cc)])
                            S.add("act", lambda e, cc=cc, h=h, cb=cb, q=q: e.activation(out=sq[:, q, h * H:(h + 1) * H], in_=s.ps[cb + h][:, 0:H], func=AF.Square,
                                                                                        bias=s.sm("c31b", cc)),
                                  r=[("ps", cb + h), ("small",)], w=[("sq", q)])
                        s.stats_mm(cc, CK, acc[:, cc, :], sq[:, q, :], ("acc", cc), ("sq", q))
                ocnt = [0]

                def emit_out(j):
                    o = ocnt[0] % 2
                    ocnt[0] += 1
                    S.add("act", lambda e, j=j, o=o: e.activation(out=yo[:, o, :], in_=acc[:, j, :], func=AF.Silu,
                                                                  scale=s.sm("clng", j), bias=s.sm("clnb", j)),
                          r=[("acc", j), ("small",)], w=[("yo", o)])
                    S.add(STQ, lambda e, j=j, o=o: e.dma_start(out=s.YC[j * 128:(j + 1) * 128, cols], in_=yo[:, o, :]),
                          r=[("yo", o)], dma=("yo", "st", o))
                s.ln_tail(acc, CK, c.DCONV, "conv", lnt, emit_out, ukeys=lambda j: [("acc", j)])
            for blk_ in range(c.NB):
                body(blk_)
            S.flush()

    def phase_scan(s, l, last):
        from contextlib import ExitStack
        c, S, nc = s.c, s.S, s.nc
        LK, TS, NH, HC, HD = c.LK, c.TS, c.NH, c.HC, c.HD
        with ExitStack() as st:
            T_ = s.mkT(st)
            gw = T_("gw", [128, 2 * 2 * NH * HC * HD], BF16)
            xrh = T_("xrh", [128, 2, LK, TS + 3])
            xc = T_("xc", [128, 2, LK, TS])
            xcb = T_("xcb", [128, 2, LK, TS], BF16)
            Rt = T_("Rt", [128, 2, LK, TS])
            It = T_("It", [128, 2, LK, TS])
            At = T_("At", [128, 2, LK, TS])
            St = T_("St", [128, 2, LK, TS])
            hh = T_("hh", [128, 3, TS])
            rf = T_("rf", [128, LK, TS])
            gg = T_("gg", [128, LK, TS])
            yr = T_("yr", [128, 2, TS], BF16)
            state = T_("state", [128, LK])
            S.add("pool", lambda e: e.dma_start(out=gw[:], in_=s.gwt[l], max_dma_last_dim=8192), w=[("gw",)], dma=("gw",))

            def gwv(d, g, hd, jc, ic):
                base = (((d * 2 + g) * NH + hd) * HC + jc) * HD + ic * 128
                return gw[:, base:base + 128]
            XRv = s.XR.rearrange("(c p) t -> p c t", p=128)
            RECv = s.REC.rearrange("(c p) t -> p c t", p=128)
            GGv = s.GG.rearrange("(c p) t -> p c t", p=128)
            cnt = {"x": 0, "h": 0, "rf": 0, "yr": 0, "cb": 0}
            seq_ctx = (c.T, c.TC)
            seq_lat = (0, c.T)

            def partA(i, d, off, ln, is_ctx, b):
                sl = i % 2
                lo = b * TS
                n = min(TS, ln - lo)
                xl = cnt["x"] % 2
                cnt["x"] += 1
                a0 = max(lo - 2, 0)
                a1 = min(lo + n + 1, ln)
                d0 = a0 - (lo - 2)
                if d0 > 0:
                    S.add("dve", lambda e: e.memset(xrh[:, xl, :, 0:d0], 0.0), w=[("xrh", xl)])
                if a1 < lo + n + 1:
                    S.add("dve", lambda e: e.memset(xrh[:, xl, :, n + 2:n + 3], 0.0), w=[("xrh", xl)])
                S.add("sp", lambda e: e.dma_start(out=xrh[:, xl, :, d0:d0 + (a1 - a0)], in_=XRv[:, :, off + a0:off + a1]),
                      w=[("xrh", xl)], dma=("xrh", xl))
                for ch in range(LK):
                    S.add("act", lambda e, ch=ch: e.activation(out=xc[:, sl, ch, 0:n], in_=xrh[:, xl, ch, 0:n], func=AF.Identity,
                                                               scale=s.sm("c4w", ch * 4 + 0), bias=s.sm("c4b", ch)),
                          r=[("xrh", xl), ("small",)], w=[("xc", sl, ch)])
                    for kk in range(1, 4):
                        S.add("dve", lambda e, ch=ch, kk=kk: e.scalar_tensor_tensor(
                            out=xc[:, sl, ch, 0:n], in0=xrh[:, xl, ch, kk:kk + n], scalar=s.sm("c4w", ch * 4 + kk), in1=xc[:, sl, ch, 0:n],
                            op0=ALU.mult, op1=ALU.add), r=[("xrh", xl), ("small",)], w=[("xc", sl, ch)])
                    S.add("pool", lambda e, ch=ch: e.tensor_copy(xcb[:, sl, ch, 0:n], xc[:, sl, ch, 0:n]),
                          r=[("xc", sl, ch)], w=[("xcb", sl, ch)])

            def partB(i, d, off, ln, is_ctx, b):
                sl = i % 2
                lo = b * TS
                n = min(TS, ln - lo)
                if d == 1 and not (is_ctx and last):
                    S.add("sp", lambda e: e.dma_start(out=rf[:, :, 0:n], in_=RECv[:, :, off + lo:off + lo + n]), w=[("rf", ch) for ch in range(LK)], dma=("rf",))
                    S.add("sp", lambda e: e.dma_start(out=gg[:, :, 0:n], in_=GGv[:, :, off + lo:off + lo + n]), w=[("gg", ch) for ch in range(LK)], dma=("gg",))

                for g, dstt, bname in ((0, Rt, "brg"), (1, It, "big")):
                    for ch in range(LK):
                        hd, jc = ch // HC, ch % HC
                        bank = ch % 8
                        for ic in range(HC):
                            S.add("pe", lambda e, g=g, hd=hd, jc=jc, ic=ic, bank=bank: e.matmul(
                                s.ps[bank][:, 0:n], gwv(d, g, hd, jc, ic), xcb[:, sl, hd * HC + ic, 0:n], start=(ic == 0), stop=(ic == HC - 1)),
                                r=[("gw",), ("xcb", sl, hd * HC + ic)], w=[("ps", bank)])
                        S.add("act", lambda e, ch=ch, bank=bank, dstt=dstt, bname=bname: e.activation(
                            out=dstt[:, sl, ch, 0:n], in_=s.ps[bank][:, 0:n], func=AF.Sigmoid, bias=s.sm(bname, d * LK + ch)),
                            r=[("ps", bank), ("small",)], w=[("g%d" % g, sl, ch)])
                for ch in range(LK):
                    S.add("act", lambda e, ch=ch: e.activation(out=At[:, sl, ch, 0:n], in_=Rt[:, sl, ch, 0:n], func=AF.Exp,
                                                               scale=s.clam[:, d * LK + ch:d * LK + ch + 1]),
                          r=[("g0", sl, ch), ("clam",)], w=[("A", sl, ch)])
                    S.add("act", lambda e, ch=ch: e.activation(out=St[:, sl, ch, 0:n], in_=Rt[:, sl, ch, 0:n], func=AF.Exp,
                                                               scale=s.clam2[:, d * LK + ch:d * LK + ch + 1]),
                          r=[("g0", sl, ch), ("clam",)], w=[("S", sl, ch)])
                for ch in range(LK):
                    S.add("act", lambda e, ch=ch: e.activation(out=St[:, sl, ch, 0:n], in_=St[:, sl, ch, 0:n], func=AF.Sqrt, scale=-1.0,
                                                               bias=s.ones[:, 0:1]),
                          r=[("S", sl, ch), ("ones",)], w=[("S", sl, ch)])
                for ch in range(LK):
                    S.add("pool", lambda e, ch=ch: e.tensor_tensor(out=St[:, sl, ch, 0:n], in0=St[:, sl, ch, 0:n], in1=It[:, sl, ch, 0:n], op=ALU.mult),
                          r=[("S", sl, ch), ("g1", sl, ch)], w=[("S", sl, ch)])
                    S.add("pool", lambda e, ch=ch: e.tensor_tensor(out=St[:, sl, ch, 0:n], in0=St[:, sl, ch, 0:n], in1=xc[:, sl, ch, 0:n], op=ALU.mult),
                          r=[("S", sl, ch), ("xc", sl, ch)], w=[("S", sl, ch)])
                    hl = cnt["h"] % 3
                    cnt["h"] += 1
                    if d == 0:
                        S.add("dve", lambda e, ch=ch, hl=hl: e.tensor_tensor_scan(hh[:, hl, 0:n], At[:, sl, ch, 0:n], St[:, sl, ch, 0:n],
                                                                                 state[:, ch:ch + 1], ALU.mult, ALU.add),
                              r=[("A", sl, ch), ("S", sl, ch), ("state", ch)], w=[("hh", hl)])
                        S.add("dve", lambda e, ch=ch, hl=hl: e.tensor_copy(state[:, ch:ch + 1], hh[:, hl, n - 1:n]),
                              r=[("hh", hl)], w=[("state", ch)])
                        if not (is_ctx and last):
                            S.add("sp", lambda e, ch=ch, hl=hl: e.dma_start(
                                out=s.REC[ch * 128:(ch + 1) * 128, off + lo:off + lo + n], in_=hh[:, hl, 0:n]),
                                r=[("hh", hl)], dma=("hh", hl))
                    else:
                        S.add("dve", lambda e, ch=ch, hl=hl: e.tensor_tensor_scan(hh[:, hl, 0:n][:, ::-1], At[:, sl, ch, 0:n][:, ::-1],
                                                                                 St[:, sl, ch, 0:n][:, ::-1],
                                                                                 state[:, ch:ch + 1], ALU.mult, ALU.add),
                              r=[("A", sl, ch), ("S", sl, ch), ("state", ch)], w=[("hh", hl)])
                        S.add("dve", lambda e, ch=ch, hl=hl: e.tensor_copy(state[:, ch:ch + 1], hh[:, hl, 0:1]),
                              r=[("hh", hl)], w=[("state", ch)])
                        if not (is_ctx and last):
                            rl = cnt["rf"] % 2
                            cnt["rf"] += 1
                            S.add("pool", lambda e, ch=ch, hl=hl: e.tensor_tensor(out=hh[:, hl, 0:n], in0=hh[:, hl, 0:n], in1=rf[:, ch, 0:n], op=ALU.add),
                                  r=[("hh", hl), ("rf", ch)], w=[("hh", hl)])
                            S.add("dve", lambda e, ch=ch, hl=hl, rl=rl: e.tensor_tensor(out=yr[:, rl, 0:n], in0=hh[:, hl, 0:n], in1=gg[:, ch, 0:n], op=ALU.mult),
                                  r=[("hh", hl), ("gg", ch)], w=[("yr", rl)])
                            S.add("sp", lambda e, ch=ch, rl=rl: e.dma_start(
                                out=s.YR[ch * 128:(ch + 1) * 128, off + lo:off + lo + n], in_=yr[:, rl, 0:n]),
                                r=[("yr", rl)], dma=("yr", rl))

            for d in range(2):
                S.add("dve", lambda e: e.memset(state[:], 0.0), w=[("state", ch) for ch in range(LK)])
                items = []
                for (off, ln), is_ctx in ((seq_ctx, True), (seq_lat, False)):
                    nblk = (ln + TS - 1) // TS
                    blks = list(range(nblk))
                    if d == 1:
                        blks = blks[::-1]
                    for b in blks:
                        items.append((d, off, ln, is_ctx, b))
                partA(0, *items[0])
                for i, it in enumerate(items):
                    if i + 1 < len(items):
                        partA(i + 1, *items[i + 1])
                    partB(i, *it)
                S.flush()

    def phase_mixc(s, l, src, dst):
        from contextlib import ExitStack
        c, S, nc = s.c, s.S, s.nc
        H, TB, KD, MK, LK, CK = c.HALF, c.TB, c.KD, c.MK, c.LK, c.CK
        with ExitStack() as st:
            T_ = s.mkT(st)
            ym = T_("ym", [128, 2, MK, TB], BF16)
            u32 = T_("u32", [128, 2, KD, TB])
            wt = T_("woutt", [128, 3, c.DMIX], BF16)
            xs = T_("xs", [128, 3, TB])
            sq = T_("sq", [128, 2, TB])
            osl = T_("osl", [128, 2, TB])
            lr = T_("lr", [128, 2, TB])
            lnn = T_("lnn", [128, 2, TB])
            lm = T_("lm", [128, TB])
            lt = T_("lt", [128, TB])
            s.eps_tiles(st)
            YRv = s.YR.rearrange("(c p) t -> p c t", p=128)
            YCv = s.YC.rearrange("(c p) t -> p c t", p=128)
            cn = {"x": 0, "w": 0, "o": 0}

            def load_ym(blk):
                par = blk % 2
                cols = slice(blk * TB, (blk + 1) * TB)
                S.add("sp", lambda e: e.dma_start(out=ym[:, par, 0:LK, :], in_=YRv[:, :, cols]), w=[("ym", par, 0)], dma=("ym", par, 0))
                S.add("sp", lambda e: e.dma_start(out=ym[:, par, LK:MK, :], in_=YCv[:, :, cols]), w=[("ym", par, 1)], dma=("ym", par, 1))

            def tail_chunk(blk, j):
                par = blk % 2
                cols = slice(blk * TB, (blk + 1) * TB)
                o = cn["o"] % 2
                cn["o"] += 1
                S.add("dve", lambda e: e.tensor_tensor(out=u32[:, par, j, :], in0=u32[:, par, j, :], in1=lr[:, par, :], op=ALU.mult),
                      r=[("u", par, j), ("lr", par)], w=[("u", par, j)])
                S.add("dve", lambda e: e.tensor_tensor(out=u32[:, par, j, :], in0=u32[:, par, j, :], in1=lnn[:, par, :], op=ALU.add),
                      r=[("u", par, j), ("lnn", par)], w=[("u", par, j)])
                S.add("act", lambda e: e.activation(out=osl[:, o, :], in_=u32[:, par, j, :], func=AF.Identity,
                                                    scale=s.sm("lng", 1 * KD + j), bias=s.sm("lnb", 1 * KD + j)),
                      r=[("u", par, j), ("small",)], w=[("os", o)])
                S.add(STQ, lambda e: e.dma_start(out=dst[j * 128:(j + 1) * 128, cols], in_=osl[:, o, :]),
                      r=[("os", o)], dma=("os", "st", o))

            def body(blk):
                par = blk % 2
                segs = s.segs(blk)
                cols = slice(blk * TB, (blk + 1) * TB)
                if blk + 1 < c.NB:
                    load_ym(blk + 1)
                pending = None
                for j in range(KD):
                    sl = cn["w"] % 3
                    cn["w"] += 1
                    S.add("pool", lambda e, j=j, sl=sl: e.dma_start(out=wt[:, sl, :], in_=s.wout[l, j], max_dma_last_dim=8192),
                          w=[("wo", sl)], dma=("wo", sl))
                    xl = cn["x"] % 3
                    cn["x"] += 1
                    S.add("sp", lambda e, j=j, xl=xl: e.dma_start(out=xs[:, xl, :], in_=src[j * 128:(j + 1) * 128, cols]),
                          w=[("xs", xl)], dma=("xs", xl))
                    bs = 2 * (j % 2)
                    for kc in range(MK):
                        for h in range(2):
                            S.add("pe", lambda e, sl=sl, kc=kc, h=h, bs=bs: e.matmul(
                                s.ps[bs + h][:, 0:H], wt[:, sl, kc * 128:(kc + 1) * 128], ym[:, par, kc, h * H:(h + 1) * H],
                                start=(kc == 0), stop=(kc == MK - 1)),
                                r=[("wo", sl), ("ym", par, 0 if kc < LK else 1)], w=[("ps", bs + h)])
                    if pending is not None:
                        pending()
                    for (lo, hi, which) in segs:
                        S.add("act", lambda e, j=j, xl=xl, lo=lo, hi=hi, which=which: e.activation(
                            out=xs[:, xl, lo:hi], in_=xs[:, xl, lo:hi], func=AF.Identity, bias=s.bgv[:, j, which:which + 1]),
                            r=[("xs", xl), ("bgv",)], w=[("xs", xl)])
                        gcol = s.modv[:, 5 * KD + j, which:which + 1]
                        for h in range(2):
                            a0, a1 = max(lo, h * H), min(hi, (h + 1) * H)
                            if a0 >= a1:
                                continue
                            S.add("dve", lambda e, j=j, h=h, a0=a0, a1=a1, gcol=gcol, xl=xl, bs=bs: e.scalar_tensor_tensor(
                                out=u32[:, par, j, a0:a1], in0=s.ps[bs + h][:, a0 - h * H:a1 - h * H], scalar=gcol, in1=xs[:, xl, a0:a1],
                                op0=ALU.mult, op1=ALU.add),
                                r=[("ps", bs + h), ("xs", xl), ("modv",)], w=[("u", par, j)])
                    q = j % 2
                    S.add("act", lambda e, j=j, q=q: e.activation(out=sq[:, q, :], in_=u32[:, par, j, :], func=AF.Square),
                          r=[("u", par, j)], w=[("sq", q)])

                    def mk(j=j, q=q):
                        return lambda: s.stats_mm(j, KD, u32[:, par, j, :], sq[:, q, :], ("u", par, j), ("sq", q))
                    pending = mk()
                    if blk > 0:
                        tail_chunk(blk - 1, j)
                pending()
                lnt = [(lm, ("ln", "mean")), (lr[:, par, :], ("lr", par)), (lnn[:, par, :], ("lnn", par)), (lt, ("ln", "tmp"))]
                s.ln_tail(None, 0, c.D, "ffn", lnt, None)

            load_ym(0)
            for blk_ in range(c.NB):
                body(blk_)
            for j in range(KD):
                tail_chunk(c.NB - 1, j)
            S.flush()


def tile_w(W):
    K, N = W.shape
    return np.ascontiguousarray(W.reshape(K // 128, 128, N // 128, 128).transpose(2, 1, 0, 3).reshape(N // 128, 128, K))


def fm(v):
    sh = v.shape
    return np.moveaxis(v.reshape(sh[:-1] + (sh[-1] // 128, 128)), -1, 0)


def prep_inputs(cfg, inp):
    c = cfg
    L = c.DEPTH
    f32 = lambda a: np.ascontiguousarray(np.asarray(a, dtype=np.float32))
    small = np.zeros((L, 128, c.NS), np.float32)
    so = c.so
    for l in range(L):
        def put(name, arr):
            arr = np.asarray(arr, np.float32).reshape(128, -1)
            small[l, :, so[name]:so[name] + arr.shape[1]] = arr
        put("bada", fm(inp["b_ada"][l]))
        put("lng", fm(inp["ln_g"][l]))
        put("lnb", fm(inp["ln_b"][l]))
        put("c4w", np.moveaxis(fm(inp["conv4_w"][l]), 1, 2))
        put("c4b", fm(inp["conv4_b"][l]))
        put("brg", fm(inp["b_rg"][l]))
        put("big", fm(inp["b_ig"][l]))
        put("lam", fm(inp["lam"][l]))
        put("c31w", np.moveaxis(fm(inp["conv31_w"][l]), 1, 2))
        put("c31b", fm(inp["conv31_b"][l]))
        put("clng", fm(inp["cln_g"][l]))
        put("clnb", fm(inp["cln_b"][l]))
        put("bout", fm(inp["b_out"][l]))
    wada = np.stack([tile_w(f32(inp["w_ada"][l])) for l in range(L)])
    w1 = np.stack([np.stack([tile_w(f32(inp[k][l])) for k in ("ff1_in", "ff2_in")]) for l in range(L)])
    w2 = np.stack([np.stack([tile_w(f32(inp[k][l])) for k in ("ff1_out", "ff2_out")]) for l in range(L)])
    win = np.stack([tile_w(f32(inp["w_in"][l])) for l in range(L)])
    wout = np.stack([tile_w(f32(inp["w_out"][l])) for l in range(L)])
    gw = np.zeros((L, 128, 2, 2, c.NH, c.HC, c.HD), np.float32)
    for l in range(L):
        for d in range(2):
            for g, nm in enumerate(("w_rg", "w_ig")):
                for h in range(c.NH):
                    tw = tile_w(f32(inp[nm][l][d][h]))
                    gw[l, :, d, g, h] = tw.transpose(1, 0, 2)
    gw = gw.reshape(L, 128, -1)
    shared = dict(small=small, wada=wada, w1=w1, w2=w2, win=win, wout=wout, gwt=np.ascontiguousarray(gw))
    maps = []
    for b in range(c.NCORES):
        xin = np.concatenate([f32(inp["x"][b]).T, f32(inp["ctx"][b]).T], axis=1)
        cv = np.stack([fm(f32(inp["c"][b])), fm(f32(inp["c_ctx"]))], axis=-1).reshape(128, -1)
        m = dict(shared)
        m["xin"] = np.ascontiguousarray(xin)
        m["cvec"] = np.ascontiguousarray(cv)
        m["ident"] = np.eye(128, dtype=np.float32)
        maps.append(m)
    return maps


_NC_CACHE = {}


def run_cfg(cfg, inp):
    key = (cfg.D, cfg.FF, cfg.T, cfg.DLRU, cfg.DCONV, getattr(cfg, "STOP", None))
    if key not in _NC_CACHE:
        _NC_CACHE[key] = Builder(cfg).build()
    nc = _NC_CACHE[key]
    maps = prep_inputs(cfg, inp)
    res = run_bass_kernel_spmd(nc, maps, core_ids=list(range(cfg.NCORES)))
    if getattr(cfg, "STOP", None) is not None:
        return np.stack([np.ascontiguousarray(res.results[b]["yout"].T) for b in range(cfg.NCORES)])
    out = np.stack([np.ascontiguousarray(res.results[b]["yout"][:, :cfg.T].T) for b in range(cfg.NCORES)])
    return out.astype(np.float32)


def kernel(**inputs):
    cfg = Cfg()
    return run_cfg(cfg, inputs)
```

```python
import numpy as np
import concourse.bass as bass
import concourse.mybir as mybir
from concourse.bass_utils import run_bass_kernel_spmd

F32 = mybir.dt.float32
BF16 = mybir.dt.bfloat16
AF = mybir.ActivationFunctionType
ALU = mybir.AluOpType

RG_C = 8.0
EPS = 1e-6
SAME_SYNC = True
STQ = "act"
ENGS = ("pe", "act", "dve", "pool", "sp")


class Cfg:
    def __init__(s, D=2048, FF=5632, DLRU=1024, NH=4, DCONV=1024, T=8192, TC=256, DEPTH=2,
                 TB=768, TS=512, NCORES=2):
        s.D, s.FF, s.DLRU, s.NH, s.DCONV, s.T, s.TC, s.DEPTH = D, FF, DLRU, NH, DCONV, T, TC, DEPTH
        s.TB, s.TS, s.NCORES = TB, TS, NCORES
        s.HALF = TB // 2
        s.GW, s.K31 = 64, 31
        s.KD, s.FK, s.LK, s.CK = D // 128, FF // 128, DLRU // 128, DCONV // 128
        s.DMIX = DLRU + DCONV
        s.MK = s.DMIX // 128
        s.DIN = 2 * DLRU + 2 * DCONV
        s.NT = T + TC
        s.NB = s.NT // TB
        s.HD = DLRU // NH
        s.HC = s.HD // 128
        s.ALPHA = (2 * DEPTH) ** 0.25
        assert s.NT % TB == 0 and TB % 128 == 0 and s.HALF % 64 == 0 and TC <= s.HALF
        assert TC <= TS and (T % TB) in (0, TB - TC)
        o = {}
        off = 0
        for name, n in (("bada", 9 * s.KD), ("lng", 3 * s.KD), ("lnb", 3 * s.KD), ("c4w", s.LK * 4),
                        ("c4b", s.LK), ("brg", 2 * s.LK), ("big", 2 * s.LK), ("lam", 2 * s.LK),
                        ("c31w", s.CK * 31), ("c31b", s.CK), ("clng", s.CK), ("clnb", s.CK),
                        ("bout", s.KD)):
            o[name] = off
            off += n
        s.so, s.NS = o, off


class Op:
    __slots__ = ("eng", "fn", "deps", "sig", "dkey", "waits", "needed")

    def __init__(s, eng, fn, dkey):
        s.eng, s.fn, s.dkey = eng, fn, dkey
        s.deps, s.sig, s.waits, s.needed = [], None, [], False


class Sched:
    def __init__(s, nc, stack):
        s.nc = nc
        s.stack = stack
        s.ops = {e: [] for e in ENGS}
        s.lastw, s.readers = {}, {}
        s.engsem = {e: stack.enter_context(nc.semaphore("es_" + e)) for e in ENGS if e != "sp"}
        s.engcnt = {e: 0 for e in s.engsem}
        s.dsem = {}
        s.seen = {e: {} for e in ENGS}
        s.first_phase = True
        s.nops = 0

    def add(s, eng, fn, r=(), w=(), dma=None):
        op = Op(eng, fn, dma)
        deps = []
        for k in r:
            lw = s.lastw.get(k)
            if lw is not None:
                deps.append(lw)
        for k in w:
            lw = s.lastw.get(k)
            if lw is not None:
                deps.append(lw)
            deps.extend(s.readers.get(k, ()))
        for k in r:
            s.readers.setdefault(k, []).append(op)
        for k in w:
            s.lastw[k] = op
            s.readers[k] = []
        seen = set()
        for d in deps:
            if d is op or id(d) in seen:
                continue
            seen.add(id(d))
            if d.dkey is None and d.eng == eng and (eng == "pe" or (not SAME_SYNC and eng != "pool")):
                continue
            op.deps.append(d)
            d.needed = True
        s.ops[eng].append(op)
        s.nops += 1
        return op

    def _dsem(s, key):
        if key not in s.dsem:
            s.dsem[key] = [s.stack.enter_context(s.nc.semaphore("ds%d" % len(s.dsem))), 0]
        return s.dsem[key]

    def flush(s, final=False):
        nc = s.nc
        pre = {e: [] for e in ENGS}
        if not s.first_phase:
            for e in ENGS:
                for e2, sem in s.engsem.items():
                    if e2 != e and s.engcnt[e2] > 0:
                        pre[e].append((sem, s.engcnt[e2]))
                for key, (sem, cnt) in s.dsem.items():
                    if cnt > 0:
                        pre[e].append((sem, cnt))
        s.first_phase = False
        for e in ENGS:
            lst = s.ops[e]
            lastc = max([i for i, op in enumerate(lst) if op.dkey is None], default=-1)
            for i, op in enumerate(lst):
                last = i == lastc
                if op.dkey is not None:
                    d = s._dsem(op.dkey)
                    d[1] += 16
                    op.sig = (d[0], d[1], 16)
                elif op.needed or last:
                    s.engcnt[e] += 1
                    op.sig = (s.engsem[e], s.engcnt[e], 1)
        for e in ENGS:
            seen = s.seen[e]
            for sem, v in pre[e]:
                seen[id(sem)] = max(seen.get(id(sem), 0), v)
            for op in s.ops[e]:
                for d in op.deps:
                    sem, v, _ = d.sig
                    if seen.get(id(sem), 0) >= v:
                        continue
                    seen[id(sem)] = v
                    op.waits.append((sem, v))
        post = []
        if final:
            post = [(sem, cnt) for (sem, cnt) in s.dsem.values() if cnt > 0]
            post += [(sem, s.engcnt[e2]) for e2, sem in s.engsem.items() if s.engcnt[e2] > 0]
        ops = s.ops

        def replay(eng_name):
            def run(e):
                for sem, v in pre[eng_name]:
                    e.wait_ge(sem, v)
                for op in ops[eng_name]:
                    for sem, v in op.waits:
                        e.wait_ge(sem, v)
                    ins = op.fn(e)
                    if op.sig is not None:
                        ins.then_inc(op.sig[0], op.sig[2])
                if eng_name == "sp":
                    for sem, v in post:
                        e.wait_ge(sem, v)
            return run

        with nc.Block() as block:
            block.tensor(replay("pe"))
            block.scalar(replay("act"))
            block.vector(replay("dve"))
            block.gpsimd(replay("pool"))
            block.sync(replay("sp"))
        s.ops = {e: [] for e in ENGS}
        s.lastw, s.readers = {}, {}


class Builder:
    def __init__(s, cfg):
        s.c = cfg

    def build(s):
        from contextlib import ExitStack
        c = s.c
        nc = bass.Bass("TRN2", target_bir_lowering=False)
        s.nc = nc
        L = c.DEPTH
        dt = nc.dram_tensor
        s.xin = dt("xin", [c.D, c.NT], F32, kind="ExternalInput").ap()
        s.cvec = dt("cvec", [128, c.KD * 2], F32, kind="ExternalInput").ap()
        s.small = dt("small", [L, 128, c.NS], F32, kind="ExternalInput").ap()
        s.wada = dt("wada", [L, 9 * c.KD, 128, c.D], F32, kind="ExternalInput").ap()
        s.w1 = dt("w1", [L, 2, 2 * c.FK, 128, c.D], F32, kind="ExternalInput").ap()
        s.w2 = dt("w2", [L, 2, c.KD, 128, c.FF], F32, kind="ExternalInput").ap()
        s.win = dt("win", [L, c.DIN // 128, 128, c.D], F32, kind="ExternalInput").ap()
        s.wout = dt("wout", [L, c.KD, 128, c.DMIX], F32, kind="ExternalInput").ap()
        s.gwt = dt("gwt", [L, 128, 2 * 2 * c.NH * c.HC * c.HD], F32, kind="ExternalInput").ap()
        s.identd = dt("ident", [128, 128], F32, kind="ExternalInput").ap()
        s.yout = dt("yout", [c.D, c.NT], F32, kind="ExternalOutput").ap()
        s.SA = dt("SA", [c.D, c.NT], F32, kind="Internal").ap()
        s.SB = dt("SB", [c.D, c.NT], F32, kind="Internal").ap()
        s.SC = dt("SC", [c.D, c.NT], F32, kind="Internal").ap()
        s.Uscr = dt("Uscr", [2, c.D, c.TB], F32, kind="Internal").ap()
        s.XR = dt("XR", [c.DLRU, c.NT], F32, kind="Internal").ap()
        s.GG = dt("GG", [c.DLRU, c.NT], F32, kind="Internal").ap()
        s.REC = dt("REC", [c.DLRU, c.NT], F32, kind="Internal").ap()
        s.YR = dt("YR", [c.DLRU, c.NT], BF16, kind="Internal").ap()
        s.YC = dt("YC", [c.DCONV, c.NT], BF16, kind="Internal").ap()

        with ExitStack() as st:
            s.st = st
            s.S = Sched(nc, st)
            s.ps = [st.enter_context(nc.psum_tensor("ps%d" % i, [128, 512], F32)) for i in range(8)]
            T_ = s.mkT(st)
            s.ones = T_("ones", [128, 128])
            s.cv32 = T_("cv32", [128, c.KD * 2])
            s.scb = T_("scb", [128, c.KD, 2], BF16)
            s.smallt = T_("smallt", [128, c.NS])
            s.modv = T_("modv", [128, 9 * c.KD, 2])
            s.bgv = T_("bgv", [128, c.KD, 2])
            s.clam = T_("clam", [128, 2 * c.LK])
            s.clam2 = T_("clam2", [128, 2 * c.LK])
            s.ctmp = [T_("ctmp%d" % i, [128, 2 * c.LK]) for i in range(4)]
            S = s.S
            s.ident = T_("ident", [128, 128])
            S.add("sp", lambda e: e.dma_start(out=s.ident[:], in_=s.identd), w=[("ident",)], dma=("ident",))
            S.add("dve", lambda e: e.memset(s.ones[:], 1.0), w=[("ones",)])
            S.add("sp", lambda e: e.dma_start(out=s.cv32[:], in_=s.cvec), w=[("cv32",)], dma=("cv32",))
            S.add("act", lambda e: e.activation(out=s.scb[:].rearrange("p k c -> p (k c)"), in_=s.cv32[:], func=AF.Silu),
                  r=[("cv32",)], w=[("scb",)])
            stop = getattr(c, "STOP", None)
            done = False
            for l in range(L):
                last = l == L - 1
                src_ = s.xin if l == 0 else s.SC
                steps = [("ada", lambda: s.phase_ada(l), None),
                         ("ffn0", lambda: s.phase_ffn(l, 0, src_, s.SA, final=(stop == (l, "ffn0"))), s.SA),
                         ("mixa", lambda: s.phase_mixa(l, s.SA), None),
                         ("scan", lambda: s.phase_scan(l, last), None),
                         ("mixc", lambda: s.phase_mixc(l, s.SA, s.SB), s.SB),
                         ("ffn1", lambda: s.phase_ffn(l, 1, s.SB, s.yout if last else s.SC, final=last), s.SC)]
                for name, fn, outstream in steps:
                    fn()
                    if stop == (l, name):
                        s.dbg_copy(outstream)
                        done = True
                        break
                if done:
                    break
        return nc

    def dbg_copy(s, stream):
        S = s.S
        c = s.c
        for j in range(c.KD):
            S.add("sp", lambda e, j=j: e.dma_start(out=s.yout[j * 128:(j + 1) * 128, :], in_=stream[j * 128:(j + 1) * 128, :]),
                  dma=("dbg", j))
        S.flush(final=True)

    def mkT(s, st):
        def T_(name, shape, dtp=F32):
            s.uid = getattr(s, "uid", 0) + 1
            return st.enter_context(s.nc.sbuf_tensor("%s_%d" % (name, s.uid), shape, dtp))
        return T_

    def sm(s, name, idx, n=1):
        o = s.c.so[name] + idx
        return s.smallt[:, o:o + n]

    def segs(s, blk):
        c = s.c
        lo, hi = blk * c.TB, (blk + 1) * c.TB
        out = []
        if lo < c.T:
            e_ = min(hi, c.T) - lo
            if getattr(c, "SPLIT", False) and e_ == c.TB:
                out.append((0, 512, 0))
                out.append((512, e_, 0))
            else:
                out.append((0, e_, 0))
        if hi > c.T:
            out.append((max(lo, c.T) - lo, c.TB, 1))
        return out

    def phase_ada(s, l):
        c, S, nc = s.c, s.S, s.nc
        NJ = 9 * c.KD
        with nc.sbuf_tensor("wa_%d" % l, [128, 6, c.D], BF16) as wa:
            S.add("sp", lambda e: e.dma_start(out=s.smallt[:], in_=s.small[l]), w=[("small",)], dma=("small",))
            for j in range(NJ):
                sl = j % 6
                S.add("pool", lambda e, j=j, sl=sl: e.dma_start(out=wa[:, sl, :], in_=s.wada[l, j], max_dma_last_dim=8192),
                      w=[("wa", sl)], dma=("wa", sl))
                for k in range(c.KD):
                    S.add("pe", lambda e, j=j, k=k, sl=sl: e.matmul(s.ps[0][:, 2 * j:2 * j + 2], wa[:, sl, k * 128:(k + 1) * 128],
                                                                  s.scb[:, k, :], start=(k == 0), stop=(k == c.KD - 1)),
                          r=[("wa", sl), ("scb",)], w=[("ps", 0)])
            psv = s.ps[0][:, 0:2 * NJ].rearrange("p (j c) -> p j c", c=2)
            for col in range(2):
                S.add("dve", lambda e, col=col: e.tensor_tensor(out=s.modv[:, :, col], in0=psv[:, :, col],
                                                                in1=s.sm("bada", 0, NJ), op=ALU.add),
                      r=[("ps", 0), ("small",)], w=[("modv",)])
            for sub in range(3):
                coef = (1.0 if sub == 1 else 0.5) / c.ALPHA
                S.add("dve", lambda e, sub=sub: e.tensor_scalar(s.modv[:, (3 * sub + 1) * c.KD:(3 * sub + 2) * c.KD, :],
                                                                s.modv[:, (3 * sub + 1) * c.KD:(3 * sub + 2) * c.KD, :],
                                                                1.0, None, ALU.add), r=[("modv",)], w=[("modv",)])
                S.add("dve", lambda e, sub=sub, coef=coef: e.tensor_scalar(s.modv[:, (3 * sub + 2) * c.KD:(3 * sub + 3) * c.KD, :],
                                                                           s.modv[:, (3 * sub + 2) * c.KD:(3 * sub + 3) * c.KD, :],
                                                                           coef, None, ALU.mult), r=[("modv",)], w=[("modv",)])
            for col in range(2):
                S.add("dve", lambda e, col=col: e.tensor_tensor(out=s.bgv[:, :, col], in0=s.modv[:, 5 * c.KD:6 * c.KD, col],
                                                                in1=s.sm("bout", 0, c.KD), op=ALU.mult),
                      r=[("modv",), ("small",)], w=[("bgv",)])
            t0, t1, t2, t3 = s.ctmp
            lam = s.sm("lam", 0, 2 * c.LK)
            K = ("clamk",)
            S.add("act", lambda e: e.activation(out=t0[:], in_=lam, func=AF.Exp, scale=-1.0), r=[("small",)], w=[K])
            S.add("act", lambda e: e.activation(out=t1[:], in_=t0[:], func=AF.Ln, bias=1.0), r=[K], w=[K])
            S.add("dve", lambda e: e.tensor_scalar(t2[:], t0[:], 0.05, None, ALU.min), r=[K], w=[K])
            S.add("dve", lambda e: e.tensor_scalar(t3[:], t2[:], 0.2, -0.25, ALU.mult, ALU.add), r=[K], w=[K])
            for cst in (1.0 / 3.0, -0.5, 1.0):
                S.add("dve", lambda e: e.tensor_tensor(out=t3[:], in0=t3[:], in1=t2[:], op=ALU.mult), r=[K], w=[K])
                S.add("dve", lambda e, cst=cst: e.tensor_scalar(t3[:], t3[:], cst, None, ALU.add), r=[K], w=[K])
            S.add("dve", lambda e: e.tensor_tensor(out=t3[:], in0=t3[:], in1=t2[:], op=ALU.mult), r=[K], w=[K])
            S.add("dve", lambda e: e.tensor_scalar(t2[:], t0[:], 0.05, None, ALU.is_lt), r=[K], w=[K])
            S.add("dve", lambda e: e.tensor_tensor(out=t3[:], in0=t3[:], in1=t1[:], op=ALU.subtract), r=[K], w=[K])
            S.add("dve", lambda e: e.tensor_tensor(out=t3[:], in0=t3[:], in1=t2[:], op=ALU.mult), r=[K], w=[K])
            S.add("dve", lambda e: e.tensor_tensor(out=t3[:], in0=t3[:], in1=t1[:], op=ALU.add), r=[K], w=[K])
            S.add("dve", lambda e: e.tensor_scalar(s.clam[:], t3[:], -RG_C, None, ALU.mult), r=[K], w=[("clam",)])
            S.add("dve", lambda e: e.tensor_scalar(s.clam2[:], t3[:], -2.0 * RG_C, None, ALU.mult), r=[K], w=[("clam",)])
            S.flush()

    def modulate(s, src, blk, sub, xmod, xs, nslot, cnt):
        c, S = s.c, s.S
        for k in range(c.KD):
            sl = cnt[0] % nslot
            cnt[0] += 1
            S.add("sp", lambda e, k=k, sl=sl: e.dma_start(out=xs[:, sl, :], in_=src[k * 128:(k + 1) * 128, blk * c.TB:(blk + 1) * c.TB]),
                  w=[("xs", sl)], dma=("xs", sl))
            for (lo, hi, which) in s.segs(blk):
                sc1 = s.modv[:, (3 * sub + 1) * c.KD + k, which:which + 1]
                sh = s.modv[:, (3 * sub) * c.KD + k, which:which + 1]
                if k % 2 == 0:
                    S.add("act", lambda e, k=k, sl=sl, lo=lo, hi=hi, sc1=sc1, sh=sh: e.activation(
                        out=xmod[:, k, lo:hi], in_=xs[:, sl, lo:hi], func=AF.Identity, scale=sc1, bias=sh),
                        r=[("xs", sl), ("modv",)], w=[("R1", k)])
                else:
                    S.add("dve", lambda e, k=k, sl=sl, lo=lo, hi=hi, sc1=sc1, sh=sh: e.tensor_scalar(
                        xmod[:, k, lo:hi], xs[:, sl, lo:hi], sc1, sh, ALU.mult, ALU.add),
                        r=[("xs", sl), ("modv",)], w=[("R1", k)])

    def ln_tail(s, u32, nch, Dn, eps, lnt, emit_out, ukeys=None):
        c, S = s.c, s.S
        (mean, kmean), (rstd, krstd), (nmr, knmr), (tmp, ktmp) = lnt
        if ukeys is None:
            ukeys = lambda j: [("u", j)]
        H = c.HALF
        inv = 1.0 / Dn
        for h in range(2):
            cs = slice(h * H, (h + 1) * H)
            S.add("act", lambda e, h=h, cs=cs: e.activation(out=mean[:, cs], in_=s.ps[4 + h][:, 0:H], func=AF.Identity, scale=inv),
                  r=[("ps", 4 + h)], w=[kmean])
            S.add("dve", lambda e, cs=cs: e.tensor_tensor(out=tmp[:, cs], in0=mean[:, cs], in1=mean[:, cs], op=ALU.mult),
                  r=[kmean], w=[ktmp])
            S.add("dve", lambda e, h=h, cs=cs: e.scalar_tensor_tensor(out=tmp[:, cs], in0=s.ps[6 + h][:, 0:H], scalar=inv, in1=tmp[:, cs],
                                                                      op0=ALU.mult, op1=ALU.subtract),
                  r=[("ps", 6 + h), ktmp], w=[ktmp])
            S.add("act", lambda e, cs=cs: e.activation(out=tmp[:, cs], in_=tmp[:, cs], func=AF.Sqrt, bias=s.epst[eps][:, 0:1]),
                  r=[ktmp, ("eps", eps)], w=[ktmp])
            S.add("dve", lambda e, cs=cs: e.reciprocal(rstd[:, cs], tmp[:, cs]), r=[ktmp], w=[krstd])
            S.add("dve", lambda e, cs=cs: e.scalar_tensor_tensor(out=nmr[:, cs], in0=mean[:, cs], scalar=-1.0, in1=rstd[:, cs],
                                                                 op0=ALU.mult, op1=ALU.mult),
                  r=[kmean, krstd], w=[knmr])
        for j in range(nch):
            S.add("dve", lambda e, j=j: e.tensor_tensor(out=u32[:, j, :], in0=u32[:, j, :], in1=rstd[:, :], op=ALU.mult),
                  r=ukeys(j) + [krstd], w=ukeys(j))
            S.add("dve", lambda e, j=j: e.tensor_tensor(out=u32[:, j, :], in0=u32[:, j, :], in1=nmr[:, :], op=ALU.add),
                  r=ukeys(j) + [knmr], w=ukeys(j))
            emit_out(j)

    def stats_mm(s, j, nch, src_u, src_sq, ukey, sqkey):
        c, S = s.c, s.S
        H = c.HALF
        for h in range(2):
            S.add("pe", lambda e, h=h: e.matmul(s.ps[4 + h][:, 0:H], s.ones[:], src_u[:, h * H:(h + 1) * H], start=(j == 0), stop=(j == nch - 1)),
                  r=[ukey, ("ones",)], w=[("ps", 4 + h)])
            S.add("pe", lambda e, h=h: e.matmul(s.ps[6 + h][:, 0:H], s.ones[:], src_sq[:, h * H:(h + 1) * H], start=(j == 0), stop=(j == nch - 1)),
                  r=[sqkey, ("ones",)], w=[("ps", 6 + h)])

    def u_keys(s, j):
        return [("R1", 2 * j), ("R1", 2 * j + 1), ("u", j)]

    def eps_tiles(s, st):
        c = s.c
        s.epst = {}
        for name, val in (("ffn", EPS / (c.ALPHA ** 2)), ("conv", EPS)):
            t = s.mkT(st)("eps_" + name, [128, 1], F32)
            s.S.add("dve", lambda e, t=t, val=val: e.memset(t[:], val), w=[("eps", name)])
            s.epst[name] = t

    def phase_ffn(s, l, which_ffn, src, dst, final=False):
        from contextlib import ExitStack
        c, S, nc = s.c, s.S, s.nc
        sub = 0 if which_ffn == 0 else 2
        H, TB, KD, FK = c.HALF, c.TB, c.KD, c.FK
        NW1 = 3
        U = s.Uscr
        with ExitStack() as st:
            T_ = s.mkT(st)
            xm = T_("xm", [128, 2, KD, TB], BF16)
            act = T_("act", [128, FK, TB], BF16)
            w1t = T_("w1t", [128, NW1, 2, c.D], BF16)
            w2t = T_("w2t", [128, 2, c.FF], BF16)
            xs = T_("xs", [128, 2, TB])
            uo = T_("uo", [128, 2, TB])
            sq = T_("sq", [128, 2, TB])
            nin = T_("nin", [128, 2, TB])
            sgt = T_("sgt", [128, 2, H])
            lr = T_("lr", [128, 2, TB])
            lnn = T_("lnn", [128, 2, TB])
            s.eps_tiles(st)
            cn = {"x": 0, "w1": 0, "w2": 0, "sg": 0, "n": 0}

            def modulate(blk):
                par = blk % 2
                for k in range(KD):
                    sl = cn["x"] % 2
                    cn["x"] += 1
                    S.add("sp", lambda e, k=k, sl=sl: e.dma_start(out=xs[:, sl, :], in_=src[k * 128:(k + 1) * 128, blk * TB:(blk + 1) * TB]),
                          w=[("xs", sl)], dma=("xs", sl))
                    for (lo, hi, which) in s.segs(blk):
                        sc1 = s.modv[:, (3 * sub + 1) * KD + k, which:which + 1]
                        sh = s.modv[:, (3 * sub) * KD + k, which:which + 1]
                        if k % 2 == 0:
                            S.add("act", lambda e, k=k, sl=sl, lo=lo, hi=hi, sc1=sc1, sh=sh: e.activation(
                                out=xm[:, par, k, lo:hi], in_=xs[:, sl, lo:hi], func=AF.Identity, scale=sc1, bias=sh),
                                r=[("xs", sl), ("modv",)], w=[("xm", par, k)])
                        else:
                            S.add("dve", lambda e, k=k, sl=sl, lo=lo, hi=hi, sc1=sc1, sh=sh: e.tensor_scalar(
                                xm[:, par, k, lo:hi], xs[:, sl, lo:hi], sc1, sh, ALU.mult, ALU.add),
                                r=[("xs", sl), ("modv",)], w=[("xm", par, k)])

            def norm_chunk(blk, j):
                par = blk % 2
                o = cn["n"] % 2
                cn["n"] += 1
                S.add("sp", lambda e: e.dma_start(out=nin[:, o, :], in_=U[par, j * 128:(j + 1) * 128, :]),
                      r=[("U", par, j)], w=[("nin", o)], dma=("nin", o))
                S.add("dve", lambda e: e.tensor_tensor(out=nin[:, o, :], in0=nin[:, o, :], in1=lr[:, par, :], op=ALU.mult),
                      r=[("nin", o), ("lr", par)], w=[("nin", o)])
                S.add("dve", lambda e: e.tensor_tensor(out=nin[:, o, :], in0=nin[:, o, :], in1=lnn[:, par, :], op=ALU.add),
                      r=[("nin", o), ("lnn", par)], w=[("nin", o)])
                S.add("act", lambda e: e.activation(out=nin[:, o, :], in_=nin[:, o, :], func=AF.Identity,
                                                    scale=s.sm("lng", sub * KD + j), bias=s.sm("lnb", sub * KD + j)),
                      r=[("nin", o), ("small",)], w=[("nin", o)])
                S.add(STQ, lambda e: e.dma_start(out=dst[j * 128:(j + 1) * 128, blk * TB:(blk + 1) * TB], in_=nin[:, o, :]),
                      r=[("nin", o)], dma=("nin", "st", o))

            def h_phase(blk):
                par = blk % 2
                for f in range(FK):
                    sl = cn["w1"] % NW1
                    cn["w1"] += 1
                    for gu in range(2):
                        S.add("pool", lambda e, f=f, sl=sl, gu=gu: e.dma_start(out=w1t[:, sl, gu, :], in_=s.w1[l, which_ffn, gu * FK + f],
                                                                              max_dma_last_dim=8192),
                              w=[("w1", sl, gu)], dma=("w1", sl, gu))
                    bs = 4 * (f % 2)
                    for gu in range(2):
                        for k in range(KD):
                            for h in range(2):
                                S.add("pe", lambda e, sl=sl, gu=gu, k=k, h=h, bs=bs: e.matmul(
                                    s.ps[bs + 2 * gu + h][:, 0:H], w1t[:, sl, gu, k * 128:(k + 1) * 128], xm[:, par, k, h * H:(h + 1) * H],
                                    start=(k == 0), stop=(k == KD - 1)),
                                    r=[("w1", sl, gu), ("xm", par, k)], w=[("ps", bs + 2 * gu + h)])
                    for h in range(2):
                        g = cn["sg"] % 2
                        cn["sg"] += 1
                        S.add("act", lambda e, g=g, h=h, bs=bs: e.activation(out=sgt[:, g, :], in_=s.ps[bs + h][:, 0:H], func=AF.Silu),
                              r=[("ps", bs + h)], w=[("sg", g)])
                        S.add("dve", lambda e, g=g, h=h, bs=bs, f=f: e.tensor_tensor(out=act[:, f, h * H:(h + 1) * H], in0=s.ps[bs + 2 + h][:, 0:H],
                                                                                     in1=sgt[:, g, :], op=ALU.mult),
                              r=[("ps", bs + 2 + h), ("sg", g)], w=[("act", f)])
                    if blk > 0 and f < KD:
                        norm_chunk(blk - 1, f)

            def y_phase(blk):
                par = blk % 2
                segs = s.segs(blk)
                pending = None
                for j in range(KD):
                    sl = cn["w2"] % 2
                    cn["w2"] += 1
                    S.add("pool", lambda e, j=j, sl=sl: e.dma_start(out=w2t[:, sl, :], in_=s.w2[l, which_ffn, j], max_dma_last_dim=8192),
                          w=[("w2", sl)], dma=("w2", sl))
                    xl = cn["x"] % 2
                    cn["x"] += 1
                    S.add("sp", lambda e, j=j, xl=xl: e.dma_start(out=xs[:, xl, :], in_=src[j * 128:(j + 1) * 128, blk * TB:(blk + 1) * TB]),
                          w=[("xs", xl)], dma=("xs", xl))
                    bs = 2 * (j % 2)
                    for f in range(FK):
                        for h in range(2):
                            S.add("pe", lambda e, sl=sl, f=f, h=h, bs=bs: e.matmul(
                                s.ps[bs + h][:, 0:H], w2t[:, sl, f * 128:(f + 1) * 128], act[:, f, h * H:(h + 1) * H],
                                start=(f == 0), stop=(f == FK - 1)),
                                r=[("w2", sl), ("act", f)], w=[("ps", bs + h)])
                    if pending is not None:
                        pending()
                    q = j % 2
                    for (lo, hi, which) in segs:
                        gcol = s.modv[:, (3 * sub + 2) * KD + j, which:which + 1]
                        for h in range(2):
                            a0, a1 = max(lo, h * H), min(hi, (h + 1) * H)
                            if a0 >= a1:
                                continue
                            S.add("dve", lambda e, h=h, a0=a0, a1=a1, gcol=gcol, xl=xl, bs=bs, q=q: e.scalar_tensor_tensor(
                                out=uo[:, q, a0:a1], in0=s.ps[bs + h][:, a0 - h * H:a1 - h * H], scalar=gcol, in1=xs[:, xl, a0:a1],
                                op0=ALU.mult, op1=ALU.add),
                                r=[("ps", bs + h), ("xs", xl), ("modv",)], w=[("uo", q)])
                    S.add("act", lambda e, q=q: e.activation(out=sq[:, q, :], in_=uo[:, q, :], func=AF.Square),
                          r=[("uo", q)], w=[("sq", q)])
                    S.add("sp", lambda e, j=j, q=q: e.dma_start(out=U[par, j * 128:(j + 1) * 128, :], in_=uo[:, q, :]),
                          r=[("uo", q)], w=[("U", par, j)], dma=("uo", q))

                    def mk(j=j, q=q):
                        return lambda: s.stats_mm(j, KD, uo[:, q, :], sq[:, q, :], ("uo", q), ("sq", q))
                    pending = mk()
                pending()
                lnt = [(sq[:, 1, :], ("sq", 1)), (lr[:, par, :], ("lr", par)), (lnn[:, par, :], ("lnn", par)), (sq[:, 0, :], ("sq", 0))]
                s.ln_tail(None, 0, c.D, "ffn", lnt, None)

            modulate(0)
            for blk in range(c.NB):
                h_phase(blk)
                if blk + 1 < c.NB:
                    modulate(blk + 1)
                y_phase(blk)
            for j in range(KD):
                norm_chunk(c.NB - 1, j)
            S.flush(final=final)

    def phase_mixa(s, l, src):
        from contextlib import ExitStack
        c, S, nc = s.c, s.S, s.nc
        H, TB, KD, LK, CK = c.HALF, c.TB, c.KD, c.LK, c.CK
        NR = TB // 64
        PW = 64 + 30
        PL = NR * PW
        with ExitStack() as st:
            T_ = s.mkT(st)
            xmod = T_("hmod", [128, KD, TB], BF16)
            xs = T_("xs", [128, 3, TB])
            wt = T_("wint", [128, 4, c.D], BF16)
            xo = T_("xo", [128, 3, TB])
            sig = T_("sig", [128, 2, TB])
            upad = T_("upad", [128, CK, PL], BF16)
            dg = T_("dg", [128, CK, c.K31, 128], BF16)
            acc = T_("acc", [128, CK, TB])
            sq = T_("sq", [128, 2, TB])
            yo = T_("yo", [128, 2, TB], BF16)
            lnt = [(T_("lnm", [128, TB]), ("ln", "mean")), (T_("lnr", [128, TB]), ("ln", "rstd")),
                   (T_("lnn", [128, TB]), ("ln", "nmr")), (T_("lnt", [128, TB]), ("ln", "tmp"))]
            s.eps_tiles(st)
            xcnt = [0]
            wcnt = 0
            xocnt = 0
            for cc in range(CK):
                for kk in range(c.K31):
                    wk = s.sm("c31w", cc * 31 + kk)
                    if (cc * 31 + kk) % 2 == 0:
                        S.add("dve", lambda e, cc=cc, kk=kk, wk=wk: e.tensor_scalar(dg[:, cc, kk, :], s.ident[:], wk, None, ALU.mult),
                              r=[("ident",), ("small",)], w=[("dg", cc)])
                    else:
                        S.add("act", lambda e, cc=cc, kk=kk, wk=wk: e.activation(out=dg[:, cc, kk, :], in_=s.ident[:], func=AF.Identity, scale=wk),
                              r=[("ident",), ("small",)], w=[("dg", cc)])

            def body(blk):
                nonlocal wcnt, xocnt
                segs = s.segs(blk)
                cols = slice(blk * TB, (blk + 1) * TB)
                has_ctx = any(w == 1 for (_, _, w) in segs)
                nlat = segs[0][1] if segs[0][2] == 0 else 0
                nrl = nlat // 64
                s.modulate(src, blk, 1, xmod, xs, 3, xcnt)
                if blk == 0 or has_ctx:
                    S.add("dve", lambda e: e.memset(upad[:], 0.0), w=[("upad", cc) for cc in range(CK)])
                order = list(range(2 * LK))
                for cc in range(CK):
                    order += [2 * LK + CK + cc, 2 * LK + cc]
                for oi, o in enumerate(order):
                    sl = wcnt % 4
                    bs = 2 * (wcnt % 2)
                    wcnt += 1
                    S.add("pool", lambda e, o=o, sl=sl: e.dma_start(out=wt[:, sl, :], in_=s.win[l, o], max_dma_last_dim=8192),
                          w=[("win", sl)], dma=("win", sl))
                    for k in range(KD):
                        for h in range(2):
                            S.add("pe", lambda e, sl=sl, k=k, h=h, bs=bs: e.matmul(
                                s.ps[bs + h][:, 0:H], wt[:, sl, k * 128:(k + 1) * 128], xmod[:, k, h * H:(h + 1) * H],
                                start=(k == 0), stop=(k == KD - 1)),
                                r=[("win", sl), ("R1", k)], w=[("ps", bs + h)])
                    if o < 2 * LK:
                        xl = xocnt % 3
                        xocnt += 1
                        isg = o >= LK
                        for h in range(2):
                            if isg:
                                S.add("act", lambda e, xl=xl, h=h, bs=bs: e.activation(out=xo[:, xl, h * H:(h + 1) * H], in_=s.ps[bs + h][:, 0:H], func=AF.Gelu),
                                      r=[("ps", bs + h)], w=[("xo", xl)])
                            else:
                                S.add("act", lambda e, xl=xl, h=h, bs=bs: e.activation(out=xo[:, xl, h * H:(h + 1) * H], in_=s.ps[bs + h][:, 0:H], func=AF.Identity),
                                      r=[("ps", bs + h)], w=[("xo", xl)])
                        dstt = s.GG if isg else s.XR
                        ch = o - LK if isg else o
                        S.add(STQ, lambda e, xl=xl, dstt=dstt, ch=ch: e.dma_start(out=dstt[ch * 128:(ch + 1) * 128, cols], in_=xo[:, xl, :]),
                              r=[("xo", xl)], dma=("xo", "st", xl))
                    elif o >= 2 * LK + CK:
                        cc = o - 2 * LK - CK
                        sgl = cc % 2
                        for h in range(2):
                            S.add("act", lambda e, sgl=sgl, h=h, bs=bs: e.activation(out=sig[:, sgl, h * H:(h + 1) * H], in_=s.ps[bs + h][:, 0:H], func=AF.Sigmoid),
                                  r=[("ps", bs + h)], w=[("sig", sgl)])
                    else:
                        cc = o - 2 * LK
                        sgl = cc % 2
                        for h in range(2):
                            r0, r1 = h * (H // 64), min((h + 1) * (H // 64), nrl)
                            if r1 > r0:
                                n = (r1 - r0) * 64
                                outv = upad[:, cc, r0 * PW:r1 * PW].rearrange("p (r w) -> p r w", w=PW)[:, :, 15:79]
                                S.add("dve", lambda e, outv=outv, n=n, h=h, bs=bs, sgl=sgl: e.tensor_tensor(
                                    out=outv, in0=s.ps[bs + h][:, 0:n].rearrange("p (r w) -> p r w", w=64),
                                    in1=sig[:, sgl, h * H:h * H + n].rearrange("p (r w) -> p r w", w=64), op=ALU.mult),
                                    r=[("ps", bs + h), ("sig", sgl)], w=[("upad", cc)])
                            if has_ctx and h == 1:
                                a0 = nlat - H
                                S.add("dve", lambda e, a0=a0, bs=bs, sgl=sgl, cc=cc: e.tensor_tensor(
                                    out=upad[:, cc, nrl * PW + 15:nrl * PW + 15 + c.TC], in0=s.ps[bs + 1][:, a0:a0 + c.TC],
                                    in1=sig[:, sgl, H + a0:H + a0 + c.TC], op=ALU.mult),
                                    r=[("ps", bs + 1), ("sig", sgl)], w=[("upad", cc)])
                        cb = 2 * (wcnt % 2)
                        wcnt += 1
                        R6 = H // 64
                        for h in range(2):
                            r0, r1 = h * R6, min((h + 1) * R6, nrl)
                            if r1 > r0:
                                n = (r1 - r0) * 64
                                for kk in range(c.K31):
                                    rv = upad[:, cc, r0 * PW:r1 * PW].rearrange("p (r w) -> p r w", w=PW)[:, :, kk:kk + 64]
                                    S.add("pe", lambda e, cc=cc, kk=kk, h=h, n=n, rv=rv, cb=cb: e.matmul(
                                        s.ps[cb + h][:, 0:n].rearrange("p (r w) -> p r w", w=64), dg[:, cc, kk, :], rv,
                                        start=(kk == 0), stop=(kk == c.K31 - 1)),
                                        r=[("dg", cc), ("upad", cc)], w=[("ps", cb + h)])
                            if has_ctx and h == 1:
                                a0 = nlat - H
                                for kk in range(c.K31):
                                    S.add("pe", lambda e, cc=cc, kk=kk, a0=a0, cb=cb: e.matmul(
                                        s.ps[cb + 1][:, a0:a0 + c.TC], dg[:, cc, kk, :], upad[:, cc, nrl * PW + kk:nrl * PW + kk + c.TC],
                                        start=(kk == 0), stop=(kk == c.K31 - 1)),
                                        r=[("dg", cc), ("upad", cc)], w=[("ps", cb + 1)])
                        q = cc % 2
                        for h in range(2):
                            S.add("act", lambda e, cc=cc, h=h, cb=cb: e.activation(out=acc[:, cc, h * H:(h + 1) * H], in_=s.ps[cb + h][:, 0:H], func=AF.Identity,
                                                                                   bias=s.sm("c31b", cc)),
                                  r=[("ps", cb + h), ("small",)], w=[("acc", cc)])
                            S.add("act", lambda e, cc=cc, h=h, cb=cb, q=q: e.activation(out=sq[:, q, h * H:(h + 1) * H], in_=s.ps[cb + h][:, 0:H], func=AF.Square,
                                                                                        bias=s.sm("c31b", cc)),
                                  r=[("ps", cb + h), ("small",)], w=[("sq", q)])
                        s.stats_mm(cc, CK, acc[:, cc, :], sq[:, q, :], ("acc", cc), ("sq", q))
                ocnt = [0]

                def emit_out(j):
                    o = ocnt[0] % 2
                    ocnt[0] += 1
                    S.add("act", lambda e, j=j, o=o: e.activation(out=yo[:, o, :], in_=acc[:, j, :], func=AF.Silu,
                                                                  scale=s.sm("clng", j), bias=s.sm("clnb", j)),
                          r=[("acc", j), ("small",)], w=[("yo", o)])
                    S.add(STQ, lambda e, j=j, o=o: e.dma_start(out=s.YC[j * 128:(j + 1) * 128, cols], in_=yo[:, o, :]),
                          r=[("yo", o)], dma=("yo", "st", o))
                s.ln_tail(acc, CK, c.DCONV, "conv", lnt, emit_out, ukeys=lambda j: [("acc", j)])
            for blk_ in range(c.NB):
                body(blk_)
            S.flush()

    def phase_scan(s, l, last):
        from contextlib import ExitStack
        c, S, nc = s.c, s.S, s.nc
        LK, TS, NH, HC, HD = c.LK, c.TS, c.NH, c.HC, c.HD
        with ExitStack() as st:
            T_ = s.mkT(st)
            gw = T_("gw", [128, 2 * 2 * NH * HC * HD], BF16)
            xrh = T_("xrh", [128, 2, LK, TS + 3])
            xc = T_("xc", [128, 2, LK, TS])
            xcb = T_("xcb", [128, 2, LK, TS], BF16)
            Rt = T_("Rt", [128, LK, TS])
            It = T_("It", [128, LK, TS])
            At = T_("At", [128, LK, TS])
            St = T_("St", [128, LK, TS])
            hh = T_("hh", [128, 3, TS])
            rf = T_("rf", [128, LK, TS])
            gg = T_("gg", [128, LK, TS])
            yr = T_("yr", [128, 2, TS], BF16)
            state = T_("state", [128, LK])
            S.add("pool", lambda e: e.dma_start(out=gw[:], in_=s.gwt[l], max_dma_last_dim=8192), w=[("gw",)], dma=("gw",))

            def gwv(d, g, hd, jc, ic):
                base = (((d * 2 + g) * NH + hd) * HC + jc) * HD + ic * 128
                return gw[:, base:base + 128]
            XRv = s.XR.rearrange("(c p) t -> p c t", p=128)
            RECv = s.REC.rearrange("(c p) t -> p c t", p=128)
            GGv = s.GG.rearrange("(c p) t -> p c t", p=128)
            cnt = {"x": 0, "h": 0, "rf": 0, "yr": 0, "cb": 0}
            seq_ctx = (c.T, c.TC)
            seq_lat = (0, c.T)

            def partA(i, d, off, ln, is_ctx, b):
                sl = i % 2
                lo = b * TS
                n = min(TS, ln - lo)
                xl = cnt["x"] % 2
                cnt["x"] += 1
                a0 = max(lo - 2, 0)
                a1 = min(lo + n + 1, ln)
                d0 = a0 - (lo - 2)
                if d0 > 0:
                    S.add("dve", lambda e: e.memset(xrh[:, xl, :, 0:d0], 0.0), w=[("xrh", xl)])
                if a1 < lo + n + 1:
                    S.add("dve", lambda e: e.memset(xrh[:, xl, :, n + 2:n + 3], 0.0), w=[("xrh", xl)])
                S.add("sp", lambda e: e.dma_start(out=xrh[:, xl, :, d0:d0 + (a1 - a0)], in_=XRv[:, :, off + a0:off + a1]),
                      w=[("xrh", xl)], dma=("xrh", xl))
                for ch in range(LK):
                    S.add("act", lambda e, ch=ch: e.activation(out=xc[:, sl, ch, 0:n], in_=xrh[:, xl, ch, 0:n], func=AF.Identity,
                                                               scale=s.sm("c4w", ch * 4 + 0), bias=s.sm("c4b", ch)),
                          r=[("xrh", xl), ("small",)], w=[("xc", sl, ch)])
                    for kk in range(1, 4):
                        S.add("dve", lambda e, ch=ch, kk=kk: e.scalar_tensor_tensor(
                            out=xc[:, sl, ch, 0:n], in0=xrh[:, xl, ch, kk:kk + n], scalar=s.sm("c4w", ch * 4 + kk), in1=xc[:, sl, ch, 0:n],
                            op0=ALU.mult, op1=ALU.add), r=[("xrh", xl), ("small",)], w=[("xc", sl, ch)])
                    S.add("pool", lambda e, ch=ch: e.tensor_copy(xcb[:, sl, ch, 0:n], xc[:, sl, ch, 0:n]),
                          r=[("xc", sl, ch)], w=[("xcb", sl, ch)])

            def partB(i, d, off, ln, is_ctx, b):
                sl = i % 2
                lo = b * TS
                n = min(TS, ln - lo)
                if d == 1 and not (is_ctx and last):
                    S.add("sp", lambda e: e.dma_start(out=rf[:, :, 0:n], in_=RECv[:, :, off + lo:off + lo + n]), w=[("rf", ch) for ch in range(LK)], dma=("rf",))
                    S.add("sp", lambda e: e.dma_start(out=gg[:, :, 0:n], in_=GGv[:, :, off + lo:off + lo + n]), w=[("gg", ch) for ch in range(LK)], dma=("gg",))

                for hf in range(2 if LK >= 4 else 1):
                    chs = range(hf * (LK // 2), (hf + 1) * (LK // 2)) if LK >= 4 else range(LK)
                    for g, dstt, bname in ((0, Rt, "brg"), (1, It, "big")):
                        for ch in chs:
                            hd, jc = ch // HC, ch % HC
                            bank = ch % 8
                            for ic in range(HC):
                                S.add("pe", lambda e, g=g, hd=hd, jc=jc, ic=ic, bank=bank: e.matmul(
                                    s.ps[bank][:, 0:n], gwv(d, g, hd, jc, ic), xcb[:, sl, hd * HC + ic, 0:n], start=(ic == 0), stop=(ic == HC - 1)),
                                    r=[("gw",), ("xcb", sl, hd * HC + ic)], w=[("ps", bank)])
                            S.add("act", lambda e, ch=ch, bank=bank, dstt=dstt, bname=bname: e.activation(
                                out=dstt[:, ch, 0:n], in_=s.ps[bank][:, 0:n], func=AF.Sigmoid, bias=s.sm(bname, d * LK + ch)),
                                r=[("ps", bank), ("small",)], w=[("g%d" % g, ch)])
                    for ch in chs:
                        S.add("act", lambda e, ch=ch: e.activation(out=At[:, ch, 0:n], in_=Rt[:, ch, 0:n], func=AF.Exp,
                                                                   scale=s.clam[:, d * LK + ch:d * LK + ch + 1]),
                              r=[("g0", ch), ("clam",)], w=[("A", ch)])
                        S.add("act", lambda e, ch=ch: e.activation(out=St[:, ch, 0:n], in_=Rt[:, ch, 0:n], func=AF.Exp,
                                                                   scale=s.clam2[:, d * LK + ch:d * LK + ch + 1]),
                              r=[("g0", ch), ("clam",)], w=[("S", ch)])
                    for ch in chs:
                        S.add("act", lambda e, ch=ch: e.activation(out=St[:, ch, 0:n], in_=St[:, ch, 0:n], func=AF.Sqrt, scale=-1.0,
                                                                   bias=s.ones[:, 0:1]),
                              r=[("S", ch), ("ones",)], w=[("S", ch)])

                for ch in range(LK):
                    S.add("pool", lambda e, ch=ch: e.tensor_tensor(out=St[:, ch, 0:n], in0=St[:, ch, 0:n], in1=It[:, ch, 0:n], op=ALU.mult),
                          r=[("S", ch), ("g1", ch)], w=[("S", ch)])
                    S.add("pool", lambda e, ch=ch: e.tensor_tensor(out=St[:, ch, 0:n], in0=St[:, ch, 0:n], in1=xc[:, sl, ch, 0:n], op=ALU.mult),
                          r=[("S", ch), ("xc", sl, ch)], w=[("S", ch)])
                    hl = cnt["h"] % 3
                    cnt["h"] += 1
                    if d == 0:
                        S.add("dve", lambda e, ch=ch, hl=hl: e.tensor_tensor_scan(hh[:, hl, 0:n], At[:, ch, 0:n], St[:, ch, 0:n],
                                                                                 state[:, ch:ch + 1], ALU.mult, ALU.add),
                              r=[("A", ch), ("S", ch), ("state", ch)], w=[("hh", hl)])
                        S.add("dve", lambda e, ch=ch, hl=hl: e.tensor_copy(state[:, ch:ch + 1], hh[:, hl, n - 1:n]),
                              r=[("hh", hl)], w=[("state", ch)])
                        if not (is_ctx and last):
                            S.add("sp", lambda e, ch=ch, hl=hl: e.dma_start(
                                out=s.REC[ch * 128:(ch + 1) * 128, off + lo:off + lo + n], in_=hh[:, hl, 0:n]),
                                r=[("hh", hl)], dma=("hh", hl))
                    else:
                        S.add("dve", lambda e, ch=ch, hl=hl: e.tensor_tensor_scan(hh[:, hl, 0:n][:, ::-1], At[:, ch, 0:n][:, ::-1],
                                                                                 St[:, ch, 0:n][:, ::-1],
                                                                                 state[:, ch:ch + 1], ALU.mult, ALU.add),
                              r=[("A", ch), ("S", ch), ("state", ch)], w=[("hh", hl)])
                        S.add("dve", lambda e, ch=ch, hl=hl: e.tensor_copy(state[:, ch:ch + 1], hh[:, hl, 0:1]),
                              r=[("hh", hl)], w=[("state", ch)])
                        if not (is_ctx and last):
                            rl = cnt["rf"] % 2
                            cnt["rf"] += 1
                            S.add("pool", lambda e, ch=ch, hl=hl: e.tensor_tensor(out=hh[:, hl, 0:n], in0=hh[:, hl, 0:n], in1=rf[:, ch, 0:n], op=ALU.add),
                                  r=[("hh", hl), ("rf", ch)], w=[("hh", hl)])
                            S.add("dve", lambda e, ch=ch, hl=hl, rl=rl: e.tensor_tensor(out=yr[:, rl, 0:n], in0=hh[:, hl, 0:n], in1=gg[:, ch, 0:n], op=ALU.mult),
                                  r=[("hh", hl), ("gg", ch)], w=[("yr", rl)])
                            S.add("sp", lambda e, ch=ch, rl=rl: e.dma_start(
                                out=s.YR[ch * 128:(ch + 1) * 128, off + lo:off + lo + n], in_=yr[:, rl, 0:n]),
                                r=[("yr", rl)], dma=("yr", rl))

            for d in range(2):
                S.add("dve", lambda e: e.memset(state[:], 0.0), w=[("state", ch) for ch in range(LK)])
                items = []
                for (off, ln), is_ctx in ((seq_ctx, True), (seq_lat, False)):
                    nblk = (ln + TS - 1) // TS
                    blks = list(range(nblk))
                    if d == 1:
                        blks = blks[::-1]
                    for b in blks:
                        items.append((d, off, ln, is_ctx, b))
                partA(0, *items[0])
                for i, it in enumerate(items):
                    if i + 1 < len(items):
                        partA(i + 1, *items[i + 1])
                    partB(i, *it)
                S.flush()

    def phase_mixc(s, l, src, dst):
        from contextlib import ExitStack
        c, S, nc = s.c, s.S, s.nc
        H, TB, KD, MK, LK, CK = c.HALF, c.TB, c.KD, c.MK, c.LK, c.CK
        with ExitStack() as st:
            T_ = s.mkT(st)
            ym = T_("ym", [128, 2, MK, TB], BF16)
            u32 = T_("u32", [128, 2, KD, TB])
            wt = T_("woutt", [128, 3, c.DMIX], BF16)
            xs = T_("xs", [128, 3, TB])
            sq = T_("sq", [128, 2, TB])
            osl = T_("osl", [128, 2, TB])
            lr = T_("lr", [128, 2, TB])
            lnn = T_("lnn", [128, 2, TB])
            lm = T_("lm", [128, TB])
            lt = T_("lt", [128, TB])
            s.eps_tiles(st)
            YRv = s.YR.rearrange("(c p) t -> p c t", p=128)
            YCv = s.YC.rearrange("(c p) t -> p c t", p=128)
            cn = {"x": 0, "w": 0, "o": 0}

            def load_ym(blk):
                par = blk % 2
                cols = slice(blk * TB, (blk + 1) * TB)
                S.add("sp", lambda e: e.dma_start(out=ym[:, par, 0:LK, :], in_=YRv[:, :, cols]), w=[("ym", par, 0)], dma=("ym", par, 0))
                S.add("sp", lambda e: e.dma_start(out=ym[:, par, LK:MK, :], in_=YCv[:, :, cols]), w=[("ym", par, 1)], dma=("ym", par, 1))

            def tail_chunk(blk, j):
                par = blk % 2
                cols = slice(blk * TB, (blk + 1) * TB)
                o = cn["o"] % 2
                cn["o"] += 1
                S.add("dve", lambda e: e.tensor_tensor(out=u32[:, par, j, :], in0=u32[:, par, j, :], in1=lr[:, par, :], op=ALU.mult),
                      r=[("u", par, j), ("lr", par)], w=[("u", par, j)])
                S.add("dve", lambda e: e.tensor_tensor(out=u32[:, par, j, :], in0=u32[:, par, j, :], in1=lnn[:, par, :], op=ALU.add),
                      r=[("u", par, j), ("lnn", par)], w=[("u", par, j)])
                S.add("act", lambda e: e.activation(out=osl[:, o, :], in_=u32[:, par, j, :], func=AF.Identity,
                                                    scale=s.sm("lng", 1 * KD + j), bias=s.sm("lnb", 1 * KD + j)),
                      r=[("u", par, j), ("small",)], w=[("os", o)])
                S.add(STQ, lambda e: e.dma_start(out=dst[j * 128:(j + 1) * 128, cols], in_=osl[:, o, :]),
                      r=[("os", o)], dma=("os", "st", o))

            def body(blk):
                par = blk % 2
                segs = s.segs(blk)
                cols = slice(blk * TB, (blk + 1) * TB)
                if blk + 1 < c.NB:
                    load_ym(blk + 1)
                pending = None
                for j in range(KD):
                    sl = cn["w"] % 3
                    cn["w"] += 1
                    S.add("pool", lambda e, j=j, sl=sl: e.dma_start(out=wt[:, sl, :], in_=s.wout[l, j], max_dma_last_dim=8192),
                          w=[("wo", sl)], dma=("wo", sl))
                    xl = cn["x"] % 3
                    cn["x"] += 1
                    S.add("sp", lambda e, j=j, xl=xl: e.dma_start(out=xs[:, xl, :], in_=src[j * 128:(j + 1) * 128, cols]),
                          w=[("xs", xl)], dma=("xs", xl))
                    bs = 2 * (j % 2)
                    for kc in range(MK):
                        for h in range(2):
                            S.add("pe", lambda e, sl=sl, kc=kc, h=h, bs=bs: e.matmul(
                                s.ps[bs + h][:, 0:H], wt[:, sl, kc * 128:(kc + 1) * 128], ym[:, par, kc, h * H:(h + 1) * H],
                                start=(kc == 0), stop=(kc == MK - 1)),
                                r=[("wo", sl), ("ym", par, 0 if kc < LK else 1)], w=[("ps", bs + h)])
                    if pending is not None:
                        pending()
                    for (lo, hi, which) in segs:
                        S.add("act", lambda e, j=j, xl=xl, lo=lo, hi=hi, which=which: e.activation(
                            out=xs[:, xl, lo:hi], in_=xs[:, xl, lo:hi], func=AF.Identity, bias=s.bgv[:, j, which:which + 1]),
                            r=[("xs", xl), ("bgv",)], w=[("xs", xl)])
                        gcol = s.modv[:, 5 * KD + j, which:which + 1]
                        for h in range(2):
                            a0, a1 = max(lo, h * H), min(hi, (h + 1) * H)
                            if a0 >= a1:
                                continue
                            S.add("dve", lambda e, j=j, h=h, a0=a0, a1=a1, gcol=gcol, xl=xl, bs=bs: e.scalar_tensor_tensor(
                                out=u32[:, par, j, a0:a1], in0=s.ps[bs + h][:, a0 - h * H:a1 - h * H], scalar=gcol, in1=xs[:, xl, a0:a1],
                                op0=ALU.mult, op1=ALU.add),
                                r=[("ps", bs + h), ("xs", xl), ("modv",)], w=[("u", par, j)])
                    q = j % 2
                    S.add("act", lambda e, j=j, q=q: e.activation(out=sq[:, q, :], in_=u32[:, par, j, :], func=AF.Square),
                          r=[("u", par, j)], w=[("sq", q)])

                    def mk(j=j, q=q):
                        return lambda: s.stats_mm(j, KD, u32[:, par, j, :], sq[:, q, :], ("u", par, j), ("sq", q))
                    pending = mk()
                    if blk > 0:
                        tail_chunk(blk - 1, j)
                pending()
                lnt = [(lm, ("ln", "mean")), (lr[:, par, :], ("lr", par)), (lnn[:, par, :], ("lnn", par)), (lt, ("ln", "tmp"))]
                s.ln_tail(None, 0, c.D, "ffn", lnt, None)

            load_ym(0)
            for blk_ in range(c.NB):
                body(blk_)
            for j in range(KD):
                tail_chunk(c.NB - 1, j)
            S.flush()


def tile_w(W):
    K, N = W.shape
    return np.ascontiguousarray(W.reshape(K // 128, 128, N // 128, 128).transpose(2, 1, 0, 3).reshape(N // 128, 128, K))


def fm(v):
    sh = v.shape
    return np.moveaxis(v.reshape(sh[:-1] + (sh[-1] // 128, 128)), -1, 0)


def prep_inputs(cfg, inp):
    c = cfg
    L = c.DEPTH
    f32 = lambda a: np.ascontiguousarray(np.asarray(a, dtype=np.float32))
    small = np.zeros((L, 128, c.NS), np.float32)
    so = c.so
    for l in range(L):
        def put(name, arr):
            arr = np.asarray(arr, np.float32).reshape(128, -1)
            small[l, :, so[name]:so[name] + arr.shape[1]] = arr
        put("bada", fm(inp["b_ada"][l]))
        put("lng", fm(inp["ln_g"][l]))
        put("lnb", fm(inp["ln_b"][l]))
        put("c4w", np.moveaxis(fm(inp["conv4_w"][l]), 1, 2))
        put("c4b", fm(inp["conv4_b"][l]))
        put("brg", fm(inp["b_rg"][l]))
        put("big", fm(inp["b_ig"][l]))
        put("lam", fm(inp["lam"][l]))
        put("c31w", np.moveaxis(fm(inp["conv31_w"][l]), 1, 2))
        put("c31b", fm(inp["conv31_b"][l]))
        put("clng", fm(inp["cln_g"][l]))
        put("clnb", fm(inp["cln_b"][l]))
        put("bout", fm(inp["b_out"][l]))
    wada = np.stack([tile_w(f32(inp["w_ada"][l])) for l in range(L)])
    w1 = np.stack([np.stack([tile_w(f32(inp[k][l])) for k in ("ff1_in", "ff2_in")]) for l in range(L)])
    w2 = np.stack([np.stack([tile_w(f32(inp[k][l])) for k in ("ff1_out", "ff2_out")]) for l in range(L)])
    win = np.stack([tile_w(f32(inp["w_in"][l])) for l in range(L)])
    wout = np.stack([tile_w(f32(inp["w_out"][l])) for l in range(L)])
    gw = np.zeros((L, 128, 2, 2, c.NH, c.HC, c.HD), np.float32)
    for l in range(L):
        for d in range(2):
            for g, nm in enumerate(("w_rg", "w_ig")):
                for h in range(c.NH):
                    tw = tile_w(f32(inp[nm][l][d][h]))
                    gw[l, :, d, g, h] = tw.transpose(1, 0, 2)
    gw = gw.reshape(L, 128, -1)
    shared = dict(small=small, wada=wada, w1=w1, w2=w2, win=win, wout=wout, gwt=np.ascontiguousarray(gw))
    maps = []
    for b in range(c.NCORES):
        xin = np.concatenate([f32(inp["x"][b]).T, f32(inp["ctx"][b]).T], axis=1)
        cv = np.stack([fm(f32(inp["c"][b])), fm(f32(inp["c_ctx"]))], axis=-1).reshape(128, -1)
        m = dict(shared)
        m["xin"] = np.ascontiguousarray(xin)
        m["cvec"] = np.ascontiguousarray(cv)
        m["ident"] = np.eye(128, dtype=np.float32)
        maps.append(m)
    return maps


_NC_CACHE = {}


def run_cfg(cfg, inp):
    key = (cfg.D, cfg.FF, cfg.T, cfg.DLRU, cfg.DCONV, getattr(cfg, "STOP", None))
    if key not in _NC_CACHE:
        _NC_CACHE[key] = Builder(cfg).build()
    nc = _NC_CACHE[key]
    maps = prep_inputs(cfg, inp)
    res = run_bass_kernel_spmd(nc, maps, core_ids=list(range(cfg.NCORES)))
    if getattr(cfg, "STOP", None) is not None:
        return np.stack([np.ascontiguousarray(res.results[b]["yout"].T) for b in range(cfg.NCORES)])
    out = np.stack([np.ascontiguousarray(res.results[b]["yout"][:, :cfg.T].T) for b in range(cfg.NCORES)])
    return out.astype(np.float32)


def kernel(**inputs):
    cfg = Cfg()
    return run_cfg(cfg, inputs)
```

```python
import numpy as np
import concourse.bass as bass
import concourse.mybir as mybir
from concourse.bass_utils import run_bass_kernel_spmd

F32 = mybir.dt.float32
BF16 = mybir.dt.bfloat16
AF = mybir.ActivationFunctionType
ALU = mybir.AluOpType

RG_C = 8.0
EPS = 1e-6
SAME_SYNC = True
STQ = "act"
ENGS = ("pe", "act", "dve", "pool", "sp")


class Cfg:
    def __init__(s, D=2048, FF=5632, DLRU=1024, NH=4, DCONV=1024, T=8192, TC=256, DEPTH=2,
                 TB=768, TS=512, NCORES=2):
        s.D, s.FF, s.DLRU, s.NH, s.DCONV, s.T, s.TC, s.DEPTH = D, FF, DLRU, NH, DCONV, T, TC, DEPTH
        s.TB, s.TS, s.NCORES = TB, TS, NCORES
        s.HALF = TB // 2
        s.GW, s.K31 = 64, 31
        s.KD, s.FK, s.LK, s.CK = D // 128, FF // 128, DLRU // 128, DCONV // 128
        s.DMIX = DLRU + DCONV
        s.MK = s.DMIX // 128
        s.DIN = 2 * DLRU + 2 * DCONV
        s.NT = T + TC
        s.NB = s.NT // TB
        s.HD = DLRU // NH
        s.HC = s.HD // 128
        s.ALPHA = (2 * DEPTH) ** 0.25
        assert s.NT % TB == 0 and TB % 128 == 0 and s.HALF % 64 == 0 and TC <= s.HALF
        assert TC <= TS and (T % TB) in (0, TB - TC)
        o = {}
        off = 0
        for name, n in (("bada", 9 * s.KD), ("lng", 3 * s.KD), ("lnb", 3 * s.KD), ("c4w", s.LK * 4),
                        ("c4b", s.LK), ("brg", 2 * s.LK), ("big", 2 * s.LK), ("lam", 2 * s.LK),
                        ("c31w", s.CK * 31), ("c31b", s.CK), ("clng", s.CK), ("clnb", s.CK),
                        ("bout", s.KD)):
            o[name] = off
            off += n
        s.so, s.NS = o, off


class Op:
    __slots__ = ("eng", "fn", "deps", "sig", "dkey", "waits", "needed")

    def __init__(s, eng, fn, dkey):
        s.eng, s.fn, s.dkey = eng, fn, dkey
        s.deps, s.sig, s.waits, s.needed = [], None, [], False


class Sched:
    def __init__(s, nc, stack):
        s.nc = nc
        s.stack = stack
        s.ops = {e: [] for e in ENGS}
        s.lastw, s.readers = {}, {}
        s.engsem = {e: stack.enter_context(nc.semaphore("es_" + e)) for e in ENGS if e != "sp"}
        s.engcnt = {e: 0 for e in s.engsem}
        s.dsem = {}
        s.seen = {e: {} for e in ENGS}
        s.first_phase = True
        s.nops = 0

    def add(s, eng, fn, r=(), w=(), dma=None):
        op = Op(eng, fn, dma)
        deps = []
        for k in r:
            lw = s.lastw.get(k)
            if lw is not None:
                deps.append(lw)
        for k in w:
            lw = s.lastw.get(k)
            if lw is not None:
                deps.append(lw)
            deps.extend(s.readers.get(k, ()))
        for k in r:
            s.readers.setdefault(k, []).append(op)
        for k in w:
            s.lastw[k] = op
            s.readers[k] = []
        seen = set()
        for d in deps:
            if d is op or id(d) in seen:
                continue
            seen.add(id(d))
            if d.dkey is None and d.eng == eng and (eng == "pe" or (not SAME_SYNC and eng != "pool")):
                continue
            op.deps.append(d)
            d.needed = True
        s.ops[eng].append(op)
        s.nops += 1
        return op

    def _dsem(s, key):
        if key not in s.dsem:
            s.dsem[key] = [s.stack.enter_context(s.nc.semaphore("ds%d" % len(s.dsem))), 0]
        return s.dsem[key]

    def flush(s, final=False):
        nc = s.nc
        pre = {e: [] for e in ENGS}
        if not s.first_phase:
            for e in ENGS:
                for e2, sem in s.engsem.items():
                    if e2 != e and s.engcnt[e2] > 0:
                        pre[e].append((sem, s.engcnt[e2]))
                for key, (sem, cnt) in s.dsem.items():
                    if cnt > 0:
                        pre[e].append((sem, cnt))
        s.first_phase = False
        for e in ENGS:
            lst = s.ops[e]
            lastc = max([i for i, op in enumerate(lst) if op.dkey is None], default=-1)
            for i, op in enumerate(lst):
                last = i == lastc
                if op.dkey is not None:
                    d = s._dsem(op.dkey)
                    d[1] += 16
                    op.sig = (d[0], d[1], 16)
                elif op.needed or last:
                    s.engcnt[e] += 1
                    op.sig = (s.engsem[e], s.engcnt[e], 1)
        for e in ENGS:
            seen = s.seen[e]
            for sem, v in pre[e]:
                seen[id(sem)] = max(seen.get(id(sem), 0), v)
            for op in s.ops[e]:
                for d in op.deps:
                    sem, v, _ = d.sig
                    if seen.get(id(sem), 0) >= v:
                        continue
                    seen[id(sem)] = v
                    op.waits.append((sem, v))
        post = []
        if final:
            post = [(sem, cnt) for (sem, cnt) in s.dsem.values() if cnt > 0]
            post += [(sem, s.engcnt[e2]) for e2, sem in s.engsem.items() if s.engcnt[e2] > 0]
        ops = s.ops

        def replay(eng_name):
            def run(e):
                for sem, v in pre[eng_name]:
                    e.wait_ge(sem, v)
                for op in ops[eng_name]:
                    for sem, v in op.waits:
                        e.wait_ge(sem, v)
                    ins = op.fn(e)
                    if op.sig is not None:
                        ins.then_inc(op.sig[0], op.sig[2])
                if eng_name == "sp":
                    for sem, v in post:
                        e.wait_ge(sem, v)
            return run

        with nc.Block() as block:
            block.tensor(replay("pe"))
            block.scalar(replay("act"))
            block.vector(replay("dve"))
            block.gpsimd(replay("pool"))
            block.sync(replay("sp"))
        s.ops = {e: [] for e in ENGS}
        s.lastw, s.readers = {}, {}


class Builder:
    def __init__(s, cfg):
        s.c = cfg

    def build(s):
        from contextlib import ExitStack
        c = s.c
        nc = bass.Bass("TRN2", target_bir_lowering=False)
        s.nc = nc
        L = c.DEPTH
        dt = nc.dram_tensor
        s.xin = dt("xin", [c.D, c.NT], F32, kind="ExternalInput").ap()
        s.cvec = dt("cvec", [128, c.KD * 2], F32, kind="ExternalInput").ap()
        s.small = dt("small", [L, 128, c.NS], F32, kind="ExternalInput").ap()
        s.wada = dt("wada", [L, 9 * c.KD, 128, c.D], F32, kind="ExternalInput").ap()
        s.w1 = dt("w1", [L, 2, 2 * c.FK, 128, c.D], F32, kind="ExternalInput").ap()
        s.w2 = dt("w2", [L, 2, c.KD, 128, c.FF], F32, kind="ExternalInput").ap()
        s.win = dt("win", [L, c.DIN // 128, 128, c.D], F32, kind="ExternalInput").ap()
        s.wout = dt("wout", [L, c.KD, 128, c.DMIX], F32, kind="ExternalInput").ap()
        s.gwt = dt("gwt", [L, 128, 2 * 2 * c.NH * c.HC * c.HD], F32, kind="ExternalInput").ap()
        s.identd = dt("ident", [128, 128], F32, kind="ExternalInput").ap()
        s.yout = dt("yout", [c.D, c.NT], F32, kind="ExternalOutput").ap()
        s.SA = dt("SA", [c.D, c.NT], F32, kind="Internal").ap()
        s.SB = dt("SB", [c.D, c.NT], F32, kind="Internal").ap()
        s.SC = dt("SC", [c.D, c.NT], F32, kind="Internal").ap()
        s.Uscr = dt("Uscr", [2, c.D, c.TB], F32, kind="Internal").ap()
        s.XR = dt("XR", [c.DLRU, c.NT], F32, kind="Internal").ap()
        s.GG = dt("GG", [c.DLRU, c.NT], F32, kind="Internal").ap()
        s.REC = dt("REC", [c.DLRU, c.NT], F32, kind="Internal").ap()
        s.YR = dt("YR", [c.DLRU, c.NT], BF16, kind="Internal").ap()
        s.YC = dt("YC", [c.DCONV, c.NT], BF16, kind="Internal").ap()

        with ExitStack() as st:
            s.st = st
            s.S = Sched(nc, st)
            s.ps = [st.enter_context(nc.psum_tensor("ps%d" % i, [128, 512], F32)) for i in range(8)]
            T_ = s.mkT(st)
            s.ones = T_("ones", [128, 128])
            s.cv32 = T_("cv32", [128, c.KD * 2])
            s.scb = T_("scb", [128, c.KD, 2], BF16)
            s.smallt = T_("smallt", [128, c.NS])
            s.modv = T_("modv", [128, 9 * c.KD, 2])
            s.bgv = T_("bgv", [128, c.KD, 2])
            s.clam = T_("clam", [128, 2 * c.LK])
            s.clam2 = T_("clam2", [128, 2 * c.LK])
            s.ctmp = [T_("ctmp%d" % i, [128, 2 * c.LK]) for i in range(4)]
            S = s.S
            s.ident = T_("ident", [128, 128])
            S.add("sp", lambda e: e.dma_start(out=s.ident[:], in_=s.identd), w=[("ident",)], dma=("ident",))
            S.add("dve", lambda e: e.memset(s.ones[:], 1.0), w=[("ones",)])
            S.add("sp", lambda e: e.dma_start(out=s.cv32[:], in_=s.cvec), w=[("cv32",)], dma=("cv32",))
            S.add("act", lambda e: e.activation(out=s.scb[:].rearrange("p k c -> p (k c)"), in_=s.cv32[:], func=AF.Silu),
                  r=[("cv32",)], w=[("scb",)])
            stop = getattr(c, "STOP", None)
            done = False
            for l in range(L):
                last = l == L - 1
                src_ = s.xin if l == 0 else s.SC
                steps = [("ada", lambda: s.phase_ada(l), None),
                         ("ffn0", lambda: s.phase_ffn(l, 0, src_, s.SA, final=(stop == (l, "ffn0"))), s.SA),
                         ("mixa", lambda: s.phase_mixa(l, s.SA), None),
                         ("scan", lambda: s.phase_scan(l, last), None),
                         ("mixc", lambda: s.phase_mixc(l, s.SA, s.SB), s.SB),
                         ("ffn1", lambda: s.phase_ffn(l, 1, s.SB, s.yout if last else s.SC, final=last), s.SC)]
                for name, fn, outstream in steps:
                    fn()
                    if stop == (l, name):
                        s.dbg_copy(outstream)
                        done = True
                        break
                if done:
                    break
        return nc

    def dbg_copy(s, stream):
        S = s.S
        c = s.c
        for j in range(c.KD):
            S.add("sp", lambda e, j=j: e.dma_start(out=s.yout[j * 128:(j + 1) * 128, :], in_=stream[j * 128:(j + 1) * 128, :]),
                  dma=("dbg", j))
        S.flush(final=True)

    def mkT(s, st):
        def T_(name, shape, dtp=F32):
            s.uid = getattr(s, "uid", 0) + 1
            return st.enter_context(s.nc.sbuf_tensor("%s_%d" % (name, s.uid), shape, dtp))
        return T_

    def sm(s, name, idx, n=1):
        o = s.c.so[name] + idx
        return s.smallt[:, o:o + n]

    def segs(s, blk):
        c = s.c
        lo, hi = blk * c.TB, (blk + 1) * c.TB
        out = []
        if lo < c.T:
            e_ = min(hi, c.T) - lo
            if getattr(c, "SPLIT", False) and e_ == c.TB:
                out.append((0, 512, 0))
                out.append((512, e_, 0))
            else:
                out.append((0, e_, 0))
        if hi > c.T:
            out.append((max(lo, c.T) - lo, c.TB, 1))
        return out

    def phase_ada(s, l):
        c, S, nc = s.c, s.S, s.nc
        NJ = 9 * c.KD
        with nc.sbuf_tensor("wa_%d" % l, [128, 6, c.D], BF16) as wa:
            S.add("sp", lambda e: e.dma_start(out=s.smallt[:], in_=s.small[l]), w=[("small",)], dma=("small",))
            for j in range(NJ):
                sl = j % 6
                S.add("pool", lambda e, j=j, sl=sl: e.dma_start(out=wa[:, sl, :], in_=s.wada[l, j], max_dma_last_dim=8192),
                      w=[("wa", sl)], dma=("wa", sl))
                for k in range(c.KD):
                    S.add("pe", lambda e, j=j, k=k, sl=sl: e.matmul(s.ps[0][:, 2 * j:2 * j + 2], wa[:, sl, k * 128:(k + 1) * 128],
                                                                  s.scb[:, k, :], start=(k == 0), stop=(k == c.KD - 1)),
                          r=[("wa", sl), ("scb",)], w=[("ps", 0)])
            psv = s.ps[0][:, 0:2 * NJ].rearrange("p (j c) -> p j c", c=2)
            for col in range(2):
                S.add("dve", lambda e, col=col: e.tensor_tensor(out=s.modv[:, :, col], in0=psv[:, :, col],
                                                                in1=s.sm("bada", 0, NJ), op=ALU.add),
                      r=[("ps", 0), ("small",)], w=[("modv",)])
            for sub in range(3):
                coef = (1.0 if sub == 1 else 0.5) / c.ALPHA
                S.add("dve", lambda e, sub=sub: e.tensor_scalar(s.modv[:, (3 * sub + 1) * c.KD:(3 * sub + 2) * c.KD, :],
                                                                s.modv[:, (3 * sub + 1) * c.KD:(3 * sub + 2) * c.KD, :],
                                                                1.0, None, ALU.add), r=[("modv",)], w=[("modv",)])
                S.add("dve", lambda e, sub=sub, coef=coef: e.tensor_scalar(s.modv[:, (3 * sub + 2) * c.KD:(3 * sub + 3) * c.KD, :],
                                                                           s.modv[:, (3 * sub + 2) * c.KD:(3 * sub + 3) * c.KD, :],
                                                                           coef, None, ALU.mult), r=[("modv",)], w=[("modv",)])
            for col in range(2):
                S.add("dve", lambda e, col=col: e.tensor_tensor(out=s.bgv[:, :, col], in0=s.modv[:, 5 * c.KD:6 * c.KD, col],
                                                                in1=s.sm("bout", 0, c.KD), op=ALU.mult),
                      r=[("modv",), ("small",)], w=[("bgv",)])
            t0, t1, t2, t3 = s.ctmp
            lam = s.sm("lam", 0, 2 * c.LK)
            K = ("clamk",)
            S.add("act", lambda e: e.activation(out=t0[:], in_=lam, func=AF.Exp, scale=-1.0), r=[("small",)], w=[K])
            S.add("act", lambda e: e.activation(out=t1[:], in_=t0[:], func=AF.Ln, bias=1.0), r=[K], w=[K])
            S.add("dve", lambda e: e.tensor_scalar(t2[:], t0[:], 0.05, None, ALU.min), r=[K], w=[K])
            S.add("dve", lambda e: e.tensor_scalar(t3[:], t2[:], 0.2, -0.25, ALU.mult, ALU.add), r=[K], w=[K])
            for cst in (1.0 / 3.0, -0.5, 1.0):
                S.add("dve", lambda e: e.tensor_tensor(out=t3[:], in0=t3[:], in1=t2[:], op=ALU.mult), r=[K], w=[K])
                S.add("dve", lambda e, cst=cst: e.tensor_scalar(t3[:], t3[:], cst, None, ALU.add), r=[K], w=[K])
            S.add("dve", lambda e: e.tensor_tensor(out=t3[:], in0=t3[:], in1=t2[:], op=ALU.mult), r=[K], w=[K])
            S.add("dve", lambda e: e.tensor_scalar(t2[:], t0[:], 0.05, None, ALU.is_lt), r=[K], w=[K])
            S.add("dve", lambda e: e.tensor_tensor(out=t3[:], in0=t3[:], in1=t1[:], op=ALU.subtract), r=[K], w=[K])
            S.add("dve", lambda e: e.tensor_tensor(out=t3[:], in0=t3[:], in1=t2[:], op=ALU.mult), r=[K], w=[K])
            S.add("dve", lambda e: e.tensor_tensor(out=t3[:], in0=t3[:], in1=t1[:], op=ALU.add), r=[K], w=[K])
            S.add("dve", lambda e: e.tensor_scalar(s.clam[:], t3[:], -RG_C, None, ALU.mult), r=[K], w=[("clam",)])
            S.add("dve", lambda e: e.tensor_scalar(s.clam2[:], t3[:], -2.0 * RG_C, None, ALU.mult), r=[K], w=[("clam",)])
            S.flush()

    def modulate(s, src, blk, sub, xmod, xs, nslot, cnt):
        c, S = s.c, s.S
        for k in range(c.KD):
            sl = cnt[0] % nslot
            cnt[0] += 1
            S.add("sp", lambda e, k=k, sl=sl: e.dma_start(out=xs[:, sl, :], in_=src[k * 128:(k + 1) * 128, blk * c.TB:(blk + 1) * c.TB]),
                  w=[("xs", sl)], dma=("xs", sl))
            for (lo, hi, which) in s.segs(blk):
                sc1 = s.modv[:, (3 * sub + 1) * c.KD + k, which:which + 1]
                sh = s.modv[:, (3 * sub) * c.KD + k, which:which + 1]
                if k % 2 == 0:
                    S.add("act", lambda e, k=k, sl=sl, lo=lo, hi=hi, sc1=sc1, sh=sh: e.activation(
                        out=xmod[:, k, lo:hi], in_=xs[:, sl, lo:hi], func=AF.Identity, scale=sc1, bias=sh),
                        r=[("xs", sl), ("modv",)], w=[("R1", k)])
                else:
                    S.add("dve", lambda e, k=k, sl=sl, lo=lo, hi=hi, sc1=sc1, sh=sh: e.tensor_scalar(
                        xmod[:, k, lo:hi], xs[:, sl, lo:hi], sc1, sh, ALU.mult, ALU.add),
                        r=[("xs", sl), ("modv",)], w=[("R1", k)])

    def ln_tail(s, u32, nch, Dn, eps, lnt, emit_out, ukeys=None):
        c, S = s.c, s.S
        (mean, kmean), (rstd, krstd), (nmr, knmr), (tmp, ktmp) = lnt
        if ukeys is None:
            ukeys = lambda j: [("u", j)]
        H = c.HALF
        inv = 1.0 / Dn
        for h in range(2):
            cs = slice(h * H, (h + 1) * H)
            S.add("act", lambda e, h=h, cs=cs: e.activation(out=mean[:, cs], in_=s.ps[4 + h][:, 0:H], func=AF.Identity, scale=inv),
                  r=[("ps", 4 + h)], w=[kmean])
            S.add("dve", lambda e, cs=cs: e.tensor_tensor(out=tmp[:, cs], in0=mean[:, cs], in1=mean[:, cs], op=ALU.mult),
                  r=[kmean], w=[ktmp])
            S.add("dve", lambda e, h=h, cs=cs: e.scalar_tensor_tensor(out=tmp[:, cs], in0=s.ps[6 + h][:, 0:H], scalar=inv, in1=tmp[:, cs],
                                                                      op0=ALU.mult, op1=ALU.subtract),
                  r=[("ps", 6 + h), ktmp], w=[ktmp])
            S.add("act", lambda e, cs=cs: e.activation(out=tmp[:, cs], in_=tmp[:, cs], func=AF.Sqrt, bias=s.epst[eps][:, 0:1]),
                  r=[ktmp, ("eps", eps)], w=[ktmp])
            S.add("dve", lambda e, cs=cs: e.reciprocal(rstd[:, cs], tmp[:, cs]), r=[ktmp], w=[krstd])
            S.add("dve", lambda e, cs=cs: e.scalar_tensor_tensor(out=nmr[:, cs], in0=mean[:, cs], scalar=-1.0, in1=rstd[:, cs],
                                                                 op0=ALU.mult, op1=ALU.mult),
                  r=[kmean, krstd], w=[knmr])
        for j in range(nch):
            S.add("dve", lambda e, j=j: e.tensor_tensor(out=u32[:, j, :], in0=u32[:, j, :], in1=rstd[:, :], op=ALU.mult),
                  r=ukeys(j) + [krstd], w=ukeys(j))
            S.add("dve", lambda e, j=j: e.tensor_tensor(out=u32[:, j, :], in0=u32[:, j, :], in1=nmr[:, :], op=ALU.add),
                  r=ukeys(j) + [knmr], w=ukeys(j))
            emit_out(j)

    def stats_mm(s, j, nch, src_u, src_sq, ukey, sqkey):
        c, S = s.c, s.S
        H = c.HALF
        for h in range(2):
            S.add("pe", lambda e, h=h: e.matmul(s.ps[4 + h][:, 0:H], s.ones[:], src_u[:, h * H:(h + 1) * H], start=(j == 0), stop=(j == nch - 1)),
                  r=[ukey, ("ones",)], w=[("ps", 4 + h)])
            S.add("pe", lambda e, h=h: e.matmul(s.ps[6 + h][:, 0:H], s.ones[:], src_sq[:, h * H:(h + 1) * H], start=(j == 0), stop=(j == nch - 1)),
                  r=[sqkey, ("ones",)], w=[("ps", 6 + h)])

    def u_keys(s, j):
        return [("R1", 2 * j), ("R1", 2 * j + 1), ("u", j)]

    def eps_tiles(s, st):
        c = s.c
        s.epst = {}
        for name, val in (("ffn", EPS / (c.ALPHA ** 2)), ("conv", EPS)):
            t = s.mkT(st)("eps_" + name, [128, 1], F32)
            s.S.add("dve", lambda e, t=t, val=val: e.memset(t[:], val), w=[("eps", name)])
            s.epst[name] = t

    def phase_ffn(s, l, which_ffn, src, dst, final=False):
        from contextlib import ExitStack
        c, S, nc = s.c, s.S, s.nc
        sub = 0 if which_ffn == 0 else 2
        H, TB, KD, FK = c.HALF, c.TB, c.KD, c.FK
        NW1 = 3
        U = s.Uscr
        with ExitStack() as st:
            T_ = s.mkT(st)
            xm = T_("xm", [128, 2, KD, TB], BF16)
            act = T_("act", [128, FK, TB], BF16)
            w1t = T_("w1t", [128, NW1, 2, c.D], BF16)
            w2t = T_("w2t", [128, 2, c.FF], BF16)
            xs = T_("xs", [128, 2, TB])
            uo = T_("uo", [128, 2, TB])
            sq = T_("sq", [128, 2, TB])
            nin = T_("nin", [128, 2, TB])
            sgt = T_("sgt", [128, 2, H])
            lr = T_("lr", [128, 2, TB])
            lnn = T_("lnn", [128, 2, TB])
            s.eps_tiles(st)
            cn = {"x": 0, "w1": 0, "w2": 0, "sg": 0, "n": 0}

            def modulate(blk):
                par = blk % 2
                for k in range(KD):
                    sl = cn["x"] % 2
                    cn["x"] += 1
                    S.add("sp", lambda e, k=k, sl=sl: e.dma_start(out=xs[:, sl, :], in_=src[k * 128:(k + 1) * 128, blk * TB:(blk + 1) * TB]),
                          w=[("xs", sl)], dma=("xs", sl))
                    for (lo, hi, which) in s.segs(blk):
                        sc1 = s.modv[:, (3 * sub + 1) * KD + k, which:which + 1]
                        sh = s.modv[:, (3 * sub) * KD + k, which:which + 1]
                        if k % 2 == 0:
                            S.add("act", lambda e, k=k, sl=sl, lo=lo, hi=hi, sc1=sc1, sh=sh: e.activation(
                                out=xm[:, par, k, lo:hi], in_=xs[:, sl, lo:hi], func=AF.Identity, scale=sc1, bias=sh),
                                r=[("xs", sl), ("modv",)], w=[("xm", par, k)])
                        else:
                            S.add("dve", lambda e, k=k, sl=sl, lo=lo, hi=hi, sc1=sc1, sh=sh: e.tensor_scalar(
                                xm[:, par, k, lo:hi], xs[:, sl, lo:hi], sc1, sh, ALU.mult, ALU.add),
                                r=[("xs", sl), ("modv",)], w=[("xm", par, k)])

            def norm_chunk(blk, j):
                par = blk % 2
                o = cn["n"] % 2
                cn["n"] += 1
                S.add("sp", lambda e: e.dma_start(out=nin[:, o, :], in_=U[par, j * 128:(j + 1) * 128, :]),
                      r=[("U", par, j)], w=[("nin", o)], dma=("nin", o))
                S.add("dve", lambda e: e.tensor_tensor(out=nin[:, o, :], in0=nin[:, o, :], in1=lr[:, par, :], op=ALU.mult),
                      r=[("nin", o), ("lr", par)], w=[("nin", o)])
                S.add("dve", lambda e: e.tensor_tensor(out=nin[:, o, :], in0=nin[:, o, :], in1=lnn[:, par, :], op=ALU.add),
                      r=[("nin", o), ("lnn", par)], w=[("nin", o)])
                S.add("act", lambda e: e.activation(out=nin[:, o, :], in_=nin[:, o, :], func=AF.Identity,
                                                    scale=s.sm("lng", sub * KD + j), bias=s.sm("lnb", sub * KD + j)),
                      r=[("nin", o), ("small",)], w=[("nin", o)])
                S.add(STQ, lambda e: e.dma_start(out=dst[j * 128:(j + 1) * 128, blk * TB:(blk + 1) * TB], in_=nin[:, o, :]),
                      r=[("nin", o)], dma=("nin", "st", o))

            def h_phase(blk):
                par = blk % 2
                for f in range(FK):
                    sl = cn["w1"] % NW1
                    cn["w1"] += 1
                    for gu in range(2):
                        S.add("pool", lambda e, f=f, sl=sl, gu=gu: e.dma_start(out=w1t[:, sl, gu, :], in_=s.w1[l, which_ffn, gu * FK + f],
                                                                              max_dma_last_dim=8192),
                              w=[("w1", sl, gu)], dma=("w1", sl, gu))
                    bs = 4 * (f % 2)
                    for gu in range(2):
                        for k in range(KD):
                            for h in range(2):
                                S.add("pe", lambda e, sl=sl, gu=gu, k=k, h=h, bs=bs: e.matmul(
                                    s.ps[bs + 2 * gu + h][:, 0:H], w1t[:, sl, gu, k * 128:(k + 1) * 128], xm[:, par, k, h * H:(h + 1) * H],
                                    start=(k == 0), stop=(k == KD - 1)),
                                    r=[("w1", sl, gu), ("xm", par, k)], w=[("ps", bs + 2 * gu + h)])
                    for h in range(2):
                        g = cn["sg"] % 2
                        cn["sg"] += 1
                        S.add("act", lambda e, g=g, h=h, bs=bs: e.activation(out=sgt[:, g, :], in_=s.ps[bs + h][:, 0:H], func=AF.Silu),
                              r=[("ps", bs + h)], w=[("sg", g)])
                        S.add("dve", lambda e, g=g, h=h, bs=bs, f=f: e.tensor_tensor(out=act[:, f, h * H:(h + 1) * H], in0=s.ps[bs + 2 + h][:, 0:H],
                                                                                     in1=sgt[:, g, :], op=ALU.mult),
                              r=[("ps", bs + 2 + h), ("sg", g)], w=[("act", f)])
                    if blk > 0 and f < KD:
                        norm_chunk(blk - 1, f)

            def y_phase(blk):
                par = blk % 2
                segs = s.segs(blk)
                pending = None
                for j in range(KD):
                    sl = cn["w2"] % 2
                    cn["w2"] += 1
                    S.add("pool", lambda e, j=j, sl=sl: e.dma_start(out=w2t[:, sl, :], in_=s.w2[l, which_ffn, j], max_dma_last_dim=8192),
                          w=[("w2", sl)], dma=("w2", sl))
                    xl = cn["x"] % 2
                    cn["x"] += 1
                    S.add("sp", lambda e, j=j, xl=xl: e.dma_start(out=xs[:, xl, :], in_=src[j * 128:(j + 1) * 128, blk * TB:(blk + 1) * TB]),
                          w=[("xs", xl)], dma=("xs", xl))
                    bs = 2 * (j % 2)
                    for f in range(FK):
                        for h in range(2):
                            S.add("pe", lambda e, sl=sl, f=f, h=h, bs=bs: e.matmul(
                                s.ps[bs + h][:, 0:H], w2t[:, sl, f * 128:(f + 1) * 128], act[:, f, h * H:(h + 1) * H],
                                start=(f == 0), stop=(f == FK - 1)),
                                r=[("w2", sl), ("act", f)], w=[("ps", bs + h)])
                    q = j % 2
                    for (lo, hi, which) in segs:
                        gcol = s.modv[:, (3 * sub + 2) * KD + j, which:which + 1]
                        for h in range(2):
                            a0, a1 = max(lo, h * H), min(hi, (h + 1) * H)
                            if a0 >= a1:
                                continue
                            S.add("dve", lambda e, h=h, a0=a0, a1=a1, gcol=gcol, xl=xl, bs=bs, q=q: e.scalar_tensor_tensor(
                                out=uo[:, q, a0:a1], in0=s.ps[bs + h][:, a0 - h * H:a1 - h * H], scalar=gcol, in1=xs[:, xl, a0:a1],
                                op0=ALU.mult, op1=ALU.add),
                                r=[("ps", bs + h), ("xs", xl), ("modv",)], w=[("uo", q)])
                    S.add("act", lambda e, q=q: e.activation(out=sq[:, q, :], in_=uo[:, q, :], func=AF.Square),
                          r=[("uo", q)], w=[("sq", q)])
                    S.add("sp", lambda e, j=j, q=q: e.dma_start(out=U[par, j * 128:(j + 1) * 128, :], in_=uo[:, q, :]),
                          r=[("uo", q)], w=[("U", par, j)], dma=("uo", q))
                    if j == 0:
                        S.add("dve", lambda e, q=q: e.tensor_copy(lr[:, par, :], uo[:, q, :]), r=[("uo", q)], w=[("lr", par)])
                        S.add("dve", lambda e, q=q: e.tensor_copy(lnn[:, par, :], sq[:, q, :]), r=[("sq", q)], w=[("lnn", par)])
                    else:
                        S.add("dve", lambda e, q=q: e.tensor_tensor(out=lr[:, par, :], in0=lr[:, par, :], in1=uo[:, q, :], op=ALU.add),
                              r=[("uo", q), ("lr", par)], w=[("lr", par)])
                        S.add("dve", lambda e, q=q: e.tensor_tensor(out=lnn[:, par, :], in0=lnn[:, par, :], in1=sq[:, q, :], op=ALU.add),
                              r=[("sq", q), ("lnn", par)], w=[("lnn", par)])
                s.stats_mm(0, 1, lr[:, par, :], lnn[:, par, :], ("lr", par), ("lnn", par))
                lnt = [(sq[:, 1, :], ("sq", 1)), (lr[:, par, :], ("lr", par)), (lnn[:, par, :], ("lnn", par)), (sq[:, 0, :], ("sq", 0))]
                s.ln_tail(None, 0, c.D, "ffn", lnt, None)

            modulate(0)
            for blk in range(c.NB):
                h_phase(blk)
                if blk + 1 < c.NB:
                    modulate(blk + 1)
                y_phase(blk)
            for j in range(KD):
                norm_chunk(c.NB - 1, j)
            S.flush(final=final)

    def phase_mixa(s, l, src):
        from contextlib import ExitStack
        c, S, nc = s.c, s.S, s.nc
        H, TB, KD, LK, CK = c.HALF, c.TB, c.KD, c.LK, c.CK
        NR = TB // 64
        PW = 64 + 30
        PL = NR * PW
        with ExitStack() as st:
            T_ = s.mkT(st)
            xmod = T_("hmod", [128, KD, TB], BF16)
            xs = T_("xs", [128, 3, TB])
            wt = T_("wint", [128, 4, c.D], BF16)
            xo = T_("xo", [128, 3, TB])
            sig = T_("sig", [128, 2, TB])
            upad = T_("upad", [128, CK, PL], BF16)
            dg = T_("dg", [128, CK, c.K31, 128], BF16)
            acc = T_("acc", [128, CK, TB])
            sq = T_("sq", [128, 2, TB])
            yo = T_("yo", [128, 2, TB], BF16)
            lnt = [(T_("lnm", [128, TB]), ("ln", "mean")), (T_("lnr", [128, TB]), ("ln", "rstd")),
                   (T_("lnn", [128, TB]), ("ln", "nmr")), (T_("lnt", [128, TB]), ("ln", "tmp"))]
            s.eps_tiles(st)
            xcnt = [0]
            wcnt = 0
            xocnt = 0
            for cc in range(CK):
                for kk in range(c.K31):
                    wk = s.sm("c31w", cc * 31 + kk)
                    if (cc * 31 + kk) % 2 == 0:
                        S.add("dve", lambda e, cc=cc, kk=kk, wk=wk: e.tensor_scalar(dg[:, cc, kk, :], s.ident[:], wk, None, ALU.mult),
                              r=[("ident",), ("small",)], w=[("dg", cc)])
                    else:
                        S.add("act", lambda e, cc=cc, kk=kk, wk=wk: e.activation(out=dg[:, cc, kk, :], in_=s.ident[:], func=AF.Identity, scale=wk),
                              r=[("ident",), ("small",)], w=[("dg", cc)])

            def body(blk):
                nonlocal wcnt, xocnt
                segs = s.segs(blk)
                cols = slice(blk * TB, (blk + 1) * TB)
                has_ctx = any(w == 1 for (_, _, w) in segs)
                nlat = segs[0][1] if segs[0][2] == 0 else 0
                nrl = nlat // 64
                s.modulate(src, blk, 1, xmod, xs, 3, xcnt)
                if blk == 0 or has_ctx:
                    S.add("dve", lambda e: e.memset(upad[:], 0.0), w=[("upad", cc) for cc in range(CK)])
                order = list(range(2 * LK))
                for cc in range(CK):
                    order += [2 * LK + CK + cc, 2 * LK + cc]
                for oi, o in enumerate(order):
                    sl = wcnt % 4
                    bs = 2 * (wcnt % 2)
                    wcnt += 1
                    S.add("pool", lambda e, o=o, sl=sl: e.dma_start(out=wt[:, sl, :], in_=s.win[l, o], max_dma_last_dim=8192),
                          w=[("win", sl)], dma=("win", sl))
                    for k in range(KD):
                        for h in range(2):
                            S.add("pe", lambda e, sl=sl, k=k, h=h, bs=bs: e.matmul(
                                s.ps[bs + h][:, 0:H], wt[:, sl, k * 128:(k + 1) * 128], xmod[:, k, h * H:(h + 1) * H],
                                start=(k == 0), stop=(k == KD - 1)),
                                r=[("win", sl), ("R1", k)], w=[("ps", bs + h)])
                    if o < 2 * LK:
                        xl = xocnt % 3
                        xocnt += 1
                        isg = o >= LK
                        for h in range(2):
                            if isg:
                                S.add("act", lambda e, xl=xl, h=h, bs=bs: e.activation(out=xo[:, xl, h * H:(h + 1) * H], in_=s.ps[bs + h][:, 0:H], func=AF.Gelu),
                                      r=[("ps", bs + h)], w=[("xo", xl)])
                            else:
                                S.add("act", lambda e, xl=xl, h=h, bs=bs: e.activation(out=xo[:, xl, h * H:(h + 1) * H], in_=s.ps[bs + h][:, 0:H], func=AF.Identity),
                                      r=[("ps", bs + h)], w=[("xo", xl)])
                        dstt = s.GG if isg else s.XR
                        ch = o - LK if isg else o
                        S.add(STQ, lambda e, xl=xl, dstt=dstt, ch=ch: e.dma_start(out=dstt[ch * 128:(ch + 1) * 128, cols], in_=xo[:, xl, :]),
                              r=[("xo", xl)], dma=("xo", "st", xl))
                    elif o >= 2 * LK + CK:
                        cc = o - 2 * LK - CK
                        sgl = cc % 2
                        for h in range(2):
                            S.add("act", lambda e, sgl=sgl, h=h, bs=bs: e.activation(out=sig[:, sgl, h * H:(h + 1) * H], in_=s.ps[bs + h][:, 0:H], func=AF.Sigmoid),
                                  r=[("ps", bs + h)], w=[("sig", sgl)])
                    else:
                        cc = o - 2 * LK
                        sgl = cc % 2
                        for h in range(2):
                            r0, r1 = h * (H // 64), min((h + 1) * (H // 64), nrl)
                            if r1 > r0:
                                n = (r1 - r0) * 64
                                outv = upad[:, cc, r0 * PW:r1 * PW].rearrange("p (r w) -> p r w", w=PW)[:, :, 15:79]
                                S.add("dve", lambda e, outv=outv, n=n, h=h, bs=bs, sgl=sgl: e.tensor_tensor(
                                    out=outv, in0=s.ps[bs + h][:, 0:n].rearrange("p (r w) -> p r w", w=64),
                                    in1=sig[:, sgl, h * H:h * H + n].rearrange("p (r w) -> p r w", w=64), op=ALU.mult),
                                    r=[("ps", bs + h), ("sig", sgl)], w=[("upad", cc)])
                            if has_ctx and h == 1:
                                a0 = nlat - H
                                S.add("dve", lambda e, a0=a0, bs=bs, sgl=sgl, cc=cc: e.tensor_tensor(
                                    out=upad[:, cc, nrl * PW + 15:nrl * PW + 15 + c.TC], in0=s.ps[bs + 1][:, a0:a0 + c.TC],
                                    in1=sig[:, sgl, H + a0:H + a0 + c.TC], op=ALU.mult),
                                    r=[("ps", bs + 1), ("sig", sgl)], w=[("upad", cc)])
                        cb = 2 * (wcnt % 2)
                        wcnt += 1
                        R6 = H // 64
                        for h in range(2):
                            r0, r1 = h * R6, min((h + 1) * R6, nrl)
                            if r1 > r0:
                                n = (r1 - r0) * 64
                                for kk in range(c.K31):
                                    rv = upad[:, cc, r0 * PW:r1 * PW].rearrange("p (r w) -> p r w", w=PW)[:, :, kk:kk + 64]
                                    S.add("pe", lambda e, cc=cc, kk=kk, h=h, n=n, rv=rv, cb=cb: e.matmul(
                                        s.ps[cb + h][:, 0:n].rearrange("p (r w) -> p r w", w=64), dg[:, cc, kk, :], rv,
                                        start=(kk == 0), stop=(kk == c.K31 - 1)),
                                        r=[("dg", cc), ("upad", cc)], w=[("ps", cb + h)])
                            if has_ctx and h == 1:
                                a0 = nlat - H
                                for kk in range(c.K31):
                                    S.add("pe", lambda e, cc=cc, kk=kk, a0=a0, cb=cb: e.matmul(
                                        s.ps[cb + 1][:, a0:a0 + c.TC], dg[:, cc, kk, :], upad[:, cc, nrl * PW + kk:nrl * PW + kk + c.TC],
                                        start=(kk == 0), stop=(kk == c.K31 - 1)),
                                        r=[("dg", cc), ("upad", cc)], w=[("ps", cb + 1)])
                        q = cc % 2
                        for h in range(2):
                            S.add("act", lambda e, cc=cc, h=h, cb=cb: e.activation(out=acc[:, cc, h * H:(h + 1) * H], in_=s.ps[cb + h][:, 0:H], func=AF.Identity,
                                                                                   bias=s.sm("c31b", cc)),
                                  r=[("ps", cb + h), ("small",)], w=[("acc", cc)])
                            S.add("act", lambda e, cc=cc, h=h, cb=cb, q=q: e.activation(out=sq[:, q, h * H:(h + 1) * H], in_=s.ps[cb + h][:, 0:H], func=AF.Square,
                                                                                        bias=s.sm("c31b", cc)),
                                  r=[("ps", cb + h), ("small",)], w=[("sq", q)])
                        s.stats_mm(cc, CK, acc[:, cc, :], sq[:, q, :], ("acc", cc), ("sq", q))
                ocnt = [0]

                def emit_out(j):
                    o = ocnt[0] % 2
                    ocnt[0] += 1
                    S.add("act", lambda e, j=j, o=o: e.activation(out=yo[:, o, :], in_=acc[:, j, :], func=AF.Silu,
                                                                  scale=s.sm("clng", j), bias=s.sm("clnb", j)),
                          r=[("acc", j), ("small",)], w=[("yo", o)])
                    S.add(STQ, lambda e, j=j, o=o: e.dma_start(out=s.YC[j * 128:(j + 1) * 128, cols], in_=yo[:, o, :]),
                          r=[("yo", o)], dma=("yo", "st", o))
                s.ln_tail(acc, CK, c.DCONV, "conv", lnt, emit_out, ukeys=lambda j: [("acc", j)])
            for blk_ in range(c.NB):
                body(blk_)
            S.flush()

    def phase_scan(s, l, last):
        from contextlib import ExitStack
        c, S, nc = s.c, s.S, s.nc
        LK, TS, NH, HC, HD = c.LK, c.TS, c.NH, c.HC, c.HD
        with ExitStack() as st:
            T_ = s.mkT(st)
            gw = T_("gw", [128, 2 * 2 * NH * HC * HD], BF16)
            xrh = T_("xrh", [128, 2, LK, TS + 3])
            xc = T_("xc", [128, 2, LK, TS])
            xcb = T_("xcb", [128, 2, LK, TS], BF16)
            Rt = T_("Rt", [128, LK, TS])
            It = T_("It", [128, LK, TS])
            At = T_("At", [128, LK, TS])
            St = T_("St", [128, LK, TS])
            hh = T_("hh", [128, 3, TS])
            rf = T_("rf", [128, LK, TS])
            gg = T_("gg", [128, LK, TS])
            yr = T_("yr", [128, 2, TS], BF16)
            state = T_("state", [128, LK])
            S.add("pool", lambda e: e.dma_start(out=gw[:], in_=s.gwt[l], max_dma_last_dim=8192), w=[("gw",)], dma=("gw",))

            def gwv(d, g, hd, jc, ic):
                base = (((d * 2 + g) * NH + hd) * HC + jc) * HD + ic * 128
                return gw[:, base:base + 128]
            XRv = s.XR.rearrange("(c p) t -> p c t", p=128)
            RECv = s.REC.rearrange("(c p) t -> p c t", p=128)
            GGv = s.GG.rearrange("(c p) t -> p c t", p=128)
            cnt = {"x": 0, "h": 0, "rf": 0, "yr": 0, "cb": 0}
            seq_ctx = (c.T, c.TC)
            seq_lat = (0, c.T)

            def partA(i, d, off, ln, is_ctx, b):
                sl = i % 2
                lo = b * TS
                n = min(TS, ln - lo)
                xl = cnt["x"] % 2
                cnt["x"] += 1
                a0 = max(lo - 2, 0)
                a1 = min(lo + n + 1, ln)
                d0 = a0 - (lo - 2)
                if d0 > 0:
                    S.add("dve", lambda e: e.memset(xrh[:, xl, :, 0:d0], 0.0), w=[("xrh", xl)])
                if a1 < lo + n + 1:
                    S.add("dve", lambda e: e.memset(xrh[:, xl, :, n + 2:n + 3], 0.0), w=[("xrh", xl)])
                S.add("sp", lambda e: e.dma_start(out=xrh[:, xl, :, d0:d0 + (a1 - a0)], in_=XRv[:, :, off + a0:off + a1]),
                      w=[("xrh", xl)], dma=("xrh", xl))
                for ch in range(LK):
                    S.add("act", lambda e, ch=ch: e.activation(out=xc[:, sl, ch, 0:n], in_=xrh[:, xl, ch, 0:n], func=AF.Identity,
                                                               scale=s.sm("c4w", ch * 4 + 0), bias=s.sm("c4b", ch)),
                          r=[("xrh", xl), ("small",)], w=[("xc", sl, ch)])
                    for kk in range(1, 4):
                        S.add("dve", lambda e, ch=ch, kk=kk: e.scalar_tensor_tensor(
                            out=xc[:, sl, ch, 0:n], in0=xrh[:, xl, ch, kk:kk + n], scalar=s.sm("c4w", ch * 4 + kk), in1=xc[:, sl, ch, 0:n],
                            op0=ALU.mult, op1=ALU.add), r=[("xrh", xl), ("small",)], w=[("xc", sl, ch)])
                    S.add("pool", lambda e, ch=ch: e.tensor_copy(xcb[:, sl, ch, 0:n], xc[:, sl, ch, 0:n]),
                          r=[("xc", sl, ch)], w=[("xcb", sl, ch)])

            def partB(i, d, off, ln, is_ctx, b):
                sl = i % 2
                lo = b * TS
                n = min(TS, ln - lo)
                if d == 1 and not (is_ctx and last):
                    S.add("sp", lambda e: e.dma_start(out=rf[:, :, 0:n], in_=RECv[:, :, off + lo:off + lo + n]), w=[("rf", ch) for ch in range(LK)], dma=("rf",))
                    S.add("sp", lambda e: e.dma_start(out=gg[:, :, 0:n], in_=GGv[:, :, off + lo:off + lo + n]), w=[("gg", ch) for ch in range(LK)], dma=("gg",))

                for hf in range(2 if LK >= 4 else 1):
                    chs = range(hf * (LK // 2), (hf + 1) * (LK // 2)) if LK >= 4 else range(LK)
                    for g, dstt, bname in ((0, Rt, "brg"), (1, It, "big")):
                        for ch in chs:
                            hd, jc = ch // HC, ch % HC
                            bank = ch % 8
                            for ic in range(HC):
                                S.add("pe", lambda e, g=g, hd=hd, jc=jc, ic=ic, bank=bank: e.matmul(
                                    s.ps[bank][:, 0:n], gwv(d, g, hd, jc, ic), xcb[:, sl, hd * HC + ic, 0:n], start=(ic == 0), stop=(ic == HC - 1)),
                                    r=[("gw",), ("xcb", sl, hd * HC + ic)], w=[("ps", bank)])
                            S.add("act", lambda e, ch=ch, bank=bank, dstt=dstt, bname=bname: e.activation(
                                out=dstt[:, ch, 0:n], in_=s.ps[bank][:, 0:n], func=AF.Sigmoid, bias=s.sm(bname, d * LK + ch)),
                                r=[("ps", bank), ("small",)], w=[("g%d" % g, ch)])
                    for ch in chs:
                        S.add("act", lambda e, ch=ch: e.activation(out=At[:, ch, 0:n], in_=Rt[:, ch, 0:n], func=AF.Exp,
                                                                   scale=s.clam[:, d * LK + ch:d * LK + ch + 1]),
                              r=[("g0", ch), ("clam",)], w=[("A", ch)])
                        S.add("act", lambda e, ch=ch: e.activation(out=St[:, ch, 0:n], in_=Rt[:, ch, 0:n], func=AF.Exp,
                                                                   scale=s.clam2[:, d * LK + ch:d * LK + ch + 1]),
                              r=[("g0", ch), ("clam",)], w=[("S", ch)])
                    for ch in chs:
                        S.add("act", lambda e, ch=ch: e.activation(out=St[:, ch, 0:n], in_=St[:, ch, 0:n], func=AF.Sqrt, scale=-1.0,
                                                                   bias=s.ones[:, 0:1]),
                              r=[("S", ch), ("ones",)], w=[("S", ch)])

                for ch in range(LK):
                    S.add("pool", lambda e, ch=ch: e.tensor_tensor(out=St[:, ch, 0:n], in0=St[:, ch, 0:n], in1=It[:, ch, 0:n], op=ALU.mult),
                          r=[("S", ch), ("g1", ch)], w=[("S", ch)])
                    S.add("pool", lambda e, ch=ch: e.tensor_tensor(out=St[:, ch, 0:n], in0=St[:, ch, 0:n], in1=xc[:, sl, ch, 0:n], op=ALU.mult),
                          r=[("S", ch), ("xc", sl, ch)], w=[("S", ch)])
                    hl = cnt["h"] % 3
                    cnt["h"] += 1
                    if d == 0:
                        S.add("dve", lambda e, ch=ch, hl=hl: e.tensor_tensor_scan(hh[:, hl, 0:n], At[:, ch, 0:n], St[:, ch, 0:n],
                                                                                 state[:, ch:ch + 1], ALU.mult, ALU.add),
                              r=[("A", ch), ("S", ch), ("state", ch)], w=[("hh", hl)])
                        S.add("dve", lambda e, ch=ch, hl=hl: e.tensor_copy(state[:, ch:ch + 1], hh[:, hl, n - 1:n]),
                              r=[("hh", hl)], w=[("state", ch)])
                        if not (is_ctx and last):
                            S.add("sp", lambda e, ch=ch, hl=hl: e.dma_start(
                                out=s.REC[ch * 128:(ch + 1) * 128, off + lo:off + lo + n], in_=hh[:, hl, 0:n]),
                                r=[("hh", hl)], dma=("hh", hl))
                    else:
                        S.add("dve", lambda e, ch=ch, hl=hl: e.tensor_tensor_scan(hh[:, hl, 0:n][:, ::-1], At[:, ch, 0:n][:, ::-1],
                                                                                 St[:, ch, 0:n][:, ::-1],
                                                                                 state[:, ch:ch + 1], ALU.mult, ALU.add),
                              r=[("A", ch), ("S", ch), ("state", ch)], w=[("hh", hl)])
                        S.add("dve", lambda e, ch=ch, hl=hl: e.tensor_copy(state[:, ch:ch + 1], hh[:, hl, 0:1]),
                              r=[("hh", hl)], w=[("state", ch)])
                        if not (is_ctx and last):
                            rl = cnt["rf"] % 2
                            cnt["rf"] += 1
                            S.add("pool", lambda e, ch=ch, hl=hl: e.tensor_tensor(out=hh[:, hl, 0:n], in0=hh[:, hl, 0:n], in1=rf[:, ch, 0:n], op=ALU.add),
                                  r=[("hh", hl), ("rf", ch)], w=[("hh", hl)])
                            S.add("dve", lambda e, ch=ch, hl=hl, rl=rl: e.tensor_tensor(out=yr[:, rl, 0:n], in0=hh[:, hl, 0:n], in1=gg[:, ch, 0:n], op=ALU.mult),
                                  r=[("hh", hl), ("gg", ch)], w=[("yr", rl)])
                            S.add("sp", lambda e, ch=ch, rl=rl: e.dma_start(
                                out=s.YR[ch * 128:(ch + 1) * 128, off + lo:off + lo + n], in_=yr[:, rl, 0:n]),
                                r=[("yr", rl)], dma=("yr", rl))

            for d in range(2):
                S.add("dve", lambda e: e.memset(state[:], 0.0), w=[("state", ch) for ch in range(LK)])
                items = []
                for (off, ln), is_ctx in ((seq_ctx, True), (seq_lat, False)):
                    nblk = (ln + TS - 1) // TS
                    blks = list(range(nblk))
                    if d == 1:
                        blks = blks[::-1]
                    for b in blks:
                        items.append((d, off, ln, is_ctx, b))
                partA(0, *items[0])
                for i, it in enumerate(items):
                    if i + 1 < len(items):
                        partA(i + 1, *items[i + 1])
                    partB(i, *it)
                S.flush()

    def phase_mixc(s, l, src, dst):
        from contextlib import ExitStack
        c, S, nc = s.c, s.S, s.nc
        H, TB, KD, MK, LK, CK = c.HALF, c.TB, c.KD, c.MK, c.LK, c.CK
        with ExitStack() as st:
            T_ = s.mkT(st)
            ym = T_("ym", [128, 2, MK, TB], BF16)
            u32 = T_("u32", [128, 2, KD, TB])
            wt = T_("woutt", [128, 3, c.DMIX], BF16)
            xs = T_("xs", [128, 3, TB])
            sq = T_("sq", [128, 2, TB])
            osl = T_("osl", [128, 2, TB])
            lr = T_("lr", [128, 2, TB])
            lnn = T_("lnn", [128, 2, TB])
            lm = T_("lm", [128, TB])
            lt = T_("lt", [128, TB])
            s.eps_tiles(st)
            YRv = s.YR.rearrange("(c p) t -> p c t", p=128)
            YCv = s.YC.rearrange("(c p) t -> p c t", p=128)
            cn = {"x": 0, "w": 0, "o": 0}

            def load_ym(blk):
                par = blk % 2
                cols = slice(blk * TB, (blk + 1) * TB)
                S.add("sp", lambda e: e.dma_start(out=ym[:, par, 0:LK, :], in_=YRv[:, :, cols]), w=[("ym", par, 0)], dma=("ym", par, 0))
                S.add("sp", lambda e: e.dma_start(out=ym[:, par, LK:MK, :], in_=YCv[:, :, cols]), w=[("ym", par, 1)], dma=("ym", par, 1))

            def tail_chunk(blk, j):
                par = blk % 2
                cols = slice(blk * TB, (blk + 1) * TB)
                o = cn["o"] % 2
                cn["o"] += 1
                S.add("dve", lambda e: e.tensor_tensor(out=u32[:, par, j, :], in0=u32[:, par, j, :], in1=lr[:, par, :], op=ALU.mult),
                      r=[("u", par, j), ("lr", par)], w=[("u", par, j)])
                S.add("dve", lambda e: e.tensor_tensor(out=u32[:, par, j, :], in0=u32[:, par, j, :], in1=lnn[:, par, :], op=ALU.add),
                      r=[("u", par, j), ("lnn", par)], w=[("u", par, j)])
                S.add("act", lambda e: e.activation(out=osl[:, o, :], in_=u32[:, par, j, :], func=AF.Identity,
                                                    scale=s.sm("lng", 1 * KD + j), bias=s.sm("lnb", 1 * KD + j)),
                      r=[("u", par, j), ("small",)], w=[("os", o)])
                S.add(STQ, lambda e: e.dma_start(out=dst[j * 128:(j + 1) * 128, cols], in_=osl[:, o, :]),
                      r=[("os", o)], dma=("os", "st", o))

            def body(blk):
                par = blk % 2
                segs = s.segs(blk)
                cols = slice(blk * TB, (blk + 1) * TB)
                if blk + 1 < c.NB:
                    load_ym(blk + 1)
                pending = None
                for j in range(KD):
                    sl = cn["w"] % 3
                    cn["w"] += 1
                    S.add("pool", lambda e, j=j, sl=sl: e.dma_start(out=wt[:, sl, :], in_=s.wout[l, j], max_dma_last_dim=8192),
                          w=[("wo", sl)], dma=("wo", sl))
                    xl = cn["x"] % 3
                    cn["x"] += 1
                    S.add("sp", lambda e, j=j, xl=xl: e.dma_start(out=xs[:, xl, :], in_=src[j * 128:(j + 1) * 128, cols]),
                          w=[("xs", xl)], dma=("xs", xl))
                    bs = 2 * (j % 2)
                    for kc in range(MK):
                        for h in range(2):
                            S.add("pe", lambda e, sl=sl, kc=kc, h=h, bs=bs: e.matmul(
                                s.ps[bs + h][:, 0:H], wt[:, sl, kc * 128:(kc + 1) * 128], ym[:, par, kc, h * H:(h + 1) * H],
                                start=(kc == 0), stop=(kc == MK - 1)),
                                r=[("wo", sl), ("ym", par, 0 if kc < LK else 1)], w=[("ps", bs + h)])
                    for (lo, hi, which) in segs:
                        S.add("act", lambda e, j=j, xl=xl, lo=lo, hi=hi, which=which: e.activation(
                            out=xs[:, xl, lo:hi], in_=xs[:, xl, lo:hi], func=AF.Identity, bias=s.bgv[:, j, which:which + 1]),
                            r=[("xs", xl), ("bgv",)], w=[("xs", xl)])
                        gcol = s.modv[:, 5 * KD + j, which:which + 1]
                        for h in range(2):
                            a0, a1 = max(lo, h * H), min(hi, (h + 1) * H)
                            if a0 >= a1:
                                continue
                            S.add("dve", lambda e, j=j, h=h, a0=a0, a1=a1, gcol=gcol, xl=xl, bs=bs: e.scalar_tensor_tensor(
                                out=u32[:, par, j, a0:a1], in0=s.ps[bs + h][:, a0 - h * H:a1 - h * H], scalar=gcol, in1=xs[:, xl, a0:a1],
                                op0=ALU.mult, op1=ALU.add),
                                r=[("ps", bs + h), ("xs", xl), ("modv",)], w=[("u", par, j)])
                    q = j % 2
                    S.add("act", lambda e, j=j, q=q: e.activation(out=sq[:, q, :], in_=u32[:, par, j, :], func=AF.Square),
                          r=[("u", par, j)], w=[("sq", q)])
                    if j == 0:
                        S.add("dve", lambda e, j=j: e.tensor_copy(lr[:, par, :], u32[:, par, j, :]), r=[("u", par, j)], w=[("lr", par)])
                        S.add("dve", lambda e, q=q: e.tensor_copy(lnn[:, par, :], sq[:, q, :]), r=[("sq", q)], w=[("lnn", par)])
                    else:
                        S.add("dve", lambda e, j=j: e.tensor_tensor(out=lr[:, par, :], in0=lr[:, par, :], in1=u32[:, par, j, :], op=ALU.add),
                              r=[("u", par, j), ("lr", par)], w=[("lr", par)])
                        S.add("dve", lambda e, q=q: e.tensor_tensor(out=lnn[:, par, :], in0=lnn[:, par, :], in1=sq[:, q, :], op=ALU.add),
                              r=[("sq", q), ("lnn", par)], w=[("lnn", par)])
                    if blk > 0:
                        tail_chunk(blk - 1, j)
                s.stats_mm(0, 1, lr[:, par, :], lnn[:, par, :], ("lr", par), ("lnn", par))
                lnt = [(lm, ("ln", "mean")), (lr[:, par, :], ("lr", par)), (lnn[:, par, :], ("lnn", par)), (lt, ("ln", "tmp"))]
                s.ln_tail(None, 0, c.D, "ffn", lnt, None)

            load_ym(0)
            for blk_ in range(c.NB):
                body(blk_)
            for j in range(KD):
                tail_chunk(c.NB - 1, j)
            S.flush()


def tile_w(W):
    K, N = W.shape
    return np.ascontiguousarray(W.reshape(K // 128, 128, N // 128, 128).transpose(2, 1, 0, 3).reshape(N // 128, 128, K))


def fm(v):
    sh = v.shape
    return np.moveaxis(v.reshape(sh[:-1] + (sh[-1] // 128, 128)), -1, 0)


def prep_inputs(cfg, inp):
    c = cfg
    L = c.DEPTH
    f32 = lambda a: np.ascontiguousarray(np.asarray(a, dtype=np.float32))
    small = np.zeros((L, 128, c.NS), np.float32)
    so = c.so
    for l in range(L):
        def put(name, arr):
            arr = np.asarray(arr, np.float32).reshape(128, -1)
            small[l, :, so[name]:so[name] + arr.shape[1]] = arr
        put("bada", fm(inp["b_ada"][l]))
        put("lng", fm(inp["ln_g"][l]))
        put("lnb", fm(inp["ln_b"][l]))
        put("c4w", np.moveaxis(fm(inp["conv4_w"][l]), 1, 2))
        put("c4b", fm(inp["conv4_b"][l]))
        put("brg", fm(inp["b_rg"][l]))
        put("big", fm(inp["b_ig"][l]))
        put("lam", fm(inp["lam"][l]))
        put("c31w", np.moveaxis(fm(inp["conv31_w"][l]), 1, 2))
        put("c31b", fm(inp["conv31_b"][l]))
        put("clng", fm(inp["cln_g"][l]))
        put("clnb", fm(inp["cln_b"][l]))
        put("bout", fm(inp["b_out"][l]))
    wada = np.stack([tile_w(f32(inp["w_ada"][l])) for l in range(L)])
    w1 = np.stack([np.stack([tile_w(f32(inp[k][l])) for k in ("ff1_in", "ff2_in")]) for l in range(L)])
    w2 = np.stack([np.stack([tile_w(f32(inp[k][l])) for k in ("ff1_out", "ff2_out")]) for l in range(L)])
    win = np.stack([tile_w(f32(inp["w_in"][l])) for l in range(L)])
    wout = np.stack([tile_w(f32(inp["w_out"][l])) for l in range(L)])
    gw = np.zeros((L, 128, 2, 2, c.NH, c.HC, c.HD), np.float32)
    for l in range(L):
        for d in range(2):
            for g, nm in enumerate(("w_rg", "w_ig")):
                for h in range(c.NH):
                    tw = tile_w(f32(inp[nm][l][d][h]))
                    gw[l, :, d, g, h] = tw.transpose(1, 0, 2)
    gw = gw.reshape(L, 128, -1)
    shared = dict(small=small, wada=wada, w1=w1, w2=w2, win=win, wout=wout, gwt=np.ascontiguousarray(gw))
    maps = []
    for b in range(c.NCORES):
        xin = np.concatenate([f32(inp["x"][b]).T, f32(inp["ctx"][b]).T], axis=1)
        cv = np.stack([fm(f32(inp["c"][b])), fm(f32(inp["c_ctx"]))], axis=-1).reshape(128, -1)
        m = dict(shared)
        m["xin"] = np.ascontiguousarray(xin)
        m["cvec"] = np.ascontiguousarray(cv)
        m["ident"] = np.eye(128, dtype=np.float32)
        maps.append(m)
    return maps


_NC_CACHE = {}


def run_cfg(cfg, inp):
    key = (cfg.D, cfg.FF, cfg.T, cfg.DLRU, cfg.DCONV, getattr(cfg, "STOP", None))
    if key not in _NC_CACHE:
        _NC_CACHE[key] = Builder(cfg).build()
    nc = _NC_CACHE[key]
    maps = prep_inputs(cfg, inp)
    res = run_bass_kernel_spmd(nc, maps, core_ids=list(range(cfg.NCORES)))
    if getattr(cfg, "STOP", None) is not None:
        return np.stack([np.ascontiguousarray(res.results[b]["yout"].T) for b in range(cfg.NCORES)])
    out = np.stack([np.ascontiguousarray(res.results[b]["yout"][:, :cfg.T].T) for b in range(cfg.NCORES)])
    return out.astype(np.float32)


def kernel(**inputs):
    cfg = Cfg()
    return run_cfg(cfg, inputs)
```

```python
import numpy as np
import concourse.bass as bass
import concourse.mybir as mybir
from concourse.bass_utils import run_bass_kernel_spmd

F32 = mybir.dt.float32
BF16 = mybir.dt.bfloat16
AF = mybir.ActivationFunctionType
ALU = mybir.AluOpType

RG_C = 8.0
EPS = 1e-6
SAME_SYNC = True
STQ = "act"
ENGS = ("pe", "act", "dve", "pool", "sp")


class Cfg:
    def __init__(s, D=2048, FF=5632, DLRU=1024, NH=4, DCONV=1024, T=8192, TC=256, DEPTH=2,
                 TB=768, TS=512, NCORES=2):
        s.D, s.FF, s.DLRU, s.NH, s.DCONV, s.T, s.TC, s.DEPTH = D, FF, DLRU, NH, DCONV, T, TC, DEPTH
        s.TB, s.TS, s.NCORES = TB, TS, NCORES
        s.HALF = TB // 2
        s.GW, s.K31 = 64, 31
        s.KD, s.FK, s.LK, s.CK = D // 128, FF // 128, DLRU // 128, DCONV // 128
        s.DMIX = DLRU + DCONV
        s.MK = s.DMIX // 128
        s.DIN = 2 * DLRU + 2 * DCONV
        s.NT = T + TC
        s.NB = s.NT // TB
        s.HD = DLRU // NH
        s.HC = s.HD // 128
        s.ALPHA = (2 * DEPTH) ** 0.25
        assert s.NT % TB == 0 and TB % 128 == 0 and s.HALF % 64 == 0 and TC <= s.HALF
        assert TC <= TS and (T % TB) in (0, TB - TC)
        o = {}
        off = 0
        for name, n in (("bada", 9 * s.KD), ("lng", 3 * s.KD), ("lnb", 3 * s.KD), ("c4w", s.LK * 4),
                        ("c4b", s.LK), ("brg", 2 * s.LK), ("big", 2 * s.LK), ("lam", 2 * s.LK),
                        ("c31w", s.CK * 31), ("c31b", s.CK), ("clng", s.CK), ("clnb", s.CK),
                        ("bout", s.KD)):
            o[name] = off
            off += n
        s.so, s.NS = o, off


class Op:
    __slots__ = ("eng", "fn", "deps", "sig", "dkey", "waits", "needed")

    def __init__(s, eng, fn, dkey):
        s.eng, s.fn, s.dkey = eng, fn, dkey
        s.deps, s.sig, s.waits, s.needed = [], None, [], False


class Sched:
    def __init__(s, nc, stack):
        s.nc = nc
        s.stack = stack
        s.ops = {e: [] for e in ENGS}
        s.lastw, s.readers = {}, {}
        s.engsem = {e: stack.enter_context(nc.semaphore("es_" + e)) for e in ENGS if e != "sp"}
        s.engcnt = {e: 0 for e in s.engsem}
        s.dsem = {}
        s.seen = {e: {} for e in ENGS}
        s.first_phase = True
        s.nops = 0

    def add(s, eng, fn, r=(), w=(), dma=None):
        op = Op(eng, fn, dma)
        deps = []
        for k in r:
            lw = s.lastw.get(k)
            if lw is not None:
                deps.append(lw)
        for k in w:
            lw = s.lastw.get(k)
            if lw is not None:
                deps.append(lw)
            deps.extend(s.readers.get(k, ()))
        for k in r:
            s.readers.setdefault(k, []).append(op)
        for k in w:
            s.lastw[k] = op
            s.readers[k] = []
        seen = set()
        for d in deps:
            if d is op or id(d) in seen:
                continue
            seen.add(id(d))
            if d.dkey is None and d.eng == eng and (eng == "pe" or (not SAME_SYNC and eng != "pool")):
                continue
            op.deps.append(d)
            d.needed = True
        s.ops[eng].append(op)
        s.nops += 1
        return op

    def _dsem(s, key):
        if key not in s.dsem:
            s.dsem[key] = [s.stack.enter_context(s.nc.semaphore("ds%d" % len(s.dsem))), 0]
        return s.dsem[key]

    def flush(s, final=False):
        nc = s.nc
        pre = {e: [] for e in ENGS}
        if not s.first_phase:
            for e in ENGS:
                for e2, sem in s.engsem.items():
                    if e2 != e and s.engcnt[e2] > 0:
                        pre[e].append((sem, s.engcnt[e2]))
                for key, (sem, cnt) in s.dsem.items():
                    if cnt > 0:
                        pre[e].append((sem, cnt))
        s.first_phase = False
        for e in ENGS:
            lst = s.ops[e]
            lastc = max([i for i, op in enumerate(lst) if op.dkey is None], default=-1)
            for i, op in enumerate(lst):
                last = i == lastc
                if op.dkey is not None:
                    d = s._dsem(op.dkey)
                    d[1] += 16
                    op.sig = (d[0], d[1], 16)
                elif op.needed or last:
                    s.engcnt[e] += 1
                    op.sig = (s.engsem[e], s.engcnt[e], 1)
        for e in ENGS:
            seen = s.seen[e]
            for sem, v in pre[e]:
                seen[id(sem)] = max(seen.get(id(sem), 0), v)
            for op in s.ops[e]:
                for d in op.deps:
                    sem, v, _ = d.sig
                    if seen.get(id(sem), 0) >= v:
                        continue
                    seen[id(sem)] = v
                    op.waits.append((sem, v))
        post = []
        if final:
            post = [(sem, cnt) for (sem, cnt) in s.dsem.values() if cnt > 0]
            post += [(sem, s.engcnt[e2]) for e2, sem in s.engsem.items() if s.engcnt[e2] > 0]
        ops = s.ops

        def replay(eng_name):
            def run(e):
                for sem, v in pre[eng_name]:
                    e.wait_ge(sem, v)
                for op in ops[eng_name]:
                    for sem, v in op.waits:
                        e.wait_ge(sem, v)
                    ins = op.fn(e)
                    if op.sig is not None:
                        ins.then_inc(op.sig[0], op.sig[2])
                if eng_name == "sp":
                    for sem, v in post:
                        e.wait_ge(sem, v)
            return run

        with nc.Block() as block:
            block.tensor(replay("pe"))
            block.scalar(replay("act"))
            block.vector(replay("dve"))
            block.gpsimd(replay("pool"))
            block.sync(replay("sp"))
        s.ops = {e: [] for e in ENGS}
        s.lastw, s.readers = {}, {}


class Builder:
    def __init__(s, cfg):
        s.c = cfg

    def build(s):
        from contextlib import ExitStack
        c = s.c
        nc = bass.Bass("TRN2", target_bir_lowering=False)
        s.nc = nc
        L = c.DEPTH
        dt = nc.dram_tensor
        s.xin = dt("xin", [c.D, c.NT], F32, kind="ExternalInput").ap()
        s.cvec = dt("cvec", [128, c.KD * 2], F32, kind="ExternalInput").ap()
        s.small = dt("small", [L, 128, c.NS], F32, kind="ExternalInput").ap()
        s.wada = dt("wada", [L, 9 * c.KD, 128, c.D], F32, kind="ExternalInput").ap()
        s.w1 = dt("w1", [L, 2, 2 * c.FK, 128, c.D], F32, kind="ExternalInput").ap()
        s.w2 = dt("w2", [L, 2, c.KD, 128, c.FF], F32, kind="ExternalInput").ap()
        s.win = dt("win", [L, c.DIN // 128, 128, c.D], F32, kind="ExternalInput").ap()
        s.wout = dt("wout", [L, c.KD, 128, c.DMIX], F32, kind="ExternalInput").ap()
        s.gwt = dt("gwt", [L, 128, 2 * 2 * c.NH * c.HC * c.HD], F32, kind="ExternalInput").ap()
        s.identd = dt("ident", [128, 128], F32, kind="ExternalInput").ap()
        s.yout = dt("yout", [c.D, c.NT], F32, kind="ExternalOutput").ap()
        s.SA = dt("SA", [c.D, c.NT], F32, kind="Internal").ap()
        s.SB = dt("SB", [c.D, c.NT], F32, kind="Internal").ap()
        s.SC = dt("SC", [c.D, c.NT], F32, kind="Internal").ap()
        s.Uscr = dt("Uscr", [2, c.D, c.TB], F32, kind="Internal").ap()
        s.XR = dt("XR", [c.DLRU, c.NT], F32, kind="Internal").ap()
        s.GG = dt("GG", [c.DLRU, c.NT], F32, kind="Internal").ap()
        s.REC = dt("REC", [c.DLRU, c.NT], F32, kind="Internal").ap()
        s.YR = dt("YR", [c.DLRU, c.NT], BF16, kind="Internal").ap()
        s.YC = dt("YC", [c.DCONV, c.NT], BF16, kind="Internal").ap()

        with ExitStack() as st:
            s.st = st
            s.S = Sched(nc, st)
            s.ps = [st.enter_context(nc.psum_tensor("ps%d" % i, [128, 512], F32)) for i in range(8)]
            T_ = s.mkT(st)
            s.ones = T_("ones", [128, 128])
            s.cv32 = T_("cv32", [128, c.KD * 2])
            s.scb = T_("scb", [128, c.KD, 2], BF16)
            s.smallt = T_("smallt", [128, c.NS])
            s.modv = T_("modv", [128, 9 * c.KD, 2])
            s.bgv = T_("bgv", [128, c.KD, 2])
            s.clam = T_("clam", [128, 2 * c.LK])
            s.clam2 = T_("clam2", [128, 2 * c.LK])
            s.ctmp = [T_("ctmp%d" % i, [128, 2 * c.LK]) for i in range(4)]
            S = s.S
            s.ident = T_("ident", [128, 128])
            S.add("sp", lambda e: e.dma_start(out=s.ident[:], in_=s.identd), w=[("ident",)], dma=("ident",))
            S.add("dve", lambda e: e.memset(s.ones[:], 1.0), w=[("ones",)])
            S.add("sp", lambda e: e.dma_start(out=s.cv32[:], in_=s.cvec), w=[("cv32",)], dma=("cv32",))
            S.add("act", lambda e: e.activation(out=s.scb[:].rearrange("p k c -> p (k c)"), in_=s.cv32[:], func=AF.Silu),
                  r=[("cv32",)], w=[("scb",)])
            stop = getattr(c, "STOP", None)
            done = False
            for l in range(L):
                last = l == L - 1
                src_ = s.xin if l == 0 else s.SC
                steps = [("ada", lambda: s.phase_ada(l), None),
                         ("ffn0", lambda: s.phase_ffn(l, 0, src_, s.SA, final=(stop == (l, "ffn0"))), s.SA),
                         ("mixa", lambda: s.phase_mixa(l, s.SA), None),
                         ("scan", lambda: s.phase_scan(l, last), None),
                         ("mixc", lambda: s.phase_mixc(l, s.SA, s.SB), s.SB),
                         ("ffn1", lambda: s.phase_ffn(l, 1, s.SB, s.yout if last else s.SC, final=last), s.SC)]
                for name, fn, outstream in steps:
                    fn()
                    if stop == (l, name):
                        s.dbg_copy(outstream)
                        done = True
                        break
                if done:
                    break
        return nc

    def dbg_copy(s, stream):
        S = s.S
        c = s.c
        for j in range(c.KD):
            S.add("sp", lambda e, j=j: e.dma_start(out=s.yout[j * 128:(j + 1) * 128, :], in_=stream[j * 128:(j + 1) * 128, :]),
                  dma=("dbg", j))
        S.flush(final=True)

    def mkT(s, st):
        def T_(name, shape, dtp=F32):
            s.uid = getattr(s, "uid", 0) + 1
            return st.enter_context(s.nc.sbuf_tensor("%s_%d" % (name, s.uid), shape, dtp))
        return T_

    def sm(s, name, idx, n=1):
        o = s.c.so[name] + idx
        return s.smallt[:, o:o + n]

    def segs(s, blk):
        c = s.c
        lo, hi = blk * c.TB, (blk + 1) * c.TB
        out = []
        if lo < c.T:
            e_ = min(hi, c.T) - lo
            if getattr(c, "SPLIT", False) and e_ == c.TB:
                out.append((0, 512, 0))
                out.append((512, e_, 0))
            else:
                out.append((0, e_, 0))
        if hi > c.T:
            out.append((max(lo, c.T) - lo, c.TB, 1))
        return out

    def phase_ada(s, l):
        c, S, nc = s.c, s.S, s.nc
        NJ = 9 * c.KD
        with nc.sbuf_tensor("wa_%d" % l, [128, 6, c.D], BF16) as wa:
            S.add("sp", lambda e: e.dma_start(out=s.smallt[:], in_=s.small[l]), w=[("small",)], dma=("small",))
            for j in range(NJ):
                sl = j % 6
                S.add("pool", lambda e, j=j, sl=sl: e.dma_start(out=wa[:, sl, :], in_=s.wada[l, j], max_dma_last_dim=8192),
                      w=[("wa", sl)], dma=("wa", sl))
                for k in range(c.KD):
                    S.add("pe", lambda e, j=j, k=k, sl=sl: e.matmul(s.ps[0][:, 2 * j:2 * j + 2], wa[:, sl, k * 128:(k + 1) * 128],
                                                                  s.scb[:, k, :], start=(k == 0), stop=(k == c.KD - 1)),
                          r=[("wa", sl), ("scb",)], w=[("ps", 0)])
            psv = s.ps[0][:, 0:2 * NJ].rearrange("p (j c) -> p j c", c=2)
            for col in range(2):
                S.add("dve", lambda e, col=col: e.tensor_tensor(out=s.modv[:, :, col], in0=psv[:, :, col],
                                                                in1=s.sm("bada", 0, NJ), op=ALU.add),
                      r=[("ps", 0), ("small",)], w=[("modv",)])
            for sub in range(3):
                coef = (1.0 if sub == 1 else 0.5) / c.ALPHA
                S.add("dve", lambda e, sub=sub: e.tensor_scalar(s.modv[:, (3 * sub + 1) * c.KD:(3 * sub + 2) * c.KD, :],
                                                                s.modv[:, (3 * sub + 1) * c.KD:(3 * sub + 2) * c.KD, :],
                                                                1.0, None, ALU.add), r=[("modv",)], w=[("modv",)])
                S.add("dve", lambda e, sub=sub, coef=coef: e.tensor_scalar(s.modv[:, (3 * sub + 2) * c.KD:(3 * sub + 3) * c.KD, :],
                                                                           s.modv[:, (3 * sub + 2) * c.KD:(3 * sub + 3) * c.KD, :],
                                                                           coef, None, ALU.mult), r=[("modv",)], w=[("modv",)])
            for col in range(2):
                S.add("dve", lambda e, col=col: e.tensor_tensor(out=s.bgv[:, :, col], in0=s.modv[:, 5 * c.KD:6 * c.KD, col],
                                                                in1=s.sm("bout", 0, c.KD), op=ALU.mult),
                      r=[("modv",), ("small",)], w=[("bgv",)])
            t0, t1, t2, t3 = s.ctmp
            lam = s.sm("lam", 0, 2 * c.LK)
            K = ("clamk",)
            S.add("act", lambda e: e.activation(out=t0[:], in_=lam, func=AF.Exp, scale=-1.0), r=[("small",)], w=[K])
            S.add("act", lambda e: e.activation(out=t1[:], in_=t0[:], func=AF.Ln, bias=1.0), r=[K], w=[K])
            S.add("dve", lambda e: e.tensor_scalar(t2[:], t0[:], 0.05, None, ALU.min), r=[K], w=[K])
            S.add("dve", lambda e: e.tensor_scalar(t3[:], t2[:], 0.2, -0.25, ALU.mult, ALU.add), r=[K], w=[K])
            for cst in (1.0 / 3.0, -0.5, 1.0):
                S.add("dve", lambda e: e.tensor_tensor(out=t3[:], in0=t3[:], in1=t2[:], op=ALU.mult), r=[K], w=[K])
                S.add("dve", lambda e, cst=cst: e.tensor_scalar(t3[:], t3[:], cst, None, ALU.add), r=[K], w=[K])
            S.add("dve", lambda e: e.tensor_tensor(out=t3[:], in0=t3[:], in1=t2[:], op=ALU.mult), r=[K], w=[K])
            S.add("dve", lambda e: e.tensor_scalar(t2[:], t0[:], 0.05, None, ALU.is_lt), r=[K], w=[K])
            S.add("dve", lambda e: e.tensor_tensor(out=t3[:], in0=t3[:], in1=t1[:], op=ALU.subtract), r=[K], w=[K])
            S.add("dve", lambda e: e.tensor_tensor(out=t3[:], in0=t3[:], in1=t2[:], op=ALU.mult), r=[K], w=[K])
            S.add("dve", lambda e: e.tensor_tensor(out=t3[:], in0=t3[:], in1=t1[:], op=ALU.add), r=[K], w=[K])
            S.add("dve", lambda e: e.tensor_scalar(s.clam[:], t3[:], -RG_C, None, ALU.mult), r=[K], w=[("clam",)])
            S.add("dve", lambda e: e.tensor_scalar(s.clam2[:], t3[:], -2.0 * RG_C, None, ALU.mult), r=[K], w=[("clam",)])
            S.flush()

    def modulate(s, src, blk, sub, xmod, xs, nslot, cnt):
        c, S = s.c, s.S
        for k in range(c.KD):
            sl = cnt[0] % nslot
            cnt[0] += 1
            S.add("sp", lambda e, k=k, sl=sl: e.dma_start(out=xs[:, sl, :], in_=src[k * 128:(k + 1) * 128, blk * c.TB:(blk + 1) * c.TB]),
                  w=[("xs", sl)], dma=("xs", sl))
            for (lo, hi, which) in s.segs(blk):
                sc1 = s.modv[:, (3 * sub + 1) * c.KD + k, which:which + 1]
                sh = s.modv[:, (3 * sub) * c.KD + k, which:which + 1]
                if k % 2 == 0:
                    S.add("act", lambda e, k=k, sl=sl, lo=lo, hi=hi, sc1=sc1, sh=sh: e.activation(
                        out=xmod[:, k, lo:hi], in_=xs[:, sl, lo:hi], func=AF.Identity, scale=sc1, bias=sh),
                        r=[("xs", sl), ("modv",)], w=[("R1", k)])
                else:
                    S.add("dve", lambda e, k=k, sl=sl, lo=lo, hi=hi, sc1=sc1, sh=sh: e.tensor_scalar(
                        xmod[:, k, lo:hi], xs[:, sl, lo:hi], sc1, sh, ALU.mult, ALU.add),
                        r=[("xs", sl), ("modv",)], w=[("R1", k)])

    def ln_tail(s, u32, nch, Dn, eps, lnt, emit_out, ukeys=None):
        c, S = s.c, s.S
        (mean, kmean), (rstd, krstd), (nmr, knmr), (tmp, ktmp) = lnt
        if ukeys is None:
            ukeys = lambda j: [("u", j)]
        H = c.HALF
        inv = 1.0 / Dn
        for h in range(2):
            cs = slice(h * H, (h + 1) * H)
            S.add("act", lambda e, h=h, cs=cs: e.activation(out=mean[:, cs], in_=s.ps[4 + h][:, 0:H], func=AF.Identity, scale=inv),
                  r=[("ps", 4 + h)], w=[kmean])
            S.add("dve", lambda e, cs=cs: e.tensor_tensor(out=tmp[:, cs], in0=mean[:, cs], in1=mean[:, cs], op=ALU.mult),
                  r=[kmean], w=[ktmp])
            S.add("dve", lambda e, h=h, cs=cs: e.scalar_tensor_tensor(out=tmp[:, cs], in0=s.ps[6 + h][:, 0:H], scalar=inv, in1=tmp[:, cs],
                                                                      op0=ALU.mult, op1=ALU.subtract),
                  r=[("ps", 6 + h), ktmp], w=[ktmp])
            S.add("act", lambda e, cs=cs: e.activation(out=tmp[:, cs], in_=tmp[:, cs], func=AF.Sqrt, bias=s.epst[eps][:, 0:1]),
                  r=[ktmp, ("eps", eps)], w=[ktmp])
            S.add("dve", lambda e, cs=cs: e.reciprocal(rstd[:, cs], tmp[:, cs]), r=[ktmp], w=[krstd])
            S.add("dve", lambda e, cs=cs: e.scalar_tensor_tensor(out=nmr[:, cs], in0=mean[:, cs], scalar=-1.0, in1=rstd[:, cs],
                                                                 op0=ALU.mult, op1=ALU.mult),
                  r=[kmean, krstd], w=[knmr])
        for j in range(nch):
            S.add("dve", lambda e, j=j: e.tensor_tensor(out=u32[:, j, :], in0=u32[:, j, :], in1=rstd[:, :], op=ALU.mult),
                  r=ukeys(j) + [krstd], w=ukeys(j))
            S.add("dve", lambda e, j=j: e.tensor_tensor(out=u32[:, j, :], in0=u32[:, j, :], in1=nmr[:, :], op=ALU.add),
                  r=ukeys(j) + [knmr], w=ukeys(j))
            emit_out(j)

    def stats_mm(s, j, nch, src_u, src_sq, ukey, sqkey):
        c, S = s.c, s.S
        H = c.HALF
        for h in range(2):
            S.add("pe", lambda e, h=h: e.matmul(s.ps[4 + h][:, 0:H], s.ones[:], src_u[:, h * H:(h + 1) * H], start=(j == 0), stop=(j == nch - 1)),
                  r=[ukey, ("ones",)], w=[("ps", 4 + h)])
            S.add("pe", lambda e, h=h: e.matmul(s.ps[6 + h][:, 0:H], s.ones[:], src_sq[:, h * H:(h + 1) * H], start=(j == 0), stop=(j == nch - 1)),
                  r=[sqkey, ("ones",)], w=[("ps", 6 + h)])

    def u_keys(s, j):
        return [("R1", 2 * j), ("R1", 2 * j + 1), ("u", j)]

    def eps_tiles(s, st):
        c = s.c
        s.epst = {}
        for name, val in (("ffn", EPS / (c.ALPHA ** 2)), ("conv", EPS)):
            t = s.mkT(st)("eps_" + name, [128, 1], F32)
            s.S.add("dve", lambda e, t=t, val=val: e.memset(t[:], val), w=[("eps", name)])
            s.epst[name] = t

    def phase_ffn(s, l, which_ffn, src, dst, final=False):
        from contextlib import ExitStack
        c, S, nc = s.c, s.S, s.nc
        sub = 0 if which_ffn == 0 else 2
        H, TB, KD, FK = c.HALF, c.TB, c.KD, c.FK
        NW1 = 3
        U = s.Uscr
        with ExitStack() as st:
            T_ = s.mkT(st)
            xm = T_("xm", [128, 2, KD, TB], BF16)
            act = T_("act", [128, FK, TB], BF16)
            w1t = T_("w1t", [128, NW1, 2, c.D], BF16)
            w2t = T_("w2t", [128, 2, c.FF], BF16)
            xs = T_("xs", [128, 2, TB])
            uo = T_("uo", [128, 2, TB])
            sq = T_("sq", [128, 2, TB])
            nin = T_("nin", [128, 2, TB])
            sgt = T_("sgt", [128, 2, H])
            lr = T_("lr", [128, 2, TB])
            lnn = T_("lnn", [128, 2, TB])
            s.eps_tiles(st)
            cn = {"x": 0, "w1": 0, "w2": 0, "sg": 0, "n": 0}

            def modulate(blk):
                par = blk % 2
                for k in range(KD):
                    sl = cn["x"] % 2
                    cn["x"] += 1
                    S.add("sp", lambda e, k=k, sl=sl: e.dma_start(out=xs[:, sl, :], in_=src[k * 128:(k + 1) * 128, blk * TB:(blk + 1) * TB]),
                          w=[("xs", sl)], dma=("xs", sl))
                    for (lo, hi, which) in s.segs(blk):
                        sc1 = s.modv[:, (3 * sub + 1) * KD + k, which:which + 1]
                        sh = s.modv[:, (3 * sub) * KD + k, which:which + 1]
                        if k % 2 == 0:
                            S.add("act", lambda e, k=k, sl=sl, lo=lo, hi=hi, sc1=sc1, sh=sh: e.activation(
                                out=xm[:, par, k, lo:hi], in_=xs[:, sl, lo:hi], func=AF.Identity, scale=sc1, bias=sh),
                                r=[("xs", sl), ("modv",)], w=[("xm", par, k)])
                        else:
                            S.add("dve", lambda e, k=k, sl=sl, lo=lo, hi=hi, sc1=sc1, sh=sh: e.tensor_scalar(
                                xm[:, par, k, lo:hi], xs[:, sl, lo:hi], sc1, sh, ALU.mult, ALU.add),
                                r=[("xs", sl), ("modv",)], w=[("xm", par, k)])

            def norm_chunk(blk, j):
                par = blk % 2
                o = cn["n"] % 2
                cn["n"] += 1
                S.add("sp", lambda e: e.dma_start(out=nin[:, o, :], in_=U[par, j * 128:(j + 1) * 128, :]),
                      r=[("U", par, j)], w=[("nin", o)], dma=("nin", o))
                S.add("dve", lambda e: e.tensor_tensor(out=nin[:, o, :], in0=nin[:, o, :], in1=lr[:, par, :], op=ALU.mult),
                      r=[("nin", o), ("lr", par)], w=[("nin", o)])
                S.add("dve", lambda e: e.tensor_tensor(out=nin[:, o, :], in0=nin[:, o, :], in1=lnn[:, par, :], op=ALU.add),
                      r=[("nin", o), ("lnn", par)], w=[("nin", o)])
                S.add("act", lambda e: e.activation(out=nin[:, o, :], in_=nin[:, o, :], func=AF.Identity,
                                                    scale=s.sm("lng", sub * KD + j), bias=s.sm("lnb", sub * KD + j)),
                      r=[("nin", o), ("small",)], w=[("nin", o)])
                S.add(STQ, lambda e: e.dma_start(out=dst[j * 128:(j + 1) * 128, blk * TB:(blk + 1) * TB], in_=nin[:, o, :]),
                      r=[("nin", o)], dma=("nin", "st", o))

            def h_phase(blk):
                par = blk % 2
                for f in range(FK):
                    sl = cn["w1"] % NW1
                    cn["w1"] += 1
                    for gu in range(2):
                        S.add("pool", lambda e, f=f, sl=sl, gu=gu: e.dma_start(out=w1t[:, sl, gu, :], in_=s.w1[l, which_ffn, gu * FK + f],
                                                                              max_dma_last_dim=8192),
                              w=[("w1", sl, gu)], dma=("w1", sl, gu))
                    bs = 4 * (f % 2)
                    for gu in range(2):
                        for k in range(KD):
                            for h in range(2):
                                S.add("pe", lambda e, sl=sl, gu=gu, k=k, h=h, bs=bs: e.matmul(
                                    s.ps[bs + 2 * gu + h][:, 0:H], w1t[:, sl, gu, k * 128:(k + 1) * 128], xm[:, par, k, h * H:(h + 1) * H],
                                    start=(k == 0), stop=(k == KD - 1)),
                                    r=[("w1", sl, gu), ("xm", par, k)], w=[("ps", bs + 2 * gu + h)])
                    for h in range(2):
                        g = cn["sg"] % 2
                        cn["sg"] += 1
                        S.add("act", lambda e, g=g, h=h, bs=bs: e.activation(out=sgt[:, g, :], in_=s.ps[bs + h][:, 0:H], func=AF.Silu),
                              r=[("ps", bs + h)], w=[("sg", g)])
                        S.add("dve", lambda e, g=g, h=h, bs=bs, f=f: e.tensor_tensor(out=act[:, f, h * H:(h + 1) * H], in0=s.ps[bs + 2 + h][:, 0:H],
                                                                                     in1=sgt[:, g, :], op=ALU.mult),
                              r=[("ps", bs + 2 + h), ("sg", g)], w=[("act", f)])
                    if blk > 0 and f < KD:
                        norm_chunk(blk - 1, f)

            def y_phase(blk):
                par = blk % 2
                segs = s.segs(blk)
                pending = None
                for j in range(KD):
                    sl = cn["w2"] % 2
                    cn["w2"] += 1
                    S.add("pool", lambda e, j=j, sl=sl: e.dma_start(out=w2t[:, sl, :], in_=s.w2[l, which_ffn, j], max_dma_last_dim=8192),
                          w=[("w2", sl)], dma=("w2", sl))
                    xl = cn["x"] % 2
                    cn["x"] += 1
                    S.add("sp", lambda e, j=j, xl=xl: e.dma_start(out=xs[:, xl, :], in_=src[j * 128:(j + 1) * 128, blk * TB:(blk + 1) * TB]),
                          w=[("xs", xl)], dma=("xs", xl))
                    bs = 2 * (j % 2)
                    for f in range(FK):
                        for h in range(2):
                            S.add("pe", lambda e, sl=sl, f=f, h=h, bs=bs: e.matmul(
                                s.ps[bs + h][:, 0:H], w2t[:, sl, f * 128:(f + 1) * 128], act[:, f, h * H:(h + 1) * H],
                                start=(f == 0), stop=(f == FK - 1)),
                                r=[("w2", sl), ("act", f)], w=[("ps", bs + h)])
                    q = j % 2
                    for (lo, hi, which) in segs:
                        gcol = s.modv[:, (3 * sub + 2) * KD + j, which:which + 1]
                        for h in range(2):
                            a0, a1 = max(lo, h * H), min(hi, (h + 1) * H)
                            if a0 >= a1:
                                continue
                            S.add("dve", lambda e, h=h, a0=a0, a1=a1, gcol=gcol, xl=xl, bs=bs, q=q: e.scalar_tensor_tensor(
                                out=uo[:, q, a0:a1], in0=s.ps[bs + h][:, a0 - h * H:a1 - h * H], scalar=gcol, in1=xs[:, xl, a0:a1],
                                op0=ALU.mult, op1=ALU.add),
                                r=[("ps", bs + h), ("xs", xl), ("modv",)], w=[("uo", q)])
                    S.add("act", lambda e, q=q: e.activation(out=sq[:, q, :], in_=uo[:, q, :], func=AF.Square),
                          r=[("uo", q)], w=[("sq", q)])
                    S.add("sp", lambda e, j=j, q=q: e.dma_start(out=U[par, j * 128:(j + 1) * 128, :], in_=uo[:, q, :]),
                          r=[("uo", q)], w=[("U", par, j)], dma=("uo", q))
                    if j == 0:
                        S.add("dve", lambda e, q=q: e.tensor_copy(lr[:, par, :], uo[:, q, :]), r=[("uo", q)], w=[("lr", par)])
                        S.add("dve", lambda e, q=q: e.tensor_copy(lnn[:, par, :], sq[:, q, :]), r=[("sq", q)], w=[("lnn", par)])
                    else:
                        S.add("dve", lambda e, q=q: e.tensor_tensor(out=lr[:, par, :], in0=lr[:, par, :], in1=uo[:, q, :], op=ALU.add),
                              r=[("uo", q), ("lr", par)], w=[("lr", par)])
                        S.add("dve", lambda e, q=q: e.tensor_tensor(out=lnn[:, par, :], in0=lnn[:, par, :], in1=sq[:, q, :], op=ALU.add),
                              r=[("sq", q), ("lnn", par)], w=[("lnn", par)])
                s.stats_mm(0, 1, lr[:, par, :], lnn[:, par, :], ("lr", par), ("lnn", par))
                lnt = [(sq[:, 1, :], ("sq", 1)), (lr[:, par, :], ("lr", par)), (lnn[:, par, :], ("lnn", par)), (sq[:, 0, :], ("sq", 0))]
                s.ln_tail(None, 0, c.D, "ffn", lnt, None)

            modulate(0)
            for blk in range(c.NB):
                h_phase(blk)
                if blk + 1 < c.NB:
                    modulate(blk + 1)
                y_phase(blk)
            for j in range(KD):
                norm_chunk(c.NB - 1, j)
            S.flush(final=final)

    def phase_mixa(s, l, src):
        from contextlib import ExitStack
        c, S, nc = s.c, s.S, s.nc
        H, TB, KD, LK, CK = c.HALF, c.TB, c.KD, c.LK, c.CK
        NR = TB // 64
        PW = 64 + 30
        PL = NR * PW
        with ExitStack() as st:
            T_ = s.mkT(st)
            xmod = T_("hmod", [128, KD, TB], BF16)
            xs = T_("xs", [128, 3, TB])
            wt = T_("wint", [128, 4, c.D], BF16)
            xo = T_("xo", [128, 3, TB])
            sig = T_("sig", [128, 2, TB])
            upad = T_("upad", [128, CK, PL], BF16)
            dg = T_("dg", [128, CK, c.K31, 128], BF16)
            acc = T_("acc", [128, CK, TB])
            sq = T_("sq", [128, 2, TB])
            yo = T_("yo", [128, 2, TB], BF16)
            lnt = [(T_("lnm", [128, TB]), ("ln", "mean")), (T_("lnr", [128, TB]), ("ln", "rstd")),
                   (T_("lnn", [128, TB]), ("ln", "nmr")), (T_("lnt", [128, TB]), ("ln", "tmp"))]
            s.eps_tiles(st)
            xcnt = [0]
            wcnt = 0
            xocnt = 0
            for cc in range(CK):
                for kk in range(c.K31):
                    wk = s.sm("c31w", cc * 31 + kk)
                    if (cc * 31 + kk) % 2 == 0:
                        S.add("dve", lambda e, cc=cc, kk=kk, wk=wk: e.tensor_scalar(dg[:, cc, kk, :], s.ident[:], wk, None, ALU.mult),
                              r=[("ident",), ("small",)], w=[("dg", cc)])
                    else:
                        S.add("act", lambda e, cc=cc, kk=kk, wk=wk: e.activation(out=dg[:, cc, kk, :], in_=s.ident[:], func=AF.Identity, scale=wk),
                              r=[("ident",), ("small",)], w=[("dg", cc)])

            def body(blk):
                nonlocal wcnt, xocnt
                segs = s.segs(blk)
                cols = slice(blk * TB, (blk + 1) * TB)
                has_ctx = any(w == 1 for (_, _, w) in segs)
                nlat = segs[0][1] if segs[0][2] == 0 else 0
                nrl = nlat // 64
                s.modulate(src, blk, 1, xmod, xs, 3, xcnt)
                if blk == 0 or has_ctx:
                    S.add("dve", lambda e: e.memset(upad[:], 0.0), w=[("upad", cc) for cc in range(CK)])
                order = list(range(2 * LK))
                for cc in range(CK):
                    order += [2 * LK + CK + cc, 2 * LK + cc]
                for oi, o in enumerate(order):
                    sl = wcnt % 4
                    bs = 2 * (wcnt % 2)
                    wcnt += 1
                    S.add("pool", lambda e, o=o, sl=sl: e.dma_start(out=wt[:, sl, :], in_=s.win[l, o], max_dma_last_dim=8192),
                          w=[("win", sl)], dma=("win", sl))
                    for k in range(KD):
                        for h in range(2):
                            S.add("pe", lambda e, sl=sl, k=k, h=h, bs=bs: e.matmul(
                                s.ps[bs + h][:, 0:H], wt[:, sl, k * 128:(k + 1) * 128], xmod[:, k, h * H:(h + 1) * H],
                                start=(k == 0), stop=(k == KD - 1)),
                                r=[("win", sl), ("R1", k)], w=[("ps", bs + h)])
                    if o < 2 * LK:
                        xl = xocnt % 3
                        xocnt += 1
                        isg = o >= LK
                        for h in range(2):
                            if isg:
                                S.add("act", lambda e, xl=xl, h=h, bs=bs: e.activation(out=xo[:, xl, h * H:(h + 1) * H], in_=s.ps[bs + h][:, 0:H], func=AF.Gelu),
                                      r=[("ps", bs + h)], w=[("xo", xl)])
                            else:
                                S.add("act", lambda e, xl=xl, h=h, bs=bs: e.activation(out=xo[:, xl, h * H:(h + 1) * H], in_=s.ps[bs + h][:, 0:H], func=AF.Identity),
                                      r=[("ps", bs + h)], w=[("xo", xl)])
                        dstt = s.GG if isg else s.XR
                        ch = o - LK if isg else o
                        S.add(STQ, lambda e, xl=xl, dstt=dstt, ch=ch: e.dma_start(out=dstt[ch * 128:(ch + 1) * 128, cols], in_=xo[:, xl, :]),
                              r=[("xo", xl)], dma=("xo", "st", xl))
                    elif o >= 2 * LK + CK:
                        cc = o - 2 * LK - CK
                        sgl = cc % 2
                        for h in range(2):
                            S.add("act", lambda e, sgl=sgl, h=h, bs=bs: e.activation(out=sig[:, sgl, h * H:(h + 1) * H], in_=s.ps[bs + h][:, 0:H], func=AF.Sigmoid),
                                  r=[("ps", bs + h)], w=[("sig", sgl)])
                    else:
                        cc = o - 2 * LK
                        sgl = cc % 2
                        for h in range(2):
                            r0, r1 = h * (H // 64), min((h + 1) * (H // 64), nrl)
                            if r1 > r0:
                                n = (r1 - r0) * 64
                                outv = upad[:, cc, r0 * PW:r1 * PW].rearrange("p (r w) -> p r w", w=PW)[:, :, 15:79]
                                S.add("dve", lambda e, outv=outv, n=n, h=h, bs=bs, sgl=sgl: e.tensor_tensor(
                                    out=outv, in0=s.ps[bs + h][:, 0:n].rearrange("p (r w) -> p r w", w=64),
                                    in1=sig[:, sgl, h * H:h * H + n].rearrange("p (r w) -> p r w", w=64), op=ALU.mult),
                                    r=[("ps", bs + h), ("sig", sgl)], w=[("upad", cc)])
                            if has_ctx and h == 1:
                                a0 = nlat - H
                                S.add("dve", lambda e, a0=a0, bs=bs, sgl=sgl, cc=cc: e.tensor_tensor(
                                    out=upad[:, cc, nrl * PW + 15:nrl * PW + 15 + c.TC], in0=s.ps[bs + 1][:, a0:a0 + c.TC],
                                    in1=sig[:, sgl, H + a0:H + a0 + c.TC], op=ALU.mult),
                                    r=[("ps", bs + 1), ("sig", sgl)], w=[("upad", cc)])
                        cb = 2 * (wcnt % 2)
                        wcnt += 1
                        R6 = H // 64
                        for h in range(2):
                            r0, r1 = h * R6, min((h + 1) * R6, nrl)
                            if r1 > r0:
                                n = (r1 - r0) * 64
                                for kk in range(c.K31):
                                    rv = upad[:, cc, r0 * PW:r1 * PW].rearrange("p (r w) -> p r w", w=PW)[:, :, kk:kk + 64]
                                    S.add("pe", lambda e, cc=cc, kk=kk, h=h, n=n, rv=rv, cb=cb: e.matmul(
                                        s.ps[cb + h][:, 0:n].rearrange("p (r w) -> p r w", w=64), dg[:, cc, kk, :], rv,
                                        start=(kk == 0), stop=(kk == c.K31 - 1)),
                                        r=[("dg", cc), ("upad", cc)], w=[("ps", cb + h)])
                            if has_ctx and h == 1:
                                a0 = nlat - H
                                for kk in range(c.K31):
                                    S.add("pe", lambda e, cc=cc, kk=kk, a0=a0, cb=cb: e.matmul(
                                        s.ps[cb + 1][:, a0:a0 + c.TC], dg[:, cc, kk, :], upad[:, cc, nrl * PW + kk:nrl * PW + kk + c.TC],
                                        start=(kk == 0), stop=(kk == c.K31 - 1)),
                                        r=[("dg", cc), ("upad", cc)], w=[("ps", cb + 1)])
                        q = cc % 2
                        for h in range(2):
                            S.add("act", lambda e, cc=cc, h=h, cb=cb: e.activation(out=acc[:, cc, h * H:(h + 1) * H], in_=s.ps[cb + h][:, 0:H], func=AF.Identity,
                                                                                   bias=s.sm("c31b", cc)),
                                  r=[("ps", cb + h), ("small",)], w=[("acc", cc)])
                            S.add("act", lambda e, cc=cc, h=h, cb=cb, q=q: e.activation(out=sq[:, q, h * H:(h + 1) * H], in_=s.ps[cb + h][:, 0:H], func=AF.Square,
                                                                                        bias=s.sm("c31b", cc)),
                                  r=[("ps", cb + h), ("small",)], w=[("sq", q)])
                        if cc == 0:
                            S.add("dve", lambda e, cc=cc: e.tensor_copy(lnt[1][0][:, :], acc[:, cc, :]), r=[("acc", cc)], w=[lnt[1][1]])
                            S.add("dve", lambda e, q=q: e.tensor_copy(lnt[2][0][:, :], sq[:, q, :]), r=[("sq", q)], w=[lnt[2][1]])
                        else:
                            S.add("dve", lambda e, cc=cc: e.tensor_tensor(out=lnt[1][0][:, :], in0=lnt[1][0][:, :], in1=acc[:, cc, :], op=ALU.add),
                                  r=[("acc", cc), lnt[1][1]], w=[lnt[1][1]])
                            S.add("dve", lambda e, q=q: e.tensor_tensor(out=lnt[2][0][:, :], in0=lnt[2][0][:, :], in1=sq[:, q, :], op=ALU.add),
                                  r=[("sq", q), lnt[2][1]], w=[lnt[2][1]])
                        if cc == CK - 1:
                            s.stats_mm(0, 1, lnt[1][0][:, :], lnt[2][0][:, :], lnt[1][1], lnt[2][1])
                ocnt = [0]

                def emit_out(j):
                    o = ocnt[0] % 2
                    ocnt[0] += 1
                    S.add("act", lambda e, j=j, o=o: e.activation(out=yo[:, o, :], in_=acc[:, j, :], func=AF.Silu,
                                                                  scale=s.sm("clng", j), bias=s.sm("clnb", j)),
                          r=[("acc", j), ("small",)], w=[("yo", o)])
                    S.add(STQ, lambda e, j=j, o=o: e.dma_start(out=s.YC[j * 128:(j + 1) * 128, cols], in_=yo[:, o, :]),
                          r=[("yo", o)], dma=("yo", "st", o))
                s.ln_tail(acc, CK, c.DCONV, "conv", lnt, emit_out, ukeys=lambda j: [("acc", j)])
            for blk_ in range(c.NB):
                body(blk_)
            S.flush()

    def phase_scan(s, l, last):
        from contextlib import ExitStack
        c, S, nc = s.c, s.S, s.nc
        LK, TS, NH, HC, HD = c.LK, c.TS, c.NH, c.HC, c.HD
        with ExitStack() as st:
            T_ = s.mkT(st)
            gw = T_("gw", [128, 2 * 2 * NH * HC * HD], BF16)
            xrh = T_("xrh", [128, 2, LK, TS + 3])
            xc = T_("xc", [128, 2, LK, TS])
            xcb = T_("xcb", [128, 2, LK, TS], BF16)
            Rt = T_("Rt", [128, LK, TS])
            It = T_("It", [128, LK, TS])
            At = T_("At", [128, LK, TS])
            St = T_("St", [128, LK, TS])
            hh = T_("hh", [128, 3, TS])
            rf = T_("rf", [128, LK, TS])
            gg = T_("gg", [128, LK, TS])
            yr = T_("yr", [128, 2, TS], BF16)
            state = T_("state", [128, LK])
            S.add("pool", lambda e: e.dma_start(out=gw[:], in_=s.gwt[l], max_dma_last_dim=8192), w=[("gw",)], dma=("gw",))

            def gwv(d, g, hd, jc, ic):
                base = (((d * 2 + g) * NH + hd) * HC + jc) * HD + ic * 128
                return gw[:, base:base + 128]
            XRv = s.XR.rearrange("(c p) t -> p c t", p=128)
            RECv = s.REC.rearrange("(c p) t -> p c t", p=128)
            GGv = s.GG.rearrange("(c p) t -> p c t", p=128)
            cnt = {"x": 0, "h": 0, "rf": 0, "yr": 0, "cb": 0}
            seq_ctx = (c.T, c.TC)
            seq_lat = (0, c.T)

            def partA(i, d, off, ln, is_ctx, b):
                sl = i % 2
                lo = b * TS
                n = min(TS, ln - lo)
                xl = cnt["x"] % 2
                cnt["x"] += 1
                a0 = max(lo - 2, 0)
                a1 = min(lo + n + 1, ln)
                d0 = a0 - (lo - 2)
                if d0 > 0:
                    S.add("dve", lambda e: e.memset(xrh[:, xl, :, 0:d0], 0.0), w=[("xrh", xl)])
                if a1 < lo + n + 1:
                    S.add("dve", lambda e: e.memset(xrh[:, xl, :, n + 2:n + 3], 0.0), w=[("xrh", xl)])
                S.add("sp", lambda e: e.dma_start(out=xrh[:, xl, :, d0:d0 + (a1 - a0)], in_=XRv[:, :, off + a0:off + a1]),
                      w=[("xrh", xl)], dma=("xrh", xl))
                for ch in range(LK):
                    S.add("act", lambda e, ch=ch: e.activation(out=xc[:, sl, ch, 0:n], in_=xrh[:, xl, ch, 0:n], func=AF.Identity,
                                                               scale=s.sm("c4w", ch * 4 + 0), bias=s.sm("c4b", ch)),
                          r=[("xrh", xl), ("small",)], w=[("xc", sl, ch)])
                    for kk in range(1, 4):
                        S.add("dve", lambda e, ch=ch, kk=kk: e.scalar_tensor_tensor(
                            out=xc[:, sl, ch, 0:n], in0=xrh[:, xl, ch, kk:kk + n], scalar=s.sm("c4w", ch * 4 + kk), in1=xc[:, sl, ch, 0:n],
                            op0=ALU.mult, op1=ALU.add), r=[("xrh", xl), ("small",)], w=[("xc", sl, ch)])
                    S.add("pool", lambda e, ch=ch: e.tensor_copy(xcb[:, sl, ch, 0:n], xc[:, sl, ch, 0:n]),
                          r=[("xc", sl, ch)], w=[("xcb", sl, ch)])

            def partB(i, d, off, ln, is_ctx, b):
                sl = i % 2
                lo = b * TS
                n = min(TS, ln - lo)
                if d == 1 and not (is_ctx and last):
                    S.add("sp", lambda e: e.dma_start(out=rf[:, :, 0:n], in_=RECv[:, :, off + lo:off + lo + n]), w=[("rf", ch) for ch in range(LK)], dma=("rf",))
                    S.add("sp", lambda e: e.dma_start(out=gg[:, :, 0:n], in_=GGv[:, :, off + lo:off + lo + n]), w=[("gg", ch) for ch in range(LK)], dma=("gg",))

                for hf in range(2 if LK >= 4 else 1):
                    chs = range(hf * (LK // 2), (hf + 1) * (LK // 2)) if LK >= 4 else range(LK)
                    for g, dstt, bname in ((0, Rt, "brg"), (1, It, "big")):
                        for ch in chs:
                            hd, jc = ch // HC, ch % HC
                            bank = ch % 8
                            for ic in range(HC):
                                S.add("pe", lambda e, g=g, hd=hd, jc=jc, ic=ic, bank=bank: e.matmul(
                                    s.ps[bank][:, 0:n], gwv(d, g, hd, jc, ic), xcb[:, sl, hd * HC + ic, 0:n], start=(ic == 0), stop=(ic == HC - 1)),
                                    r=[("gw",), ("xcb", sl, hd * HC + ic)], w=[("ps", bank)])
                            S.add("act", lambda e, ch=ch, bank=bank, dstt=dstt, bname=bname: e.activation(
                                out=dstt[:, ch, 0:n], in_=s.ps[bank][:, 0:n], func=AF.Sigmoid, bias=s.sm(bname, d * LK + ch)),
                                r=[("ps", bank), ("small",)], w=[("g%d" % g, ch)])
                    for ch in chs:
                        S.add("act", lambda e, ch=ch: e.activation(out=At[:, ch, 0:n], in_=Rt[:, ch, 0:n], func=AF.Exp,
                                                                   scale=s.clam[:, d * LK + ch:d * LK + ch + 1]),
                              r=[("g0", ch), ("clam",)], w=[("A", ch)])
                        S.add("act", lambda e, ch=ch: e.activation(out=St[:, ch, 0:n], in_=Rt[:, ch, 0:n], func=AF.Exp,
                                                                   scale=s.clam2[:, d * LK + ch:d * LK + ch + 1]),
                              r=[("g0", ch), ("clam",)], w=[("S", ch)])
                    for ch in chs:
                        S.add("act", lambda e, ch=ch: e.activation(out=St[:, ch, 0:n], in_=St[:, ch, 0:n], func=AF.Sqrt, scale=-1.0,
                                                                   bias=s.ones[:, 0:1]),
                              r=[("S", ch), ("ones",)], w=[("S", ch)])

                for ch in range(LK):
                    S.add("pool", lambda e, ch=ch: e.tensor_tensor(out=St[:, ch, 0:n], in0=St[:, ch, 0:n], in1=It[:, ch, 0:n], op=ALU.mult),
                          r=[("S", ch), ("g1", ch)], w=[("S", ch)])
                    S.add("pool", lambda e, ch=ch: e.tensor_tensor(out=St[:, ch, 0:n], in0=St[:, ch, 0:n], in1=xc[:, sl, ch, 0:n], op=ALU.mult),
                          r=[("S", ch), ("xc", sl, ch)], w=[("S", ch)])
                    hl = cnt["h"] % 3
                    cnt["h"] += 1
                    if d == 0:
                        S.add("dve", lambda e, ch=ch, hl=hl: e.tensor_tensor_scan(hh[:, hl, 0:n], At[:, ch, 0:n], St[:, ch, 0:n],
                                                                                 state[:, ch:ch + 1], ALU.mult, ALU.add),
                              r=[("A", ch), ("S", ch), ("state", ch)], w=[("hh", hl)])
                        S.add("dve", lambda e, ch=ch, hl=hl: e.tensor_copy(state[:, ch:ch + 1], hh[:, hl, n - 1:n]),
                              r=[("hh", hl)], w=[("state", ch)])
                        if not (is_ctx and last):
                            S.add("sp", lambda e, ch=ch, hl=hl: e.dma_start(
                                out=s.REC[ch * 128:(ch + 1) * 128, off + lo:off + lo + n], in_=hh[:, hl, 0:n]),
                                r=[("hh", hl)], dma=("hh", hl))
                    else:
                        S.add("dve", lambda e, ch=ch, hl=hl: e.tensor_tensor_scan(hh[:, hl, 0:n][:, ::-1], At[:, ch, 0:n][:, ::-1],
                                                                                 St[:, ch, 0:n][:, ::-1],
                                                                                 state[:, ch:ch + 1], ALU.mult, ALU.add),
                              r=[("A", ch), ("S", ch), ("state", ch)], w=[("hh", hl)])
                        S.add("dve", lambda e, ch=ch, hl=hl: e.tensor_copy(state[:, ch:ch + 1], hh[:, hl, 0:1]),
                              r=[("hh", hl)], w=[("state", ch)])
                        if not (is_ctx and last):
                            rl = cnt["rf"] % 2
                            cnt["rf"] += 1
                            S.add("pool", lambda e, ch=ch, hl=hl: e.tensor_tensor(out=hh[:, hl, 0:n], in0=hh[:, hl, 0:n], in1=rf[:, ch, 0:n], op=ALU.add),
                                  r=[("hh", hl), ("rf", ch)], w=[("hh", hl)])
                            S.add("dve", lambda e, ch=ch, hl=hl, rl=rl: e.tensor_tensor(out=yr[:, rl, 0:n], in0=hh[:, hl, 0:n], in1=gg[:, ch, 0:n], op=ALU.mult),
                                  r=[("hh", hl), ("gg", ch)], w=[("yr", rl)])
                            S.add("sp", lambda e, ch=ch, rl=rl: e.dma_start(
                                out=s.YR[ch * 128:(ch + 1) * 128, off + lo:off + lo + n], in_=yr[:, rl, 0:n]),
                                r=[("yr", rl)], dma=("yr", rl))

            for d in range(2):
                S.add("dve", lambda e: e.memset(state[:], 0.0), w=[("state", ch) for ch in range(LK)])
                items = []
                for (off, ln), is_ctx in ((seq_ctx, True), (seq_lat, False)):
                    nblk = (ln + TS - 1) // TS
                    blks = list(range(nblk))
                    if d == 1:
                        blks = blks[::-1]
                    for b in blks:
                        items.append((d, off, ln, is_ctx, b))
                partA(0, *items[0])
                for i, it in enumerate(items):
                    if i + 1 < len(items):
                        partA(i + 1, *items[i + 1])
                    partB(i, *it)
                S.flush()

    def phase_mixc(s, l, src, dst):
        from contextlib import ExitStack
        c, S, nc = s.c, s.S, s.nc
        H, TB, KD, MK, LK, CK = c.HALF, c.TB, c.KD, c.MK, c.LK, c.CK
        with ExitStack() as st:
            T_ = s.mkT(st)
            ym = T_("ym", [128, 2, MK, TB], BF16)
            u32 = T_("u32", [128, 2, KD, TB])
            wt = T_("woutt", [128, 3, c.DMIX], BF16)
            xs = T_("xs", [128, 3, TB])
            sq = T_("sq", [128, 2, TB])
            osl = T_("osl", [128, 2, TB])
            lr = T_("lr", [128, 2, TB])
            lnn = T_("lnn", [128, 2, TB])
            lm = T_("lm", [128, TB])
            lt = T_("lt", [128, TB])
            s.eps_tiles(st)
            YRv = s.YR.rearrange("(c p) t -> p c t", p=128)
            YCv = s.YC.rearrange("(c p) t -> p c t", p=128)
            cn = {"x": 0, "w": 0, "o": 0}

            def load_ym(blk):
                par = blk % 2
                cols = slice(blk * TB, (blk + 1) * TB)
                S.add("sp", lambda e: e.dma_start(out=ym[:, par, 0:LK, :], in_=YRv[:, :, cols]), w=[("ym", par, 0)], dma=("ym", par, 0))
                S.add("sp", lambda e: e.dma_start(out=ym[:, par, LK:MK, :], in_=YCv[:, :, cols]), w=[("ym", par, 1)], dma=("ym", par, 1))

            def tail_chunk(blk, j):
                par = blk % 2
                cols = slice(blk * TB, (blk + 1) * TB)
                o = cn["o"] % 2
                cn["o"] += 1
                S.add("dve", lambda e: e.tensor_tensor(out=u32[:, par, j, :], in0=u32[:, par, j, :], in1=lr[:, par, :], op=ALU.mult),
                      r=[("u", par, j), ("lr", par)], w=[("u", par, j)])
                S.add("dve", lambda e: e.tensor_tensor(out=u32[:, par, j, :], in0=u32[:, par, j, :], in1=lnn[:, par, :], op=ALU.add),
                      r=[("u", par, j), ("lnn", par)], w=[("u", par, j)])
                S.add("act", lambda e: e.activation(out=osl[:, o, :], in_=u32[:, par, j, :], func=AF.Identity,
                                                    scale=s.sm("lng", 1 * KD + j), bias=s.sm("lnb", 1 * KD + j)),
                      r=[("u", par, j), ("small",)], w=[("os", o)])
                S.add(STQ, lambda e: e.dma_start(out=dst[j * 128:(j + 1) * 128, cols], in_=osl[:, o, :]),
                      r=[("os", o)], dma=("os", "st", o))

            def body(blk):
                par = blk % 2
                segs = s.segs(blk)
                cols = slice(blk * TB, (blk + 1) * TB)
                if blk + 1 < c.NB:
                    load_ym(blk + 1)
                pending = None
                for j in range(KD):
                    sl = cn["w"] % 3
                    cn["w"] += 1
                    S.add("pool", lambda e, j=j, sl=sl: e.dma_start(out=wt[:, sl, :], in_=s.wout[l, j], max_dma_last_dim=8192),
                          w=[("wo", sl)], dma=("wo", sl))
                    xl = cn["x"] % 3
                    cn["x"] += 1
                    S.add("sp", lambda e, j=j, xl=xl: e.dma_start(out=xs[:, xl, :], in_=src[j * 128:(j + 1) * 128, cols]),
                          w=[("xs", xl)], dma=("xs", xl))
                    bs = 2 * (j % 2)
                    for kc in range(MK):
                        for h in range(2):
                            S.add("pe", lambda e, sl=sl, kc=kc, h=h, bs=bs: e.matmul(
                                s.ps[bs + h][:, 0:H], wt[:, sl, kc * 128:(kc + 1) * 128], ym[:, par, kc, h * H:(h + 1) * H],
                                start=(kc == 0), stop=(kc == MK - 1)),
                                r=[("wo", sl), ("ym", par, 0 if kc < LK else 1)], w=[("ps", bs + h)])
                    for (lo, hi, which) in segs:
                        S.add("act", lambda e, j=j, xl=xl, lo=lo, hi=hi, which=which: e.activation(
                            out=xs[:, xl, lo:hi], in_=xs[:, xl, lo:hi], func=AF.Identity, bias=s.bgv[:, j, which:which + 1]),
                            r=[("xs", xl), ("bgv",)], w=[("xs", xl)])
                        gcol = s.modv[:, 5 * KD + j, which:which + 1]
                        for h in range(2):
                            a0, a1 = max(lo, h * H), min(hi, (h + 1) * H)
                            if a0 >= a1:
                                continue
                            S.add("dve", lambda e, j=j, h=h, a0=a0, a1=a1, gcol=gcol, xl=xl, bs=bs: e.scalar_tensor_tensor(
                                out=u32[:, par, j, a0:a1], in0=s.ps[bs + h][:, a0 - h * H:a1 - h * H], scalar=gcol, in1=xs[:, xl, a0:a1],
                                op0=ALU.mult, op1=ALU.add),
                                r=[("ps", bs + h), ("xs", xl), ("modv",)], w=[("u", par, j)])
                    q = j % 2
                    S.add("act", lambda e, j=j, q=q: e.activation(out=sq[:, q, :], in_=u32[:, par, j, :], func=AF.Square),
                          r=[("u", par, j)], w=[("sq", q)])
                    if j == 0:
                        S.add("dve", lambda e, j=j: e.tensor_copy(lr[:, par, :], u32[:, par, j, :]), r=[("u", par, j)], w=[("lr", par)])
                        S.add("dve", lambda e, q=q: e.tensor_copy(lnn[:, par, :], sq[:, q, :]), r=[("sq", q)], w=[("lnn", par)])
                    else:
                        S.add("dve", lambda e, j=j: e.tensor_tensor(out=lr[:, par, :], in0=lr[:, par, :], in1=u32[:, par, j, :], op=ALU.add),
                              r=[("u", par, j), ("lr", par)], w=[("lr", par)])
                        S.add("dve", lambda e, q=q: e.tensor_tensor(out=lnn[:, par, :], in0=lnn[:, par, :], in1=sq[:, q, :], op=ALU.add),
                              r=[("sq", q), ("lnn", par)], w=[("lnn", par)])
                    if blk > 0:
                        tail_chunk(blk - 1, j)
                s.stats_mm(0, 1, lr[:, par, :], lnn[:, par, :], ("lr", par), ("lnn", par))
                lnt = [(lm, ("ln", "mean")), (lr[:, par, :], ("lr", par)), (lnn[:, par, :], ("lnn", par)), (lt, ("ln", "tmp"))]
                s.ln_tail(None, 0, c.D, "ffn", lnt, None)

            load_ym(0)
            for blk_ in range(c.NB):
                body(blk_)
            for j in range(KD):
                tail_chunk(c.NB - 1, j)
            S.flush()


def tile_w(W):
    K, N = W.shape
    return np.ascontiguousarray(W.reshape(K // 128, 128, N // 128, 128).transpose(2, 1, 0, 3).reshape(N // 128, 128, K))


def fm(v):
    sh = v.shape
    return np.moveaxis(v.reshape(sh[:-1] + (sh[-1] // 128, 128)), -1, 0)


def prep_inputs(cfg, inp):
    c = cfg
    L = c.DEPTH
    f32 = lambda a: np.ascontiguousarray(np.asarray(a, dtype=np.float32))
    small = np.zeros((L, 128, c.NS), np.float32)
    so = c.so
    for l in range(L):
        def put(name, arr):
            arr = np.asarray(arr, np.float32).reshape(128, -1)
            small[l, :, so[name]:so[name] + arr.shape[1]] = arr
        put("bada", fm(inp["b_ada"][l]))
        put("lng", fm(inp["ln_g"][l]))
        put("lnb", fm(inp["ln_b"][l]))
        put("c4w", np.moveaxis(fm(inp["conv4_w"][l]), 1, 2))
        put("c4b", fm(inp["conv4_b"][l]))
        put("brg", fm(inp["b_rg"][l]))
        put("big", fm(inp["b_ig"][l]))
        put("lam", fm(inp["lam"][l]))
        put("c31w", np.moveaxis(fm(inp["conv31_w"][l]), 1, 2))
        put("c31b", fm(inp["conv31_b"][l]))
        put("clng", fm(inp["cln_g"][l]))
        put("clnb", fm(inp["cln_b"][l]))
        put("bout", fm(inp["b_out"][l]))
    wada = np.stack([tile_w(f32(inp["w_ada"][l])) for l in range(L)])
    w1 = np.stack([np.stack([tile_w(f32(inp[k][l])) for k in ("ff1_in", "ff2_in")]) for l in range(L)])
    w2 = np.stack([np.stack([tile_w(f32(inp[k][l])) for k in ("ff1_out", "ff2_out")]) for l in range(L)])
    win = np.stack([tile_w(f32(inp["w_in"][l])) for l in range(L)])
    wout = np.stack([tile_w(f32(inp["w_out"][l])) for l in range(L)])
    gw = np.zeros((L, 128, 2, 2, c.NH, c.HC, c.HD), np.float32)
    for l in range(L):
        for d in range(2):
            for g, nm in enumerate(("w_rg", "w_ig")):
                for h in range(c.NH):
                    tw = tile_w(f32(inp[nm][l][d][h]))
                    gw[l, :, d, g, h] = tw.transpose(1, 0, 2)
    gw = gw.reshape(L, 128, -1)
    shared = dict(small=small, wada=wada, w1=w1, w2=w2, win=win, wout=wout, gwt=np.ascontiguousarray(gw))
    maps = []
    for b in range(c.NCORES):
        xin = np.concatenate([f32(inp["x"][b]).T, f32(inp["ctx"][b]).T], axis=1)
        cv = np.stack([fm(f32(inp["c"][b])), fm(f32(inp["c_ctx"]))], axis=-1).reshape(128, -1)
        m = dict(shared)
        m["xin"] = np.ascontiguousarray(xin)
        m["cvec"] = np.ascontiguousarray(cv)
        m["ident"] = np.eye(128, dtype=np.float32)
        maps.append(m)
    return maps


_NC_CACHE = {}


def run_cfg(cfg, inp):
    key = (cfg.D, cfg.FF, cfg.T, cfg.DLRU, cfg.DCONV, getattr(cfg, "STOP", None))
    if key not in _NC_CACHE:
        _NC_CACHE[key] = Builder(cfg).build()
    nc = _NC_CACHE[key]
    maps = prep_inputs(cfg, inp)
    res = run_bass_kernel_spmd(nc, maps, core_ids=list(range(cfg.NCORES)))
    if getattr(cfg, "STOP", None) is not None:
        return np.stack([np.ascontiguousarray(res.results[b]["yout"].T) for b in range(cfg.NCORES)])
    out = np.stack([np.ascontiguousarray(res.results[b]["yout"][:, :cfg.T].T) for b in range(cfg.NCORES)])
    return out.astype(np.float32)


def kernel(**inputs):
    cfg = Cfg()
    return run_cfg(cfg, inputs)
```

```python
import numpy as np
import concourse.bass as bass
import concourse.mybir as mybir
from concourse.bass_utils import run_bass_kernel_spmd

F32 = mybir.dt.float32
BF16 = mybir.dt.bfloat16
AF = mybir.ActivationFunctionType
ALU = mybir.AluOpType

RG_C = 8.0
EPS = 1e-6
SAME_SYNC = True
STQ = "act"
ENGS = ("pe", "act", "dve", "pool", "sp")


class Cfg:
    def __init__(s, D=2048, FF=5632, DLRU=1024, NH=4, DCONV=1024, T=8192, TC=256, DEPTH=2,
                 TB=768, TS=512, NCORES=2):
        s.D, s.FF, s.DLRU, s.NH, s.DCONV, s.T, s.TC, s.DEPTH = D, FF, DLRU, NH, DCONV, T, TC, DEPTH
        s.TB, s.TS, s.NCORES = TB, TS, NCORES
        s.HALF = TB // 2
        s.GW, s.K31 = 64, 31
        s.KD, s.FK, s.LK, s.CK = D // 128, FF // 128, DLRU // 128, DCONV // 128
        s.DMIX = DLRU + DCONV
        s.MK = s.DMIX // 128
        s.DIN = 2 * DLRU + 2 * DCONV
        s.NT = T + TC
        s.NB = s.NT // TB
        s.HD = DLRU // NH
        s.HC = s.HD // 128
        s.ALPHA = (2 * DEPTH) ** 0.25
        assert s.NT % TB == 0 and TB % 128 == 0 and s.HALF % 64 == 0 and TC <= s.HALF
        assert TC <= TS and (T % TB) in (0, TB - TC)
        o = {}
        off = 0
        for name, n in (("bada", 9 * s.KD), ("lng", 3 * s.KD), ("lnb", 3 * s.KD), ("c4w", s.LK * 4),
                        ("c4b", s.LK), ("brg", 2 * s.LK), ("big", 2 * s.LK), ("lam", 2 * s.LK),
                        ("c31w", s.CK * 31), ("c31b", s.CK), ("clng", s.CK), ("clnb", s.CK),
                        ("bout", s.KD)):
            o[name] = off
            off += n
        s.so, s.NS = o, off


class Op:
    __slots__ = ("eng", "fn", "deps", "sig", "dkey", "waits", "needed")

    def __init__(s, eng, fn, dkey):
        s.eng, s.fn, s.dkey = eng, fn, dkey
        s.deps, s.sig, s.waits, s.needed = [], None, [], False


class Sched:
    def __init__(s, nc, stack):
        s.nc = nc
        s.stack = stack
        s.ops = {e: [] for e in ENGS}
        s.lastw, s.readers = {}, {}
        s.engsem = {e: stack.enter_context(nc.semaphore("es_" + e)) for e in ENGS if e != "sp"}
        s.engcnt = {e: 0 for e in s.engsem}
        s.dsem = {}
        s.seen = {e: {} for e in ENGS}
        s.first_phase = True
        s.nops = 0

    def add(s, eng, fn, r=(), w=(), dma=None):
        op = Op(eng, fn, dma)
        deps = []
        for k in r:
            lw = s.lastw.get(k)
            if lw is not None:
                deps.append(lw)
        for k in w:
            lw = s.lastw.get(k)
            if lw is not None:
                deps.append(lw)
            deps.extend(s.readers.get(k, ()))
        for k in r:
            s.readers.setdefault(k, []).append(op)
        for k in w:
            s.lastw[k] = op
            s.readers[k] = []
        seen = set()
        for d in deps:
            if d is op or id(d) in seen:
                continue
            seen.add(id(d))
            if d.dkey is None and d.eng == eng and (eng == "pe" or (not SAME_SYNC and eng != "pool")):
                continue
            op.deps.append(d)
            d.needed = True
        s.ops[eng].append(op)
        s.nops += 1
        return op

    def _dsem(s, key):
        if key not in s.dsem:
            s.dsem[key] = [s.stack.enter_context(s.nc.semaphore("ds%d" % len(s.dsem))), 0]
        return s.dsem[key]

    def flush(s, final=False):
        nc = s.nc
        pre = {e: [] for e in ENGS}
        if not s.first_phase:
            for e in ENGS:
                for e2, sem in s.engsem.items():
                    if e2 != e and s.engcnt[e2] > 0:
                        pre[e].append((sem, s.engcnt[e2]))
                for key, (sem, cnt) in s.dsem.items():
                    if cnt > 0:
                        pre[e].append((sem, cnt))
        s.first_phase = False
        for e in ENGS:
            lst = s.ops[e]
            lastc = max([i for i, op in enumerate(lst) if op.dkey is None], default=-1)
            for i, op in enumerate(lst):
                last = i == lastc
                if op.dkey is not None:
                    d = s._dsem(op.dkey)
                    d[1] += 16
                    op.sig = (d[0], d[1], 16)
                elif op.needed or last:
                    s.engcnt[e] += 1
                    op.sig = (s.engsem[e], s.engcnt[e], 1)
        for e in ENGS:
            seen = s.seen[e]
            for sem, v in pre[e]:
                seen[id(sem)] = max(seen.get(id(sem), 0), v)
            for op in s.ops[e]:
                for d in op.deps:
                    sem, v, _ = d.sig
                    if seen.get(id(sem), 0) >= v:
                        continue
                    seen[id(sem)] = v
                    op.waits.append((sem, v))
        post = []
        if final:
            post = [(sem, cnt) for (sem, cnt) in s.dsem.values() if cnt > 0]
            post += [(sem, s.engcnt[e2]) for e2, sem in s.engsem.items() if s.engcnt[e2] > 0]
        ops = s.ops

        def replay(eng_name):
            def run(e):
                for sem, v in pre[eng_name]:
                    e.wait_ge(sem, v)
                for op in ops[eng_name]:
                    for sem, v in op.waits:
                        e.wait_ge(sem, v)
                    ins = op.fn(e)
                    if op.sig is not None:
                        ins.then_inc(op.sig[0], op.sig[2])
                if eng_name == "sp":
                    for sem, v in post:
                        e.wait_ge(sem, v)
            return run

        with nc.Block() as block:
            block.tensor(replay("pe"))
            block.scalar(replay("act"))
            block.vector(replay("dve"))
            block.gpsimd(replay("pool"))
            block.sync(replay("sp"))
        s.ops = {e: [] for e in ENGS}
        s.lastw, s.readers = {}, {}


class Builder:
    def __init__(s, cfg):
        s.c = cfg

    def build(s):
        from contextlib import ExitStack
        c = s.c
        nc = bass.Bass("TRN2", target_bir_lowering=False)
        s.nc = nc
        L = c.DEPTH
        dt = nc.dram_tensor
        s.xin = dt("xin", [c.D, c.NT], F32, kind="ExternalInput").ap()
        s.cvec = dt("cvec", [128, c.KD * 2], F32, kind="ExternalInput").ap()
        s.small = dt("small", [L, 128, c.NS], F32, kind="ExternalInput").ap()
        s.wada = dt("wada", [L, 9 * c.KD, 128, c.D], F32, kind="ExternalInput").ap()
        s.w1 = dt("w1", [L, 2, 2 * c.FK, 128, c.D], F32, kind="ExternalInput").ap()
        s.w2 = dt("w2", [L, 2, c.KD, 128, c.FF], F32, kind="ExternalInput").ap()
        s.win = dt("win", [L, c.DIN // 128, 128, c.D], F32, kind="ExternalInput").ap()
        s.wout = dt("wout", [L, c.KD, 128, c.DMIX], F32, kind="ExternalInput").ap()
        s.gwt = dt("gwt", [L, 128, 2 * 2 * c.NH * c.HC * c.HD], F32, kind="ExternalInput").ap()
        s.identd = dt("ident", [128, 128], F32, kind="ExternalInput").ap()
        s.yout = dt("yout", [c.D, c.NT], F32, kind="ExternalOutput").ap()
        s.SA = dt("SA", [c.D, c.NT], F32, kind="Internal").ap()
        s.SB = dt("SB", [c.D, c.NT], F32, kind="Internal").ap()
        s.SC = dt("SC", [c.D, c.NT], F32, kind="Internal").ap()
        s.Uscr = dt("Uscr", [2, c.D, c.TB], F32, kind="Internal").ap()
        s.XR = dt("XR", [c.DLRU, c.NT], F32, kind="Internal").ap()
        s.GG = dt("GG", [c.DLRU, c.NT], F32, kind="Internal").ap()
        s.REC = dt("REC", [c.DLRU, c.NT], F32, kind="Internal").ap()
        s.YR = dt("YR", [c.DLRU, c.NT], BF16, kind="Internal").ap()
        s.YC = dt("YC", [c.DCONV, c.NT], BF16, kind="Internal").ap()

        with ExitStack() as st:
            s.st = st
            s.S = Sched(nc, st)
            s.ps = [st.enter_context(nc.psum_tensor("ps%d" % i, [128, 512], F32)) for i in range(8)]
            T_ = s.mkT(st)
            s.ones = T_("ones", [128, 128])
            s.cv32 = T_("cv32", [128, c.KD * 2])
            s.scb = T_("scb", [128, c.KD, 2], BF16)
            s.smallt = T_("smallt", [128, c.NS])
            s.modv = T_("modv", [128, 9 * c.KD, 2])
            s.bgv = T_("bgv", [128, c.KD, 2])
            s.clam = T_("clam", [128, 2 * c.LK])
            s.clam2 = T_("clam2", [128, 2 * c.LK])
            s.ctmp = [T_("ctmp%d" % i, [128, 2 * c.LK]) for i in range(4)]
            S = s.S
            s.ident = T_("ident", [128, 128])
            S.add("sp", lambda e: e.dma_start(out=s.ident[:], in_=s.identd), w=[("ident",)], dma=("ident",))
            S.add("dve", lambda e: e.memset(s.ones[:], 1.0), w=[("ones",)])
            S.add("sp", lambda e: e.dma_start(out=s.cv32[:], in_=s.cvec), w=[("cv32",)], dma=("cv32",))
            S.add("act", lambda e: e.activation(out=s.scb[:].rearrange("p k c -> p (k c)"), in_=s.cv32[:], func=AF.Silu),
                  r=[("cv32",)], w=[("scb",)])
            stop = getattr(c, "STOP", None)
            done = False
            for l in range(L):
                last = l == L - 1
                src_ = s.xin if l == 0 else s.SC
                steps = [("ada", lambda: s.phase_ada(l), None),
                         ("ffn0", lambda: s.phase_ffn(l, 0, src_, s.SA, final=(stop == (l, "ffn0"))), s.SA),
                         ("mixa", lambda: s.phase_mixa(l, s.SA), None),
                         ("scan", lambda: s.phase_scan(l, last), None),
                         ("mixc", lambda: s.phase_mixc(l, s.SA, s.SB), s.SB),
                         ("ffn1", lambda: s.phase_ffn(l, 1, s.SB, s.yout if last else s.SC, final=last), s.SC)]
                for name, fn, outstream in steps:
                    fn()
                    if stop == (l, name):
                        s.dbg_copy(outstream)
                        done = True
                        break
                if done:
                    break
        return nc

    def dbg_copy(s, stream):
        S = s.S
        c = s.c
        for j in range(c.KD):
            S.add("sp", lambda e, j=j: e.dma_start(out=s.yout[j * 128:(j + 1) * 128, :], in_=stream[j * 128:(j + 1) * 128, :]),
                  dma=("dbg", j))
        S.flush(final=True)

    def mkT(s, st):
        def T_(name, shape, dtp=F32):
            s.uid = getattr(s, "uid", 0) + 1
            return st.enter_context(s.nc.sbuf_tensor("%s_%d" % (name, s.uid), shape, dtp))
        return T_

    def sm(s, name, idx, n=1):
        o = s.c.so[name] + idx
        return s.smallt[:, o:o + n]

    def segs(s, blk):
        c = s.c
        lo, hi = blk * c.TB, (blk + 1) * c.TB
        out = []
        if lo < c.T:
            e_ = min(hi, c.T) - lo
            if getattr(c, "SPLIT", False) and e_ == c.TB:
                out.append((0, 512, 0))
                out.append((512, e_, 0))
            else:
                out.append((0, e_, 0))
        if hi > c.T:
            out.append((max(lo, c.T) - lo, c.TB, 1))
        return out

    def phase_ada(s, l):
        c, S, nc = s.c, s.S, s.nc
        NJ = 9 * c.KD
        with nc.sbuf_tensor("wa_%d" % l, [128, 6, c.D], BF16) as wa:
            S.add("sp", lambda e: e.dma_start(out=s.smallt[:], in_=s.small[l]), w=[("small",)], dma=("small",))
            for j in range(NJ):
                sl = j % 6
                S.add("pool", lambda e, j=j, sl=sl: e.dma_start(out=wa[:, sl, :], in_=s.wada[l, j], max_dma_last_dim=8192),
                      w=[("wa", sl)], dma=("wa", sl))
                for k in range(c.KD):
                    S.add("pe", lambda e, j=j, k=k, sl=sl: e.matmul(s.ps[0][:, 2 * j:2 * j + 2], wa[:, sl, k * 128:(k + 1) * 128],
                                                                  s.scb[:, k, :], start=(k == 0), stop=(k == c.KD - 1)),
                          r=[("wa", sl), ("scb",)], w=[("ps", 0)])
            psv = s.ps[0][:, 0:2 * NJ].rearrange("p (j c) -> p j c", c=2)
            for col in range(2):
                S.add("dve", lambda e, col=col: e.tensor_tensor(out=s.modv[:, :, col], in0=psv[:, :, col],
                                                                in1=s.sm("bada", 0, NJ), op=ALU.add),
                      r=[("ps", 0), ("small",)], w=[("modv",)])
            for sub in range(3):
                coef = (1.0 if sub == 1 else 0.5) / c.ALPHA
                S.add("dve", lambda e, sub=sub: e.tensor_scalar(s.modv[:, (3 * sub + 1) * c.KD:(3 * sub + 2) * c.KD, :],
                                                                s.modv[:, (3 * sub + 1) * c.KD:(3 * sub + 2) * c.KD, :],
                                                                1.0, None, ALU.add), r=[("modv",)], w=[("modv",)])
                S.add("dve", lambda e, sub=sub, coef=coef: e.tensor_scalar(s.modv[:, (3 * sub + 2) * c.KD:(3 * sub + 3) * c.KD, :],
                                                                           s.modv[:, (3 * sub + 2) * c.KD:(3 * sub + 3) * c.KD, :],
                                                                           coef, None, ALU.mult), r=[("modv",)], w=[("modv",)])
            for col in range(2):
                S.add("dve", lambda e, col=col: e.tensor_tensor(out=s.bgv[:, :, col], in0=s.modv[:, 5 * c.KD:6 * c.KD, col],
                                                                in1=s.sm("bout", 0, c.KD), op=ALU.mult),
                      r=[("modv",), ("small",)], w=[("bgv",)])
            t0, t1, t2, t3 = s.ctmp
            lam = s.sm("lam", 0, 2 * c.LK)
            K = ("clamk",)
            S.add("act", lambda e: e.activation(out=t0[:], in_=lam, func=AF.Exp, scale=-1.0), r=[("small",)], w=[K])
            S.add("act", lambda e: e.activation(out=t1[:], in_=t0[:], func=AF.Ln, bias=1.0), r=[K], w=[K])
            S.add("dve", lambda e: e.tensor_scalar(t2[:], t0[:], 0.05, None, ALU.min), r=[K], w=[K])
            S.add("dve", lambda e: e.tensor_scalar(t3[:], t2[:], 0.2, -0.25, ALU.mult, ALU.add), r=[K], w=[K])
            for cst in (1.0 / 3.0, -0.5, 1.0):
                S.add("dve", lambda e: e.tensor_tensor(out=t3[:], in0=t3[:], in1=t2[:], op=ALU.mult), r=[K], w=[K])
                S.add("dve", lambda e, cst=cst: e.tensor_scalar(t3[:], t3[:], cst, None, ALU.add), r=[K], w=[K])
            S.add("dve", lambda e: e.tensor_tensor(out=t3[:], in0=t3[:], in1=t2[:], op=ALU.mult), r=[K], w=[K])
            S.add("dve", lambda e: e.tensor_scalar(t2[:], t0[:], 0.05, None, ALU.is_lt), r=[K], w=[K])
            S.add("dve", lambda e: e.tensor_tensor(out=t3[:], in0=t3[:], in1=t1[:], op=ALU.subtract), r=[K], w=[K])
            S.add("dve", lambda e: e.tensor_tensor(out=t3[:], in0=t3[:], in1=t2[:], op=ALU.mult), r=[K], w=[K])
            S.add("dve", lambda e: e.tensor_tensor(out=t3[:], in0=t3[:], in1=t1[:], op=ALU.add), r=[K], w=[K])
            S.add("dve", lambda e: e.tensor_scalar(s.clam[:], t3[:], -RG_C, None, ALU.mult), r=[K], w=[("clam",)])
            S.add("dve", lambda e: e.tensor_scalar(s.clam2[:], t3[:], -2.0 * RG_C, None, ALU.mult), r=[K], w=[("clam",)])
            S.flush()

    def modulate(s, src, blk, sub, xmod, xs, nslot, cnt):
        c, S = s.c, s.S
        for k in range(c.KD):
            sl = cnt[0] % nslot
            cnt[0] += 1
            S.add("sp", lambda e, k=k, sl=sl: e.dma_start(out=xs[:, sl, :], in_=src[k * 128:(k + 1) * 128, blk * c.TB:(blk + 1) * c.TB]),
                  w=[("xs", sl)], dma=("xs", sl))
            for (lo, hi, which) in s.segs(blk):
                sc1 = s.modv[:, (3 * sub + 1) * c.KD + k, which:which + 1]
                sh = s.modv[:, (3 * sub) * c.KD + k, which:which + 1]
                if k % 2 == 0:
                    S.add("act", lambda e, k=k, sl=sl, lo=lo, hi=hi, sc1=sc1, sh=sh: e.activation(
                        out=xmod[:, k, lo:hi], in_=xs[:, sl, lo:hi], func=AF.Identity, scale=sc1, bias=sh),
                        r=[("xs", sl), ("modv",)], w=[("R1", k)])
                else:
                    S.add("dve", lambda e, k=k, sl=sl, lo=lo, hi=hi, sc1=sc1, sh=sh: e.tensor_scalar(
                        xmod[:, k, lo:hi], xs[:, sl, lo:hi], sc1, sh, ALU.mult, ALU.add),
                        r=[("xs", sl), ("modv",)], w=[("R1", k)])

    def ln_tail(s, u32, nch, Dn, eps, lnt, emit_out, ukeys=None):
        c, S = s.c, s.S
        (mean, kmean), (rstd, krstd), (nmr, knmr), (tmp, ktmp) = lnt
        if ukeys is None:
            ukeys = lambda j: [("u", j)]
        H = c.HALF
        inv = 1.0 / Dn
        for h in range(2):
            cs = slice(h * H, (h + 1) * H)
            S.add("act", lambda e, h=h, cs=cs: e.activation(out=mean[:, cs], in_=s.ps[4 + h][:, 0:H], func=AF.Identity, scale=inv),
                  r=[("ps", 4 + h)], w=[kmean])
            S.add("dve", lambda e, cs=cs: e.tensor_tensor(out=tmp[:, cs], in0=mean[:, cs], in1=mean[:, cs], op=ALU.mult),
                  r=[kmean], w=[ktmp])
            S.add("dve", lambda e, h=h, cs=cs: e.scalar_tensor_tensor(out=tmp[:, cs], in0=s.ps[6 + h][:, 0:H], scalar=inv, in1=tmp[:, cs],
                                                                      op0=ALU.mult, op1=ALU.subtract),
                  r=[("ps", 6 + h), ktmp], w=[ktmp])
            S.add("act", lambda e, cs=cs: e.activation(out=tmp[:, cs], in_=tmp[:, cs], func=AF.Sqrt, bias=s.epst[eps][:, 0:1]),
                  r=[ktmp, ("eps", eps)], w=[ktmp])
            S.add("dve", lambda e, cs=cs: e.reciprocal(rstd[:, cs], tmp[:, cs]), r=[ktmp], w=[krstd])
            S.add("dve", lambda e, cs=cs: e.scalar_tensor_tensor(out=nmr[:, cs], in0=mean[:, cs], scalar=-1.0, in1=rstd[:, cs],
                                                                 op0=ALU.mult, op1=ALU.mult),
                  r=[kmean, krstd], w=[knmr])
        for j in range(nch):
            S.add("dve", lambda e, j=j: e.tensor_tensor(out=u32[:, j, :], in0=u32[:, j, :], in1=rstd[:, :], op=ALU.mult),
                  r=ukeys(j) + [krstd], w=ukeys(j))
            S.add("dve", lambda e, j=j: e.tensor_tensor(out=u32[:, j, :], in0=u32[:, j, :], in1=nmr[:, :], op=ALU.add),
                  r=ukeys(j) + [knmr], w=ukeys(j))
            emit_out(j)

    def stats_mm(s, j, nch, src_u, src_sq, ukey, sqkey):
        c, S = s.c, s.S
        H = c.HALF
        for h in range(2):
            S.add("pe", lambda e, h=h: e.matmul(s.ps[4 + h][:, 0:H], s.ones[:], src_u[:, h * H:(h + 1) * H], start=(j == 0), stop=(j == nch - 1)),
                  r=[ukey, ("ones",)], w=[("ps", 4 + h)])
            S.add("pe", lambda e, h=h: e.matmul(s.ps[6 + h][:, 0:H], s.ones[:], src_sq[:, h * H:(h + 1) * H], start=(j == 0), stop=(j == nch - 1)),
                  r=[sqkey, ("ones",)], w=[("ps", 6 + h)])

    def u_keys(s, j):
        return [("R1", 2 * j), ("R1", 2 * j + 1), ("u", j)]

    def eps_tiles(s, st):
        c = s.c
        s.epst = {}
        for name, val in (("ffn", EPS / (c.ALPHA ** 2)), ("conv", EPS)):
            t = s.mkT(st)("eps_" + name, [128, 1], F32)
            s.S.add("dve", lambda e, t=t, val=val: e.memset(t[:], val), w=[("eps", name)])
            s.epst[name] = t

    def phase_ffn(s, l, which_ffn, src, dst, final=False):
        from contextlib import ExitStack
        c, S, nc = s.c, s.S, s.nc
        sub = 0 if which_ffn == 0 else 2
        H, TB, KD, FK = c.HALF, c.TB, c.KD, c.FK
        NW1 = 3
        U = s.Uscr
        with ExitStack() as st:
            T_ = s.mkT(st)
            xm = T_("xm", [128, 2, KD, TB], BF16)
            act = T_("act", [128, FK, TB], BF16)
            w1t = T_("w1t", [128, NW1, 2, c.D], BF16)
            w2t = T_("w2t", [128, 2, c.FF], BF16)
            xs = T_("xs", [128, 2, TB])
            uo = T_("uo", [128, 2, TB])
            sq = T_("sq", [128, 2, TB])
            nin = T_("nin", [128, 2, TB])
            sgt = T_("sgt", [128, 2, H])
            lr = T_("lr", [128, 2, TB])
            lnn = T_("lnn", [128, 2, TB])
            s.eps_tiles(st)
            cn = {"x": 0, "w1": 0, "w2": 0, "sg": 0, "n": 0}

            def modulate(blk):
                par = blk % 2
                for k in range(KD):
                    sl = cn["x"] % 2
                    cn["x"] += 1
                    S.add("sp", lambda e, k=k, sl=sl: e.dma_start(out=xs[:, sl, :], in_=src[k * 128:(k + 1) * 128, blk * TB:(blk + 1) * TB]),
                          w=[("xs", sl)], dma=("xs", sl))
                    for (lo, hi, which) in s.segs(blk):
                        sc1 = s.modv[:, (3 * sub + 1) * KD + k, which:which + 1]
                        sh = s.modv[:, (3 * sub) * KD + k, which:which + 1]
                        if k % 2 == 0:
                            S.add("act", lambda e, k=k, sl=sl, lo=lo, hi=hi, sc1=sc1, sh=sh: e.activation(
                                out=xm[:, par, k, lo:hi], in_=xs[:, sl, lo:hi], func=AF.Identity, scale=sc1, bias=sh),
                                r=[("xs", sl), ("modv",)], w=[("xm", par, k)])
                        else:
                            S.add("dve", lambda e, k=k, sl=sl, lo=lo, hi=hi, sc1=sc1, sh=sh: e.tensor_scalar(
                                xm[:, par, k, lo:hi], xs[:, sl, lo:hi], sc1, sh, ALU.mult, ALU.add),
                                r=[("xs", sl), ("modv",)], w=[("xm", par, k)])

            def norm_chunk(blk, j):
                par = blk % 2
                o = cn["n"] % 2
                cn["n"] += 1
                S.add("sp", lambda e: e.dma_start(out=nin[:, o, :], in_=U[par, j * 128:(j + 1) * 128, :]),
                      r=[("U", par, j)], w=[("nin", o)], dma=("nin", o))
                S.add("dve", lambda e: e.tensor_tensor(out=nin[:, o, :], in0=nin[:, o, :], in1=lr[:, par, :], op=ALU.mult),
                      r=[("nin", o), ("lr", par)], w=[("nin", o)])
                S.add("dve", lambda e: e.tensor_tensor(out=nin[:, o, :], in0=nin[:, o, :], in1=lnn[:, par, :], op=ALU.add),
                      r=[("nin", o), ("lnn", par)], w=[("nin", o)])
                S.add("act", lambda e: e.activation(out=nin[:, o, :], in_=nin[:, o, :], func=AF.Identity,
                                                    scale=s.sm("lng", sub * KD + j), bias=s.sm("lnb", sub * KD + j)),
                      r=[("nin", o), ("small",)], w=[("nin", o)])
                S.add(STQ, lambda e: e.dma_start(out=dst[j * 128:(j + 1) * 128, blk * TB:(blk + 1) * TB], in_=nin[:, o, :]),
                      r=[("nin", o)], dma=("nin", "st", o))

            def h_phase(blk):
                par = blk % 2
                for f in range(FK):
                    sl = cn["w1"] % NW1
                    cn["w1"] += 1
                    for gu in range(2):
                        S.add("pool", lambda e, f=f, sl=sl, gu=gu: e.dma_start(out=w1t[:, sl, gu, :], in_=s.w1[l, which_ffn, gu * FK + f],
                                                                              max_dma_last_dim=8192),
                              w=[("w1", sl, gu)], dma=("w1", sl, gu))
                    bs = 4 * (f % 2)
                    for gu in range(2):
                        for k in range(KD):
                            for h in range(2):
                                S.add("pe", lambda e, sl=sl, gu=gu, k=k, h=h, bs=bs: e.matmul(
                                    s.ps[bs + 2 * gu + h][:, 0:H], w1t[:, sl, gu, k * 128:(k + 1) * 128], xm[:, par, k, h * H:(h + 1) * H],
                                    start=(k == 0), stop=(k == KD - 1)),
                                    r=[("w1", sl, gu), ("xm", par, k)], w=[("ps", bs + 2 * gu + h)])
                    for h in range(2):
                        g = cn["sg"] % 2
                        cn["sg"] += 1
                        S.add("act", lambda e, g=g, h=h, bs=bs: e.activation(out=sgt[:, g, :], in_=s.ps[bs + h][:, 0:H], func=AF.Silu),
                              r=[("ps", bs + h)], w=[("sg", g)])
                        S.add("dve", lambda e, g=g, h=h, bs=bs, f=f: e.tensor_tensor(out=act[:, f, h * H:(h + 1) * H], in0=s.ps[bs + 2 + h][:, 0:H],
                                                                                     in1=sgt[:, g, :], op=ALU.mult),
                              r=[("ps", bs + 2 + h), ("sg", g)], w=[("act", f)])
                    if blk > 0 and f < KD:
                        norm_chunk(blk - 1, f)

            def y_phase(blk):
                par = blk % 2
                segs = s.segs(blk)
                pending = None
                for j in range(KD):
                    sl = cn["w2"] % 2
                    cn["w2"] += 1
                    S.add("pool", lambda e, j=j, sl=sl: e.dma_start(out=w2t[:, sl, :], in_=s.w2[l, which_ffn, j], max_dma_last_dim=8192),
                          w=[("w2", sl)], dma=("w2", sl))
                    xl = cn["x"] % 2
                    cn["x"] += 1
                    S.add("sp", lambda e, j=j, xl=xl: e.dma_start(out=xs[:, xl, :], in_=src[j * 128:(j + 1) * 128, blk * TB:(blk + 1) * TB]),
                          w=[("xs", xl)], dma=("xs", xl))
                    bs = 2 * (j % 2)
                    for f in range(FK):
                        for h in range(2):
                            S.add("pe", lambda e, sl=sl, f=f, h=h, bs=bs: e.matmul(
                                s.ps[bs + h][:, 0:H], w2t[:, sl, f * 128:(f + 1) * 128], act[:, f, h * H:(h + 1) * H],
                                start=(f == 0), stop=(f == FK - 1)),
                                r=[("w2", sl), ("act", f)], w=[("ps", bs + h)])
                    q = j % 2
                    for (lo, hi, which) in segs:
                        gcol = s.modv[:, (3 * sub + 2) * KD + j, which:which + 1]
                        for h in range(2):
                            a0, a1 = max(lo, h * H), min(hi, (h + 1) * H)
                            if a0 >= a1:
                                continue
                            S.add("dve", lambda e, h=h, a0=a0, a1=a1, gcol=gcol, xl=xl, bs=bs, q=q: e.scalar_tensor_tensor(
                                out=uo[:, q, a0:a1], in0=s.ps[bs + h][:, a0 - h * H:a1 - h * H], scalar=gcol, in1=xs[:, xl, a0:a1],
                                op0=ALU.mult, op1=ALU.add),
                                r=[("ps", bs + h), ("xs", xl), ("modv",)], w=[("uo", q)])
                    S.add("act", lambda e, q=q: e.activation(out=sq[:, q, :], in_=uo[:, q, :], func=AF.Square),
                          r=[("uo", q)], w=[("sq", q)])
                    S.add("sp", lambda e, j=j, q=q: e.dma_start(out=U[par, j * 128:(j + 1) * 128, :], in_=uo[:, q, :]),
                          r=[("uo", q)], w=[("U", par, j)], dma=("uo", q))
                    if j == 0:
                        S.add("dve", lambda e, q=q: e.tensor_copy(lr[:, par, :], uo[:, q, :]), r=[("uo", q)], w=[("lr", par)])
                        S.add("dve", lambda e, q=q: e.tensor_copy(lnn[:, par, :], sq[:, q, :]), r=[("sq", q)], w=[("lnn", par)])
                    else:
                        S.add("dve", lambda e, q=q: e.tensor_tensor(out=lr[:, par, :], in0=lr[:, par, :], in1=uo[:, q, :], op=ALU.add),
                              r=[("uo", q), ("lr", par)], w=[("lr", par)])
                        S.add("dve", lambda e, q=q: e.tensor_tensor(out=lnn[:, par, :], in0=lnn[:, par, :], in1=sq[:, q, :], op=ALU.add),
                              r=[("sq", q), ("lnn", par)], w=[("lnn", par)])
                s.stats_mm(0, 1, lr[:, par, :], lnn[:, par, :], ("lr", par), ("lnn", par))
                lnt = [(sq[:, 1, :], ("sq", 1)), (lr[:, par, :], ("lr", par)), (lnn[:, par, :], ("lnn", par)), (sq[:, 0, :], ("sq", 0))]
                s.ln_tail(None, 0, c.D, "ffn", lnt, None)

            modulate(0)
            for blk in range(c.NB):
                h_phase(blk)
                if blk + 1 < c.NB:
                    modulate(blk + 1)
                y_phase(blk)
            for j in range(KD):
                norm_chunk(c.NB - 1, j)
            S.flush(final=final)

    def phase_mixa(s, l, src):
        from contextlib import ExitStack
        c, S, nc = s.c, s.S, s.nc
        H, TB, KD, LK, CK = c.HALF, c.TB, c.KD, c.LK, c.CK
        NR = TB // 64
        PW = 64 + 30
        PL = NR * PW
        with ExitStack() as st:
            T_ = s.mkT(st)
            xmod = T_("hmod", [128, KD, TB], BF16)
            xs = T_("xs", [128, 3, TB])
            wt = T_("wint", [128, 4, c.D], BF16)
            xo = T_("xo", [128, 3, TB])
            sig = T_("sig", [128, 2, TB])
            upad = T_("upad", [128, CK, PL], BF16)
            dg = T_("dg", [128, CK, c.K31, 128], BF16)
            acc = T_("acc", [128, CK, TB])
            sq = T_("sq", [128, 2, TB])
            yo = T_("yo", [128, 2, TB], BF16)
            lnt = [(T_("lnm", [128, TB]), ("ln", "mean")), (T_("lnr", [128, TB]), ("ln", "rstd")),
                   (T_("lnn", [128, TB]), ("ln", "nmr")), (T_("lnt", [128, TB]), ("ln", "tmp"))]
            s.eps_tiles(st)
            xcnt = [0]
            wcnt = 0
            xocnt = 0
            for cc in range(CK):
                for kk in range(c.K31):
                    wk = s.sm("c31w", cc * 31 + kk)
                    if (cc * 31 + kk) % 2 == 0:
                        S.add("dve", lambda e, cc=cc, kk=kk, wk=wk: e.tensor_scalar(dg[:, cc, kk, :], s.ident[:], wk, None, ALU.mult),
                              r=[("ident",), ("small",)], w=[("dg", cc)])
                    else:
                        S.add("act", lambda e, cc=cc, kk=kk, wk=wk: e.activation(out=dg[:, cc, kk, :], in_=s.ident[:], func=AF.Identity, scale=wk),
                              r=[("ident",), ("small",)], w=[("dg", cc)])

            ocnt = [0]

            def tail_chunk(blk, j):
                cols = slice(blk * TB, (blk + 1) * TB)
                o = ocnt[0] % 2
                ocnt[0] += 1
                rstd_t, krstd = lnt[1]
                nmr_t, knmr = lnt[2]
                S.add("dve", lambda e: e.tensor_tensor(out=acc[:, j, :], in0=acc[:, j, :], in1=rstd_t[:, :], op=ALU.mult),
                      r=[("acc", j), krstd], w=[("acc", j)])
                S.add("dve", lambda e: e.tensor_tensor(out=acc[:, j, :], in0=acc[:, j, :], in1=nmr_t[:, :], op=ALU.add),
                      r=[("acc", j), knmr], w=[("acc", j)])
                S.add("act", lambda e: e.activation(out=yo[:, o, :], in_=acc[:, j, :], func=AF.Silu,
                                                    scale=s.sm("clng", j), bias=s.sm("clnb", j)),
                      r=[("acc", j), ("small",)], w=[("yo", o)])
                S.add(STQ, lambda e: e.dma_start(out=s.YC[j * 128:(j + 1) * 128, cols], in_=yo[:, o, :]),
                      r=[("yo", o)], dma=("yo", "st", o))

            def body(blk):
                nonlocal wcnt, xocnt
                segs = s.segs(blk)
                cols = slice(blk * TB, (blk + 1) * TB)
                has_ctx = any(w == 1 for (_, _, w) in segs)
                nlat = segs[0][1] if segs[0][2] == 0 else 0
                nrl = nlat // 64
                if blk == 0 or has_ctx:
                    S.add("dve", lambda e: e.memset(upad[:], 0.0), w=[("upad", cc) for cc in range(CK)])
                order = list(range(2 * LK))
                for cc in range(CK):
                    order += [2 * LK + CK + cc, 2 * LK + cc]
                for oi, o in enumerate(order):
                    sl = wcnt % 4
                    bs = 2 * (wcnt % 2)
                    wcnt += 1
                    S.add("pool", lambda e, o=o, sl=sl: e.dma_start(out=wt[:, sl, :], in_=s.win[l, o], max_dma_last_dim=8192),
                          w=[("win", sl)], dma=("win", sl))
                    for k in range(KD):
                        for h in range(2):
                            S.add("pe", lambda e, sl=sl, k=k, h=h, bs=bs: e.matmul(
                                s.ps[bs + h][:, 0:H], wt[:, sl, k * 128:(k + 1) * 128], xmod[:, k, h * H:(h + 1) * H],
                                start=(k == 0), stop=(k == KD - 1)),
                                r=[("win", sl), ("R1", k)], w=[("ps", bs + h)])
                    if oi == len(order) - 1 and blk + 1 < c.NB:
                        s.modulate(src, blk + 1, 1, xmod, xs, 3, xcnt)
                    if blk > 0 and oi < CK:
                        tail_chunk(blk - 1, oi)
                    if o < 2 * LK:
                        xl = xocnt % 3
                        xocnt += 1
                        isg = o >= LK
                        for h in range(2):
                            if isg:
                                S.add("act", lambda e, xl=xl, h=h, bs=bs: e.activation(out=xo[:, xl, h * H:(h + 1) * H], in_=s.ps[bs + h][:, 0:H], func=AF.Gelu),
                                      r=[("ps", bs + h)], w=[("xo", xl)])
                            else:
                                S.add("act", lambda e, xl=xl, h=h, bs=bs: e.activation(out=xo[:, xl, h * H:(h + 1) * H], in_=s.ps[bs + h][:, 0:H], func=AF.Identity),
                                      r=[("ps", bs + h)], w=[("xo", xl)])
                        dstt = s.GG if isg else s.XR
                        ch = o - LK if isg else o
                        S.add(STQ, lambda e, xl=xl, dstt=dstt, ch=ch: e.dma_start(out=dstt[ch * 128:(ch + 1) * 128, cols], in_=xo[:, xl, :]),
                              r=[("xo", xl)], dma=("xo", "st", xl))
                    elif o >= 2 * LK + CK:
                        cc = o - 2 * LK - CK
                        sgl = cc % 2
                        for h in range(2):
                            S.add("act", lambda e, sgl=sgl, h=h, bs=bs: e.activation(out=sig[:, sgl, h * H:(h + 1) * H], in_=s.ps[bs + h][:, 0:H], func=AF.Sigmoid),
                                  r=[("ps", bs + h)], w=[("sig", sgl)])
                    else:
                        cc = o - 2 * LK
                        sgl = cc % 2
                        for h in range(2):
                            r0, r1 = h * (H // 64), min((h + 1) * (H // 64), nrl)
                            if r1 > r0:
                                n = (r1 - r0) * 64
                                outv = upad[:, cc, r0 * PW:r1 * PW].rearrange("p (r w) -> p r w", w=PW)[:, :, 15:79]
                                S.add("dve", lambda e, outv=outv, n=n, h=h, bs=bs, sgl=sgl: e.tensor_tensor(
                                    out=outv, in0=s.ps[bs + h][:, 0:n].rearrange("p (r w) -> p r w", w=64),
                                    in1=sig[:, sgl, h * H:h * H + n].rearrange("p (r w) -> p r w", w=64), op=ALU.mult),
                                    r=[("ps", bs + h), ("sig", sgl)], w=[("upad", cc)])
                            if has_ctx and h == 1:
                                a0 = nlat - H
                                S.add("dve", lambda e, a0=a0, bs=bs, sgl=sgl, cc=cc: e.tensor_tensor(
                                    out=upad[:, cc, nrl * PW + 15:nrl * PW + 15 + c.TC], in0=s.ps[bs + 1][:, a0:a0 + c.TC],
                                    in1=sig[:, sgl, H + a0:H + a0 + c.TC], op=ALU.mult),
                                    r=[("ps", bs + 1), ("sig", sgl)], w=[("upad", cc)])
                        cb = 2 * (wcnt % 2)
                        wcnt += 1
                        R6 = H // 64
                        for h in range(2):
                            r0, r1 = h * R6, min((h + 1) * R6, nrl)
                            if r1 > r0:
                                n = (r1 - r0) * 64
                                for kk in range(c.K31):
                                    rv = upad[:, cc, r0 * PW:r1 * PW].rearrange("p (r w) -> p r w", w=PW)[:, :, kk:kk + 64]
                                    S.add("pe", lambda e, cc=cc, kk=kk, h=h, n=n, rv=rv, cb=cb: e.matmul(
                                        s.ps[cb + h][:, 0:n].rearrange("p (r w) -> p r w", w=64), dg[:, cc, kk, :], rv,
                                        start=(kk == 0), stop=(kk == c.K31 - 1)),
                                        r=[("dg", cc), ("upad", cc)], w=[("ps", cb + h)])
                            if has_ctx and h == 1:
                                a0 = nlat - H
                                for kk in range(c.K31):
                                    S.add("pe", lambda e, cc=cc, kk=kk, a0=a0, cb=cb: e.matmul(
                                        s.ps[cb + 1][:, a0:a0 + c.TC], dg[:, cc, kk, :], upad[:, cc, nrl * PW + kk:nrl * PW + kk + c.TC],
                                        start=(kk == 0), stop=(kk == c.K31 - 1)),
                                        r=[("dg", cc), ("upad", cc)], w=[("ps", cb + 1)])
                        q = cc % 2
                        for h in range(2):
                            S.add("act", lambda e, cc=cc, h=h, cb=cb: e.activation(out=acc[:, cc, h * H:(h + 1) * H], in_=s.ps[cb + h][:, 0:H], func=AF.Identity,
                                                                                   bias=s.sm("c31b", cc)),
                                  r=[("ps", cb + h), ("small",)], w=[("acc", cc)])
                            S.add("act", lambda e, cc=cc, h=h, cb=cb, q=q: e.activation(out=sq[:, q, h * H:(h + 1) * H], in_=s.ps[cb + h][:, 0:H], func=AF.Square,
                                                                                        bias=s.sm("c31b", cc)),
                                  r=[("ps", cb + h), ("small",)], w=[("sq", q)])
                        if cc == 0:
                            S.add("dve", lambda e, cc=cc: e.tensor_copy(lnt[1][0][:, :], acc[:, cc, :]), r=[("acc", cc)], w=[lnt[1][1]])
                            S.add("dve", lambda e, q=q: e.tensor_copy(lnt[2][0][:, :], sq[:, q, :]), r=[("sq", q)], w=[lnt[2][1]])
                        else:
                            S.add("dve", lambda e, cc=cc: e.tensor_tensor(out=lnt[1][0][:, :], in0=lnt[1][0][:, :], in1=acc[:, cc, :], op=ALU.add),
                                  r=[("acc", cc), lnt[1][1]], w=[lnt[1][1]])
                            S.add("dve", lambda e, q=q: e.tensor_tensor(out=lnt[2][0][:, :], in0=lnt[2][0][:, :], in1=sq[:, q, :], op=ALU.add),
                                  r=[("sq", q), lnt[2][1]], w=[lnt[2][1]])
                        if cc == CK - 1:
                            s.stats_mm(0, 1, lnt[1][0][:, :], lnt[2][0][:, :], lnt[1][1], lnt[2][1])
                s.ln_tail(None, 0, c.DCONV, "conv", lnt, None)
            s.modulate(src, 0, 1, xmod, xs, 3, xcnt)
            for blk_ in range(c.NB):
                body(blk_)
            for j in range(CK):
                tail_chunk(c.NB - 1, j)
            S.flush()

    def phase_scan(s, l, last):
        from contextlib import ExitStack
        c, S, nc = s.c, s.S, s.nc
        LK, TS, NH, HC, HD = c.LK, c.TS, c.NH, c.HC, c.HD
        with ExitStack() as st:
            T_ = s.mkT(st)
            gw = T_("gw", [128, 2 * 2 * NH * HC * HD], BF16)
            xrh = T_("xrh", [128, 2, LK, TS + 3])
            xc = T_("xc", [128, 2, LK, TS])
            xcb = T_("xcb", [128, 2, LK, TS], BF16)
            Rt = T_("Rt", [128, LK, TS])
            It = T_("It", [128, LK, TS])
            At = T_("At", [128, LK, TS])
            St = T_("St", [128, LK, TS])
            hh = T_("hh", [128, 3, TS])
            rf = T_("rf", [128, LK, TS])
            gg = T_("gg", [128, LK, TS])
            yr = T_("yr", [128, 2, TS], BF16)
            state = T_("state", [128, LK])
            S.add("pool", lambda e: e.dma_start(out=gw[:], in_=s.gwt[l], max_dma_last_dim=8192), w=[("gw",)], dma=("gw",))

            def gwv(d, g, hd, jc, ic):
                base = (((d * 2 + g) * NH + hd) * HC + jc) * HD + ic * 128
                return gw[:, base:base + 128]
            XRv = s.XR.rearrange("(c p) t -> p c t", p=128)
            RECv = s.REC.rearrange("(c p) t -> p c t", p=128)
            GGv = s.GG.rearrange("(c p) t -> p c t", p=128)
            cnt = {"x": 0, "h": 0, "rf": 0, "yr": 0, "cb": 0}
            seq_ctx = (c.T, c.TC)
            seq_lat = (0, c.T)

            def partA(i, d, off, ln, is_ctx, b):
                sl = i % 2
                lo = b * TS
                n = min(TS, ln - lo)
                xl = cnt["x"] % 2
                cnt["x"] += 1
                a0 = max(lo - 2, 0)
                a1 = min(lo + n + 1, ln)
                d0 = a0 - (lo - 2)
                if d0 > 0:
                    S.add("dve", lambda e: e.memset(xrh[:, xl, :, 0:d0], 0.0), w=[("xrh", xl)])
                if a1 < lo + n + 1:
                    S.add("dve", lambda e: e.memset(xrh[:, xl, :, n + 2:n + 3], 0.0), w=[("xrh", xl)])
                S.add("sp", lambda e: e.dma_start(out=xrh[:, xl, :, d0:d0 + (a1 - a0)], in_=XRv[:, :, off + a0:off + a1]),
                      w=[("xrh", xl)], dma=("xrh", xl))
                for ch in range(LK):
                    S.add("act", lambda e, ch=ch: e.activation(out=xc[:, sl, ch, 0:n], in_=xrh[:, xl, ch, 0:n], func=AF.Identity,
                                                               scale=s.sm("c4w", ch * 4 + 0), bias=s.sm("c4b", ch)),
                          r=[("xrh", xl), ("small",)], w=[("xc", sl, ch)])
                    for kk in range(1, 4):
                        S.add("dve", lambda e, ch=ch, kk=kk: e.scalar_tensor_tensor(
                            out=xc[:, sl, ch, 0:n], in0=xrh[:, xl, ch, kk:kk + n], scalar=s.sm("c4w", ch * 4 + kk), in1=xc[:, sl, ch, 0:n],
                            op0=ALU.mult, op1=ALU.add), r=[("xrh", xl), ("small",)], w=[("xc", sl, ch)])
                    S.add("pool", lambda e, ch=ch: e.tensor_copy(xcb[:, sl, ch, 0:n], xc[:, sl, ch, 0:n]),
                          r=[("xc", sl, ch)], w=[("xcb", sl, ch)])

            def partB(i, d, off, ln, is_ctx, b):
                sl = i % 2
                lo = b * TS
                n = min(TS, ln - lo)
                if d == 1 and not (is_ctx and last):
                    S.add("sp", lambda e: e.dma_start(out=rf[:, :, 0:n], in_=RECv[:, :, off + lo:off + lo + n]), w=[("rf", ch) for ch in range(LK)], dma=("rf",))
                    S.add("sp", lambda e: e.dma_start(out=gg[:, :, 0:n], in_=GGv[:, :, off + lo:off + lo + n]), w=[("gg", ch) for ch in range(LK)], dma=("gg",))

                for hf in range(2 if LK >= 4 else 1):
                    chs = range(hf * (LK // 2), (hf + 1) * (LK // 2)) if LK >= 4 else range(LK)
                    for g, dstt, bname in ((0, Rt, "brg"), (1, It, "big")):
                        for ch in chs:
                            hd, jc = ch // HC, ch % HC
                            bank = ch % 8
                            for ic in range(HC):
                                S.add("pe", lambda e, g=g, hd=hd, jc=jc, ic=ic, bank=bank: e.matmul(
                                    s.ps[bank][:, 0:n], gwv(d, g, hd, jc, ic), xcb[:, sl, hd * HC + ic, 0:n], start=(ic == 0), stop=(ic == HC - 1)),
                                    r=[("gw",), ("xcb", sl, hd * HC + ic)], w=[("ps", bank)])
                            S.add("act", lambda e, ch=ch, bank=bank, dstt=dstt, bname=bname: e.activation(
                                out=dstt[:, ch, 0:n], in_=s.ps[bank][:, 0:n], func=AF.Sigmoid, bias=s.sm(bname, d * LK + ch)),
                                r=[("ps", bank), ("small",)], w=[("g%d" % g, ch)])
                    for ch in chs:
                        S.add("act", lambda e, ch=ch: e.activation(out=At[:, ch, 0:n], in_=Rt[:, ch, 0:n], func=AF.Exp,
                                                                   scale=s.clam[:, d * LK + ch:d * LK + ch + 1]),
                              r=[("g0", ch), ("clam",)], w=[("A", ch)])
                        S.add("act", lambda e, ch=ch: e.activation(out=St[:, ch, 0:n], in_=Rt[:, ch, 0:n], func=AF.Exp,
                                                                   scale=s.clam2[:, d * LK + ch:d * LK + ch + 1]),
                              r=[("g0", ch), ("clam",)], w=[("S", ch)])
                    for ch in chs:
                        S.add("act", lambda e, ch=ch: e.activation(out=St[:, ch, 0:n], in_=St[:, ch, 0:n], func=AF.Sqrt, scale=-1.0,
                                                                   bias=s.ones[:, 0:1]),
                              r=[("S", ch), ("ones",)], w=[("S", ch)])

                for ch in range(LK):
                    S.add("pool", lambda e, ch=ch: e.tensor_tensor(out=St[:, ch, 0:n], in0=St[:, ch, 0:n], in1=It[:, ch, 0:n], op=ALU.mult),
                          r=[("S", ch), ("g1", ch)], w=[("S", ch)])
                    S.add("pool", lambda e, ch=ch: e.tensor_tensor(out=St[:, ch, 0:n], in0=St[:, ch, 0:n], in1=xc[:, sl, ch, 0:n], op=ALU.mult),
                          r=[("S", ch), ("xc", sl, ch)], w=[("S", ch)])
                    hl = cnt["h"] % 3
                    cnt["h"] += 1
                    if d == 0:
                        S.add("dve", lambda e, ch=ch, hl=hl: e.tensor_tensor_scan(hh[:, hl, 0:n], At[:, ch, 0:n], St[:, ch, 0:n],
                                                                                 state[:, ch:ch + 1], ALU.mult, ALU.add),
                              r=[("A", ch), ("S", ch), ("state", ch)], w=[("hh", hl)])
                        S.add("dve", lambda e, ch=ch, hl=hl: e.tensor_copy(state[:, ch:ch + 1], hh[:, hl, n - 1:n]),
                              r=[("hh", hl)], w=[("state", ch)])
                        if not (is_ctx and last):
                            S.add("sp", lambda e, ch=ch, hl=hl: e.dma_start(
                                out=s.REC[ch * 128:(ch + 1) * 128, off + lo:off + lo + n], in_=hh[:, hl, 0:n]),
                                r=[("hh", hl)], dma=("hh", hl))
                    else:
                        S.add("dve", lambda e, ch=ch, hl=hl: e.tensor_tensor_scan(hh[:, hl, 0:n][:, ::-1], At[:, ch, 0:n][:, ::-1],
                                                                                 St[:, ch, 0:n][:, ::-1],
                                                                                 state[:, ch:ch + 1], ALU.mult, ALU.add),
                              r=[("A", ch), ("S", ch), ("state", ch)], w=[("hh", hl)])
                        S.add("dve", lambda e, ch=ch, hl=hl: e.tensor_copy(state[:, ch:ch + 1], hh[:, hl, 0:1]),
                              r=[("hh", hl)], w=[("state", ch)])
                        if not (is_ctx and last):
                            rl = cnt["rf"] % 2
                            cnt["rf"] += 1
                            S.add("pool", lambda e, ch=ch, hl=hl: e.tensor_tensor(out=hh[:, hl, 0:n], in0=hh[:, hl, 0:n], in1=rf[:, ch, 0:n], op=ALU.add),
                                  r=[("hh", hl), ("rf", ch)], w=[("hh", hl)])
                            S.add("dve", lambda e, ch=ch, hl=hl, rl=rl: e.tensor_tensor(out=yr[:, rl, 0:n], in0=hh[:, hl, 0:n], in1=gg[:, ch, 0:n], op=ALU.mult),
                                  r=[("hh", hl), ("gg", ch)], w=[("yr", rl)])
                            S.add("sp", lambda e, ch=ch, rl=rl: e.dma_start(
                                out=s.YR[ch * 128:(ch + 1) * 128, off + lo:off + lo + n], in_=yr[:, rl, 0:n]),
                                r=[("yr", rl)], dma=("yr", rl))

            for d in range(2):
                S.add("dve", lambda e: e.memset(state[:], 0.0), w=[("state", ch) for ch in range(LK)])
                items = []
                for (off, ln), is_ctx in ((seq_ctx, True), (seq_lat, False)):
                    nblk = (ln + TS - 1) // TS
                    blks = list(range(nblk))
                    if d == 1:
                        blks = blks[::-1]
                    for b in blks:
                        items.append((d, off, ln, is_ctx, b))
                partA(0, *items[0])
                for i, it in enumerate(items):
                    if i + 1 < len(items):
                        partA(i + 1, *items[i + 1])
                    partB(i, *it)
                S.flush()

    def phase_mixc(s, l, src, dst):
        from contextlib import ExitStack
        c, S, nc = s.c, s.S, s.nc
        H, TB, KD, MK, LK, CK = c.HALF, c.TB, c.KD, c.MK, c.LK, c.CK
        with ExitStack() as st:
            T_ = s.mkT(st)
            ym = T_("ym", [128, 2, MK, TB], BF16)
            u32 = T_("u32", [128, 2, KD, TB])
            wt = T_("woutt", [128, 3, c.DMIX], BF16)
            xs = T_("xs", [128, 3, TB])
            sq = T_("sq", [128, 2, TB])
            osl = T_("osl", [128, 2, TB])
            lr = T_("lr", [128, 2, TB])
            lnn = T_("lnn", [128, 2, TB])
            lm = T_("lm", [128, TB])
            lt = T_("lt", [128, TB])
            s.eps_tiles(st)
            YRv = s.YR.rearrange("(c p) t -> p c t", p=128)
            YCv = s.YC.rearrange("(c p) t -> p c t", p=128)
            cn = {"x": 0, "w": 0, "o": 0}

            def load_ym(blk):
                par = blk % 2
                cols = slice(blk * TB, (blk + 1) * TB)
                S.add("sp", lambda e: e.dma_start(out=ym[:, par, 0:LK, :], in_=YRv[:, :, cols]), w=[("ym", par, 0)], dma=("ym", par, 0))
                S.add("sp", lambda e: e.dma_start(out=ym[:, par, LK:MK, :], in_=YCv[:, :, cols]), w=[("ym", par, 1)], dma=("ym", par, 1))

            def tail_chunk(blk, j):
                par = blk % 2
                cols = slice(blk * TB, (blk + 1) * TB)
                o = cn["o"] % 2
                cn["o"] += 1
                S.add("dve", lambda e: e.tensor_tensor(out=u32[:, par, j, :], in0=u32[:, par, j, :], in1=lr[:, par, :], op=ALU.mult),
                      r=[("u", par, j), ("lr", par)], w=[("u", par, j)])
                S.add("dve", lambda e: e.tensor_tensor(out=u32[:, par, j, :], in0=u32[:, par, j, :], in1=lnn[:, par, :], op=ALU.add),
                      r=[("u", par, j), ("lnn", par)], w=[("u", par, j)])
                S.add("act", lambda e: e.activation(out=osl[:, o, :], in_=u32[:, par, j, :], func=AF.Identity,
                                                    scale=s.sm("lng", 1 * KD + j), bias=s.sm("lnb", 1 * KD + j)),
                      r=[("u", par, j), ("small",)], w=[("os", o)])
                S.add(STQ, lambda e: e.dma_start(out=dst[j * 128:(j + 1) * 128, cols], in_=osl[:, o, :]),
                      r=[("os", o)], dma=("os", "st", o))

            def body(blk):
                par = blk % 2
                segs = s.segs(blk)
                cols = slice(blk * TB, (blk + 1) * TB)
                if blk + 1 < c.NB:
                    load_ym(blk + 1)
                pending = None
                for j in range(KD):
                    sl = cn["w"] % 3
                    cn["w"] += 1
                    S.add("pool", lambda e, j=j, sl=sl: e.dma_start(out=wt[:, sl, :], in_=s.wout[l, j], max_dma_last_dim=8192),
                          w=[("wo", sl)], dma=("wo", sl))
                    xl = cn["x"] % 3
                    cn["x"] += 1
                    S.add("sp", lambda e, j=j, xl=xl: e.dma_start(out=xs[:, xl, :], in_=src[j * 128:(j + 1) * 128, cols]),
                          w=[("xs", xl)], dma=("xs", xl))
                    bs = 2 * (j % 2)
                    for kc in range(MK):
                        for h in range(2):
                            S.add("pe", lambda e, sl=sl, kc=kc, h=h, bs=bs: e.matmul(
                                s.ps[bs + h][:, 0:H], wt[:, sl, kc * 128:(kc + 1) * 128], ym[:, par, kc, h * H:(h + 1) * H],
                                start=(kc == 0), stop=(kc == MK - 1)),
                                r=[("wo", sl), ("ym", par, 0 if kc < LK else 1)], w=[("ps", bs + h)])
                    for (lo, hi, which) in segs:
                        S.add("act", lambda e, j=j, xl=xl, lo=lo, hi=hi, which=which: e.activation(
                            out=xs[:, xl, lo:hi], in_=xs[:, xl, lo:hi], func=AF.Identity, bias=s.bgv[:, j, which:which + 1]),
                            r=[("xs", xl), ("bgv",)], w=[("xs", xl)])
                        gcol = s.modv[:, 5 * KD + j, which:which + 1]
                        for h in range(2):
                            a0, a1 = max(lo, h * H), min(hi, (h + 1) * H)
                            if a0 >= a1:
                                continue
                            S.add("dve", lambda e, j=j, h=h, a0=a0, a1=a1, gcol=gcol, xl=xl, bs=bs: e.scalar_tensor_tensor(
                                out=u32[:, par, j, a0:a1], in0=s.ps[bs + h][:, a0 - h * H:a1 - h * H], scalar=gcol, in1=xs[:, xl, a0:a1],
                                op0=ALU.mult, op1=ALU.add),
                                r=[("ps", bs + h), ("xs", xl), ("modv",)], w=[("u", par, j)])
                    q = j % 2
                    S.add("act", lambda e, j=j, q=q: e.activation(out=sq[:, q, :], in_=u32[:, par, j, :], func=AF.Square),
                          r=[("u", par, j)], w=[("sq", q)])
                    if j == 0:
                        S.add("dve", lambda e, j=j: e.tensor_copy(lr[:, par, :], u32[:, par, j, :]), r=[("u", par, j)], w=[("lr", par)])
                        S.add("dve", lambda e, q=q: e.tensor_copy(lnn[:, par, :], sq[:, q, :]), r=[("sq", q)], w=[("lnn", par)])
                    else:
                        S.add("dve", lambda e, j=j: e.tensor_tensor(out=lr[:, par, :], in0=lr[:, par, :], in1=u32[:, par, j, :], op=ALU.add),
                              r=[("u", par, j), ("lr", par)], w=[("lr", par)])
                        S.add("dve", lambda e, q=q: e.tensor_tensor(out=lnn[:, par, :], in0=lnn[:, par, :], in1=sq[:, q, :], op=ALU.add),
                              r=[("sq", q), ("lnn", par)], w=[("lnn", par)])
                    if blk > 0:
                        tail_chunk(blk - 1, j)
                s.stats_mm(0, 1, lr[:, par, :], lnn[:, par, :], ("lr", par), ("lnn", par))
                lnt = [(lm, ("ln", "mean")), (lr[:, par, :], ("lr", par)), (lnn[:, par, :], ("lnn", par)), (lt, ("ln", "tmp"))]
                s.ln_tail(None, 0, c.D, "ffn", lnt, None)

            load_ym(0)
            for blk_ in range(c.NB):
                body(blk_)
            for j in range(KD):
                tail_chunk(c.NB - 1, j)
            S.flush()


def tile_w(W):
    K, N = W.shape
    return np.ascontiguousarray(W.reshape(K // 128, 128, N // 128, 128).transpose(2, 1, 0, 3).reshape(N // 128, 128, K))


def fm(v):
    sh = v.shape
    return np.moveaxis(v.reshape(sh[:-1] + (sh[-1] // 128, 128)), -1, 0)


def prep_inputs(cfg, inp):
    c = cfg
    L = c.DEPTH
    f32 = lambda a: np.ascontiguousarray(np.asarray(a, dtype=np.float32))
    small = np.zeros((L, 128, c.NS), np.float32)
    so = c.so
    for l in range(L):
        def put(name, arr):
            arr = np.asarray(arr, np.float32).reshape(128, -1)
            small[l, :, so[name]:so[name] + arr.shape[1]] = arr
        put("bada", fm(inp["b_ada"][l]))
        put("lng", fm(inp["ln_g"][l]))
        put("lnb", fm(inp["ln_b"][l]))
        put("c4w", np.moveaxis(fm(inp["conv4_w"][l]), 1, 2))
        put("c4b", fm(inp["conv4_b"][l]))
        put("brg", fm(inp["b_rg"][l]))
        put("big", fm(inp["b_ig"][l]))
        put("lam", fm(inp["lam"][l]))
        put("c31w", np.moveaxis(fm(inp["conv31_w"][l]), 1, 2))
        put("c31b", fm(inp["conv31_b"][l]))
        put("clng", fm(inp["cln_g"][l]))
        put("clnb", fm(inp["cln_b"][l]))
        put("bout", fm(inp["b_out"][l]))
    wada = np.stack([tile_w(f32(inp["w_ada"][l])) for l in range(L)])
    w1 = np.stack([np.stack([tile_w(f32(inp[k][l])) for k in ("ff1_in", "ff2_in")]) for l in range(L)])
    w2 = np.stack([np.stack([tile_w(f32(inp[k][l])) for k in ("ff1_out", "ff2_out")]) for l in range(L)])
    win = np.stack([tile_w(f32(inp["w_in"][l])) for l in range(L)])
    wout = np.stack([tile_w(f32(inp["w_out"][l])) for l in range(L)])
    gw = np.zeros((L, 128, 2, 2, c.NH, c.HC, c.HD), np.float32)
    for l in range(L):
        for d in range(2):
            for g, nm in enumerate(("w_rg", "w_ig")):
                for h in range(c.NH):
                    tw = tile_w(f32(inp[nm][l][d][h]))
                    gw[l, :, d, g, h] = tw.transpose(1, 0, 2)
    gw = gw.reshape(L, 128, -1)
    shared = dict(small=small, wada=wada, w1=w1, w2=w2, win=win, wout=wout, gwt=np.ascontiguousarray(gw))
    maps = []
    for b in range(c.NCORES):
        xin = np.concatenate([f32(inp["x"][b]).T, f32(inp["ctx"][b]).T], axis=1)
        cv = np.stack([fm(f32(inp["c"][b])), fm(f32(inp["c_ctx"]))], axis=-1).reshape(128, -1)
        m = dict(shared)
        m["xin"] = np.ascontiguousarray(xin)
        m["cvec"] = np.ascontiguousarray(cv)
        m["ident"] = np.eye(128, dtype=np.float32)
        maps.append(m)
    return maps


_NC_CACHE = {}


def run_cfg(cfg, inp):
    key = (cfg.D, cfg.FF, cfg.T, cfg.DLRU, cfg.DCONV, getattr(cfg, "STOP", None))
    if key not in _NC_CACHE:
        _NC_CACHE[key] = Builder(cfg).build()
    nc = _NC_CACHE[key]
    maps = prep_inputs(cfg, inp)
    res = run_bass_kernel_spmd(nc, maps, core_ids=list(range(cfg.NCORES)))
    if getattr(cfg, "STOP", None) is not None:
        return np.stack([np.ascontiguousarray(res.results[b]["yout"].T) for b in range(cfg.NCORES)])
    out = np.stack([np.ascontiguousarray(res.results[b]["yout"][:, :cfg.T].T) for b in range(cfg.NCORES)])
    return out.astype(np.float32)


def kernel(**inputs):
    cfg = Cfg()
    return run_cfg(cfg, inputs)
```
